# Optimizing a Trainium2 kernel written in Bass

```python
import math
import jax, jax.numpy as jnp
from jax import lax
import numpy as np

D_MODEL = 2048
BATCH = 4
SEQ = 2048
DEPTH = 1

HEAD_DIM = 128
MEM_LEN = 256
ATT_GROUPS = ((128, 1), (512, 4), (2048, 16))
ATT_HEADS_PER_GROUP = 4
N_ATT_GROUPS = len(ATT_GROUPS)
ATT_WIDTH = N_ATT_GROUPS * ATT_HEADS_PER_GROUP * HEAD_DIM
ATT_OUT_WIDTH = ATT_HEADS_PER_GROUP * HEAD_DIM
ATT_BLOCK = 128
HG_HEADS = 8
HG_KEY = 128
HG_VAL = 128
HG_WIDTH = HG_HEADS * HG_KEY
HG_CHUNK = 64
N_BRANCH = 2
IN_WIDTH = 3 * ATT_WIDTH + 4 * HG_WIDTH + N_BRANCH * D_MODEL
CROSS_HEADS = 4
CROSS_WIDTH = CROSS_HEADS * HEAD_DIM
D_FF = int(math.ceil(8 * D_MODEL / 3 / 256) * 256)
RMS_EPS = 1e-6

kernel_name = "hybrid_dilated_attn_hgrn2_gated_block"


def rms_norm(x, w):
    xf = x.astype(jnp.float32)
    y = xf * lax.rsqrt(jnp.mean(xf * xf, axis=-1, keepdims=True) + RMS_EPS)
    return (y * w.astype(jnp.float32)).astype(x.dtype)


def banded_window_attention(q, k, v, window):
    assert window <= ATT_BLOCK
    lead = q.shape[:-2]
    L, dh = q.shape[-2], q.shape[-1]
    nb = -(-L // ATT_BLOCK)
    Lp = nb * ATT_BLOCK
    if Lp != L:
        pad = [(0, 0)] * len(lead) + [(0, Lp - L), (0, 0)]
        q, k, v = jnp.pad(q, pad), jnp.pad(k, pad), jnp.pad(v, pad)
    qb = q.reshape(*lead, nb, ATT_BLOCK, dh)
    kb = k.reshape(*lead, nb, ATT_BLOCK, dh)
    vb = v.reshape(*lead, nb, ATT_BLOCK, dh)
    kk = jnp.concatenate([jnp.concatenate([jnp.zeros_like(kb[..., :1, :, :]), kb[..., :-1, :, :]], axis=-3), kb], axis=-2)
    vv = jnp.concatenate([jnp.concatenate([jnp.zeros_like(vb[..., :1, :, :]), vb[..., :-1, :, :]], axis=-3), vb], axis=-2)
    blk = jnp.arange(nb)[:, None, None] * ATT_BLOCK
    qpos = blk + jnp.arange(ATT_BLOCK)[None, :, None]
    kpos = blk - ATT_BLOCK + jnp.arange(2 * ATT_BLOCK)[None, None, :]
    dist = qpos - kpos
    mask = (dist >= 0) & (dist <= window) & (kpos >= 0)
    s = jnp.einsum('...nqd,...nkd->...nqk', qb, kk).astype(jnp.float32) * (dh ** -0.5)
    s = jnp.where(mask, s, -jnp.inf)
    m = jnp.max(s, axis=-1, keepdims=True)
    p = jnp.exp(s - m)
    l = jnp.sum(p, axis=-1, keepdims=True)
    o = jnp.einsum('...nqk,...nkd->...nqd', (p / l).astype(v.dtype), vv)
    lse = (m + jnp.log(l))[..., 0]
    o = o.reshape(*lead, Lp, dh)[..., :L, :]
    lse = lse.reshape(*lead, Lp)[..., :L]
    return o, lse


def dilated_window_attention(q, k, v, window, dilation):
    B, H, S, dh = q.shape
    L = S // dilation

    def to_residue(t):
        return t.reshape(B, H, L, dilation, dh).transpose(0, 1, 3, 2, 4)

    o, lse = banded_window_attention(to_residue(q), to_residue(k), to_residue(v), window // dilation)
    o = o.transpose(0, 1, 3, 2, 4).reshape(B, H, S, dh)
    lse = lse.transpose(0, 1, 3, 2).reshape(B, H, S)
    return o, lse


def hgrn2_chunked(q, log_f, k, v):
    B, H, S, K = q.shape
    V = v.shape[-1]
    C = HG_CHUNK
    N = S // C
    q, log_f, k = (t.reshape(B, H, N, C, K) for t in (q, log_f, k))
    v = v.reshape(B, H, N, C, V)
    b = jnp.cumsum(log_f, axis=-2)
    b_last = b[..., -1:, :]
    chunk_kv = jnp.einsum('bhnck,bhncv->bhnkv', k * jnp.exp(b_last - b), v)
    decay = jnp.exp(b_last[..., 0, :])

    def step(state, inp):
        dec, kv = inp
        return dec[..., None] * state + kv, state

    _, states = lax.scan(step, jnp.zeros((B, H, K, V), jnp.float32),
                         (jnp.moveaxis(decay, 2, 0), jnp.moveaxis(chunk_kv, 2, 0)))
    states = jnp.moveaxis(states, 0, 2)
    inter = jnp.einsum('bhnck,bhnkv->bhncv', q * jnp.exp(b), states)
    b_ref = b[..., C // 2:C // 2 + 1, :]
    a = jnp.einsum('bhnck,bhnsk->bhncs', q * jnp.exp(b - b_ref), k * jnp.exp(b_ref - b))
    causal = jnp.arange(C)[:, None] >= jnp.arange(C)[None, :]
    a = jnp.where(causal, a, 0.0)
    intra = jnp.einsum('bhncs,bhnsv->bhncv', a, v)
    return (inter + intra).reshape(B, H, S, V)


def setup_inputs(seed: int = 0) -> dict:
    key = jax.random.key(seed)
    ks = jax.random.split(key, 20)
    nrm = lambda k, shape, fan_in: jax.random.normal(k, shape, jnp.float32) * (fan_in ** -0.5)
    gain = lambda k, shape: 1.0 + 0.02 * jax.random.normal(k, shape, jnp.float32)
    return {
        "x": jax.random.normal(ks[0], (BATCH, SEQ, D_MODEL), jnp.float32),
        "mem": jax.random.normal(ks[1], (BATCH, MEM_LEN, D_MODEL), jnp.float32),
        "ln_mix_w": gain(ks[2], (DEPTH, D_MODEL)),
        "w_in": nrm(ks[3], (DEPTH, D_MODEL, IN_WIDTH), D_MODEL),
        "hg_norm_w": gain(ks[4], (DEPTH, HG_VAL)),
        "hg_lower_bounds": 0.1 * jax.random.normal(ks[5], (DEPTH + 1, HG_WIDTH), jnp.float32),
        "w_branch_a": nrm(ks[6], (DEPTH, ATT_OUT_WIDTH, D_MODEL), ATT_OUT_WIDTH),
        "w_branch_b": nrm(ks[7], (DEPTH, HG_WIDTH, D_MODEL), HG_WIDTH),
        "w_out": nrm(ks[8], (DEPTH, D_MODEL, D_MODEL), D_MODEL),
        "ln_cross_w": gain(ks[9], (DEPTH, D_MODEL)),
        "ln_mem_w": gain(ks[10], (DEPTH, D_MODEL)),
        "wq_cross": nrm(ks[11], (DEPTH, D_MODEL, CROSS_WIDTH), D_MODEL),
        "wkv_cross": nrm(ks[12], (DEPTH, D_MODEL, 2 * CROSS_WIDTH), D_MODEL),
        "wo_cross": nrm(ks[13], (DEPTH, CROSS_WIDTH, D_MODEL), CROSS_WIDTH),
        "ln_ffn_w": gain(ks[14], (DEPTH, D_MODEL)),
        "w1": nrm(ks[15], (DEPTH, D_MODEL, D_FF), D_MODEL),
        "w3": nrm(ks[16], (DEPTH, D_MODEL, D_FF), D_MODEL),
        "w2": nrm(ks[17], (DEPTH, D_FF, D_MODEL), D_FF),
        "ln_final_w": gain(ks[18], (D_MODEL,)),
    }


def reference(x, mem, ln_mix_w, w_in, hg_norm_w, hg_lower_bounds, w_branch_a, w_branch_b, w_out,
              ln_cross_w, ln_mem_w, wq_cross, wkv_cross, wo_cross, ln_ffn_w, w1, w3, w2, ln_final_w):
    B, S, D = x.shape
    M = mem.shape[1]
    lower_bounds = jnp.cumsum(jax.nn.softmax(hg_lower_bounds.astype(jnp.float32), axis=0), axis=0)
    for l in range(DEPTH):
        h = rms_norm(x, ln_mix_w[l])
        proj = h @ w_in[l]
        q_a, k_a, v_a, q_h, f_h, i_h, g_h, gates = jnp.split(
            proj, np.cumsum([ATT_WIDTH] * 3 + [HG_WIDTH] * 4).tolist(), axis=-1)

        def att_heads(t):
            return t.reshape(B, S, N_ATT_GROUPS, ATT_HEADS_PER_GROUP, HEAD_DIM).transpose(0, 2, 3, 1, 4)
        qa, ka, va = att_heads(q_a), att_heads(k_a), att_heads(v_a)
        outs, lses = [], []
        for g, (window, dilation) in enumerate(ATT_GROUPS):
            o_g, lse_g = dilated_window_attention(qa[:, g], ka[:, g], va[:, g], window, dilation)
            outs.append(o_g)
            lses.append(lse_g)
        outs = jnp.stack(outs, axis=1)
        alpha = jax.nn.softmax(jnp.stack(lses, axis=1), axis=1)
        o_att = jnp.sum(alpha[..., None].astype(outs.dtype) * outs, axis=1)
        o_att = o_att.transpose(0, 2, 1, 3).reshape(B, S, ATT_OUT_WIDTH)

        def hg_heads(t):
            return t.reshape(B, S, HG_HEADS, HG_KEY).transpose(0, 2, 1, 3).astype(jnp.float32)
        lb = lower_bounds[l].reshape(HG_HEADS, HG_KEY)[None, :, None, :]
        f = lb + (1.0 - lb) * jax.nn.sigmoid(hg_heads(f_h))
        o_hg = hgrn2_chunked(jax.nn.silu(hg_heads(q_h)), jnp.log(f), 1.0 - f, hg_heads(i_h))
        o_hg = o_hg * lax.rsqrt(jnp.mean(o_hg * o_hg, axis=-1, keepdims=True) + RMS_EPS) * hg_norm_w[l].astype(jnp.float32)
        o_hg = (o_hg * jax.nn.silu(hg_heads(g_h))).transpose(0, 2, 1, 3).reshape(B, S, HG_WIDTH).astype(x.dtype)

        gate_a, gate_b = jnp.split(jax.nn.sigmoid(gates), N_BRANCH, axis=-1)
        merged = gate_a * (o_att @ w_branch_a[l]) + gate_b * (o_hg @ w_branch_b[l])
        x = x + merged @ w_out[l]

        hc = rms_norm(x, ln_cross_w[l])
        mn = rms_norm(mem, ln_mem_w[l])
        qc = (hc @ wq_cross[l]).reshape(B, S, CROSS_HEADS, HEAD_DIM)
        kvc = (mn @ wkv_cross[l]).reshape(B, M, 2, CROSS_HEADS, HEAD_DIM)
        sc = jnp.einsum('bshd,bmhd->bhsm', qc, kvc[:, :, 0]).astype(jnp.float32) * (HEAD_DIM ** -0.5)
        pc = jax.nn.softmax(sc, axis=-1).astype(x.dtype)
        oc = jnp.einsum('bhsm,bmhd->bshd', pc, kvc[:, :, 1]).reshape(B, S, CROSS_WIDTH)
        x = x + oc @ wo_cross[l]

        hf = rms_norm(x, ln_ffn_w[l])
        x = x + (jax.nn.silu(hf @ w1[l]) * (hf @ w3[l])) @ w2[l]
    return rms_norm(x, ln_final_w)
```

```python
import numpy as np
from contextlib import ExitStack
import concourse.bass as bass
import concourse.mybir as mybir
from concourse.bass_utils import run_bass_kernel_spmd

F32 = mybir.dt.float32
BF16 = mybir.dt.bfloat16
AF = mybir.ActivationFunctionType
ALU = mybir.AluOpType

ENGS = ("pe", "act", "dve", "pool", "sp")
D = 2048
T = 1024
EPS = 1e-6
NW = 7
SCALE = 128.0 ** -0.5


class Op:
    __slots__ = ("eng", "fn", "dma", "idx", "eidx", "deps", "seq", "dsem", "dval", "has_dep")

    def __init__(self, eng, fn, dma):
        self.eng = eng
        self.fn = fn
        self.dma = dma
        self.deps = []
        self.seq = None
        self.dsem = None
        self.dval = None
        self.has_dep = False


class Prog:
    N_DMA_SEMS = 40

    def __init__(self, nc):
        self.nc = nc
        self.ops = []
        self.last_w = {}
        self.readers = {}
        self.ecount = {e: 0 for e in ENGS}

    def op(self, eng, fn, reads=(), writes=(), dma=False):
        xk = [k for k in reads if isinstance(k, tuple) and k[0] in ("ps", "tb")]
        if xk:
            reads = [k for k in reads if k not in xk]
            writes = list(writes) + [k for k in xk if k not in writes]
        o = Op(eng, fn, dma)
        o.idx = len(self.ops)
        o.eidx = self.ecount[eng]
        self.ecount[eng] += 1
        deps = {}
        for k in reads:
            w = self.last_w.get(k)
            if w is not None:
                deps[w.idx] = (w, "raw")
        for k in writes:
            w = self.last_w.get(k)
            if w is not None and w.idx not in deps:
                deps[w.idx] = (w, "waw")
            rd = self.readers.get(k)
            if rd:
                for r in rd.values():
                    if r.idx not in deps and r is not o:
                        deps[r.idx] = (r, "war")
        for k in reads:
            rd = self.readers.setdefault(k, {})
            rd[("dma", o.idx) if dma else eng] = o
        for k in writes:
            self.last_w[k] = o
            self.readers[k] = {}
        for p, kind in deps.values():
            if (not p.dma) and p.eng == eng and not dma:
                if kind == "raw" and eng != "pe" and p.eidx >= o.eidx - 2:
                    o.deps.append(p)
                    p.has_dep = True
            else:
                o.deps.append(p)
                p.has_dep = True
        self.ops.append(o)
        return o

    def pe(self, fn, reads=(), writes=()):
        return self.op("pe", fn, reads, writes)

    def act(self, fn, reads=(), writes=()):
        return self.op("act", fn, reads, writes)

    def dve(self, fn, reads=(), writes=()):
        return self.op("dve", fn, reads, writes)

    def pool(self, fn, reads=(), writes=()):
        return self.op("pool", fn, reads, writes)

    def dma(self, q, fn, reads=(), writes=()):
        return self.op(q, fn, reads, writes, dma=True)

    def emit(self, stack):
        nc = self.nc
        esem = {e: stack.enter_context(nc.semaphore("s_" + e)) for e in ENGS}
        dsems = [stack.enter_context(nc.semaphore("d%d" % i)) for i in range(self.N_DMA_SEMS)]
        cnt = {e: 0 for e in ENGS}
        dtot = [0] * self.N_DMA_SEMS
        NSW = 24
        nxt = {"sw": 0, "hw": 0}
        for o in self.ops:
            if o.dma:
                if o.eng == "pool":
                    d = nxt["sw"]
                    nxt["sw"] = (d + 1) % NSW
                else:
                    d = NSW + nxt["hw"]
                    nxt["hw"] = (nxt["hw"] + 1) % (self.N_DMA_SEMS - NSW)
                o.dsem = d
                dtot[d] += 16
                o.dval = dtot[d]
            elif o.has_dep:
                cnt[o.eng] += 1
                o.seq = cnt[o.eng]
        for e in ENGS:
            assert cnt[e] < 60000, (e, cnt[e])
        for d in range(self.N_DMA_SEMS):
            assert dtot[d] < 60000
        block = stack.enter_context(nc.Block())
        by_eng = {e: [o for o in self.ops if o.eng == e] for e in ENGS}

        def run(ename, eng):
            waited_e = {e: 0 for e in ENGS}
            waited_d = [0] * self.N_DMA_SEMS
            for o in by_eng[ename]:
                need_e = {}
                need_d = {}
                for p in o.deps:
                    if p.dma:
                        need_d[p.dsem] = max(need_d.get(p.dsem, 0), p.dval)
                    else:
                        need_e[p.eng] = max(need_e.get(p.eng, 0), p.seq)
                if o.dma and o.dval > 16:
                    need_d[o.dsem] = max(need_d.get(o.dsem, 0), o.dval - 16)
                for e, v in need_e.items():
                    if v > waited_e[e]:
                        eng.wait_ge(esem[e], v)
                        waited_e[e] = v
                for d, v in need_d.items():
                    if v > waited_d[d]:
                        eng.wait_ge(dsems[d], v)
                        waited_d[d] = v
                ins = o.fn(eng)
                if o.dma:
                    ins.then_inc(dsems[o.dsem], 16)
                elif o.seq is not None:
                    ins.then_inc(esem[ename], 1)
            if ename == "sp":
                for d in range(self.N_DMA_SEMS):
                    if dtot[d] > waited_d[d]:
                        eng.wait_ge(dsems[d], dtot[d])

        block.tensor(lambda eng: run("pe", eng))
        block.scalar(lambda eng: run("act", eng))
        block.vector(lambda eng: run("dve", eng))
        block.gpsimd(lambda eng: run("pool", eng))
        block.sync(lambda eng: run("sp", eng))


def sl(start, n, step=1):
    return slice(start, start + (n - 1) * step + 1, step)


def build(debug=(), limit=99):
    nc = bass.Bass("TRN2", target_bir_lowering=False)

    def din(name, shape):
        return nc.dram_tensor(name, shape, F32, kind="ExternalInput").ap()

    xo = din("xo", [T, D])
    xp = din("xp", [T, D])
    memb = din("memb", [256, D])
    w_in = din("w_in", [D, 12800])
    hg_lb = din("hg_lb", [2, 1024])
    hg_nw = din("hg_nw", [1, 128])
    w_ba = din("w_ba", [512, D])
    w_bb = din("w_bb", [1024, D])
    w_out = din("w_out", [D, D])
    wq_c = din("wq_c", [D, 512])
    wkv_c = din("wkv_c", [D, 1024])
    wo_c = din("wo_c", [512, D])
    w1 = din("w1", [D, 5632])
    w3 = din("w3", [D, 5632])
    w2 = din("w2", [5632, D])
    lnw = din("lnw", [5, D])
    pmk = din("pmk", [128, 2])
    cst = din("cst", [128, 264])
    cstb = din("cstb", [128, 1024])
    y = nc.dram_tensor("y", [T, D], F32, kind="ExternalOutput").ap()
    dbg = {}
    for name, shape in debug:
        dbg[name] = nc.dram_tensor("dbg_" + name, shape, F32, kind="ExternalOutput").ap()

    def kview(w):
        return w.rearrange("(kc p) n -> p kc n", p=128)

    w_in_v = kview(w_in)

    with ExitStack() as st:
        P = Prog(nc)

        def sb(name, shape, dt):
            return st.enter_context(nc.sbuf_tensor(name, shape, dt))

        def ps(name, shape, dt):
            return st.enter_context(nc.psum_tensor(name, shape, dt))

        hT = sb("hT", [128, 2, 16, 1024], BF16)
        R = sb("R", [128, 16384], F32)
        wbuf = [sb("wb%d" % i, [128, 4096], BF16) for i in range(NW)]
        cf = sb("cf", [128, 264], F32)
        cb = sb("cb", [128, 1024], BF16)
        pm = sb("pm", [128, 2], F32)
        lbb = sb("lbb", [128, 256], F32)
        oml = sb("oml", [128, 256], F32)
        lbt = sb("lbt", [128, 256], F32)
        kcT = sb("kcT", [128, 4, 256], BF16)
        vc = sb("vc", [128, 2, 512], BF16)
        state = sb("state", [128, 2, 128], F32)
        st_sb = [sb("st_s%d" % i, [128, 2, 128], BF16) for i in range(2)]
        ssv = sb("ssv", [128, 32], F32)
        dummy = sb("dummyk", [128, 8], F32)
        rsv = sb("rsv", [128, 32], F32)
        ecsb = [sb("ecs%d" % i, [128, 8], F32) for i in range(3)]
        tmpf = [sb("tmpf%d" % i, [128, 512], F32) for i in range(2)]
        hgnb = sb("hgnb", [128, 128], F32)

        PB = [ps("P%d" % i, [128, 512], F32) for i in range(8)]
        TBk = [PB[6][:, :].bitcast(BF16), PB[7][:, 0:256].bitcast(BF16)]
        CSB = PB[7][:, 256:512]

        Hprev = hT[:, 0].rearrange("p a b -> p (a b)")
        xs = R[:].rearrange("p (t d) -> p t d", t=8)

        def Rb(a, b):
            return R[:, a:b].bitcast(BF16)

        P.dma("sp", lambda e: e.dma_start(out=cf[:], in_=cst), writes=["cf"])
        P.dma("sp", lambda e: e.dma_start(out=pm[:], in_=pmk), writes=["pm"])
        P.dma("pool", lambda e: e.dma_start(out=cb[:], in_=cstb), writes=["cb"])
        TD1 = cf[:, 0:128]
        TD2 = cf[:, 128:256]
        SEL = cf[:, 256:260]
        ident = cb[:, 0:128]
        M01 = cb[:, 128:384]
        M2 = cb[:, 384:448]
        MBLK = cb[:, 448:576]
        ones = cb[:, 576:704]
        MP2 = cb[:, 704:832]
        MO2 = cb[:, 832:960]
        P.dma("sp", lambda e: e.dma_start(out=hgnb[:], in_=hg_nw.broadcast_to([128, 128])), writes=["hgnb"])

        if limit == 0:
            P.dma("sp", lambda e: e.dma_start(out=y[0:128, 0:264], in_=cf[:]), reads=["cf", "cb", "pm", "hgnb"], writes=["y"])
            P.emit(st)
            return nc
        ring = [0]

        def wslot():
            k = ring[0] % NW
            ring[0] += 1
            return k

        def wdma(k, dst, src):
            P.dma("pool", lambda e: e.dma_start(out=dst, in_=src), writes=[("w", k)])

        def wload(src, a, b):
            k = wslot()
            dst = wbuf[k][:, 0:a * b].rearrange("p (a b) -> p a b", a=a)
            wdma(k, dst, src)
            return dst, ("w", k)

        pre = {}

        def prefetch(name, fn):
            if name not in pre:
                pre[name] = fn()

        def take(name, fn):
            if name not in pre:
                pre[name] = fn()
            return pre.pop(name)

        def mm(out, lhsT, rhs, start, stop, reads, writes, skip=False):
            P.pe(lambda e: e.matmul(out, lhsT, rhs, start=start, stop=stop, skip_group_check=skip), reads=reads, writes=writes)

        evc = [0]

        def evac(out, in_, reads, writes, eng=None):
            if eng is None:
                eng = "act" if evc[0] % 2 == 0 else "dve"
                evc[0] += 1
            if eng == "act":
                P.act(lambda e: e.copy(out=out, in_=in_), reads=reads, writes=writes)
            elif eng == "dve":
                P.dve(lambda e: e.tensor_copy(out=out, in_=in_), reads=reads, writes=writes)
            else:
                P.pool(lambda e: e.tensor_copy(out=out, in_=in_), reads=reads, writes=writes)

        pbc = [0]

        npb = [5]

        def pbank():
            i = pbc[0] % npb[0]
            pbc[0] += 1
            if npb[0] == 7 and i >= 5:
                i += 1
            return i

        tbc = [0]

        def norm_tile(src, src_key, lnwb, hn, junk, idx, dst_fn, dst_keys, out_f32=None):
            norm_s1(src, src_key, junk, idx)
            norm_s2(src, src_key, lnwb, hn, idx, dst_fn, dst_keys, out_f32)

        def norm_s1(src, src_key, junk, idx):
            P.act(lambda e: e.activation(out=junk, in_=src, func=AF.Square, accum_out=ssv[:, idx:idx + 1]),
                  reads=[src_key], writes=["junk", ("ss", idx)])
            P.dve(lambda e: e.tensor_scalar(out=rsv[:, idx:idx + 1], in0=ssv[:, idx:idx + 1], scalar1=1.0 / D, scalar2=EPS,
                                            op0=ALU.mult, op1=ALU.add),
                  reads=[("ss", idx)], writes=[("rs", idx)])
            P.act(lambda e: e.activation(out=rsv[:, idx:idx + 1], in_=rsv[:, idx:idx + 1], func=AF.Sqrt),
                  reads=[("rs", idx)], writes=[("rs", idx)])
            P.dve(lambda e: e.reciprocal(out=rsv[:, idx:idx + 1], in_=rsv[:, idx:idx + 1]),
                  reads=[("rs", idx)], writes=[("rs", idx)])

        def norm_s2(src, src_key, lnwb, hn, idx, dst_fn, dst_keys, out_f32=None):
            if out_f32 is not None:
                out_ap, out_key = out_f32
                P.dve(lambda e: e.scalar_tensor_tensor(out=out_ap, in0=src, scalar=rsv[:, idx:idx + 1], in1=lnwb,
                                                       op0=ALU.mult, op1=ALU.mult),
                      reads=[src_key, ("rs", idx), "lnwb"], writes=[out_key])
                return
            hn_ap, hn_key = hn
            P.dve(lambda e: e.scalar_tensor_tensor(out=hn_ap, in0=src, scalar=rsv[:, idx:idx + 1], in1=lnwb,
                                                   op0=ALU.mult, op1=ALU.mult),
                  reads=[src_key, ("rs", idx), "lnwb"], writes=[hn_key])
            for r4 in range(4):
                slot = tbc[0] % 2
                tbc[0] += 1
                for j in range(4):
                    kc = r4 * 4 + j
                    o_ap = TBk[slot][:, j * 128:(j + 1) * 128]
                    i_ap = hn_ap[:, kc * 128:(kc + 1) * 128]
                    P.pe(lambda e, o_ap=o_ap, i_ap=i_ap: e.transpose(out=o_ap, in_=i_ap, identity=ident),
                         reads=[hn_key, "cb"], writes=[("ps", 6 + slot)])
                evac(dst_fn(r4 * 4), TBk[slot][:, 0:512].rearrange("p (a b) -> p a b", a=4),
                     reads=[("ps", 6 + slot)], writes=dst_keys)

        def load_lnw(i, lnwb):
            P.dma("sp", lambda e: e.dma_start(out=lnwb, in_=lnw[i:i + 1, :].broadcast_to([128, D])),
                  writes=["lnwb"])

        def hkeys(w0, w1_):
            return [("h", t) for t in range(w0 // 128, (w1_ - 1) // 128 + 1)]

        def hw(kc, w0, n, step=1):
            half = w0 // 1024
            c0 = w0 % 1024
            if step == 1:
                return hT[:, half, kc, c0:c0 + n]
            return hT[:, half, kc, sl(c0, n, step)]

        xst = [R[:, 0:2048], R[:, 2048:4096], R[:, 12288:14336], R[:, 14336:16384]]
        hnA = [(Rb(4096, 5120), "hn0"), (Rb(5120, 6144), "hn1")]
        junkA = Rb(6144, 7168)
        lnwbA = R[:, 8192:10240]
        mnT = Rb(10240, 12288).rearrange("p (a b) -> p a b", a=16)

        load_lnw(2, lnwbA)
        for mt in range(2):
            P.dma("sp", lambda e, mt=mt: e.dma_start(out=xst[mt], in_=memb[mt * 128:(mt + 1) * 128, :]),
                  writes=[("xst", mt)])
            norm_s1(xst[mt], ("xst", mt), junkA, mt)
        for mt in range(2):
            norm_s2(xst[mt], ("xst", mt), lnwbA, hnA[mt], mt,
                    lambda kc0, mt=mt: mnT[:, kc0:kc0 + 4, mt * 128:(mt + 1) * 128], ["mnT"])
        wkv_v = kview(wkv_c)
        npb[0] = 7
        for hh2 in range(2):
            wk, wkk = wload(wkv_v[:, :, hh2 * 256:(hh2 + 1) * 256], 16, 256)
            for j in range(2):
                hh = hh2 * 2 + j
                b = pbank()
                for kc in range(16):
                    mm(PB[b][:, 0:256], wk[:, kc, j * 128:(j + 1) * 128], mnT[:, kc, :], kc == 0, kc == 15,
                       [wkk, "mnT"], [("ps", b)])
                evac(kcT[:, hh, :], PB[b][:, 0:256], [("ps", b)], ["kcT"])
        for vb in range(2):
            wv_, wvk = wload(wkv_v[:, :, 512 + vb * 256:512 + (vb + 1) * 256], 16, 256)
            for mt in range(2):
                b = pbank()
                for kc in range(16):
                    mm(PB[b][:, 0:256], mnT[:, kc, mt * 128:(mt + 1) * 128], wv_[:, kc, :], kc == 0, kc == 15,
                       [wvk, "mnT"], [("ps", b)])
                evac(vc[:, mt, vb * 256:(vb + 1) * 256], PB[b][:, 0:256], [("ps", b)], ["vc"])

        npb[0] = 5
        load_lnw(0, lnwbA)
        def a1_load(wt):
            s_ = wt % 4
            src = xp if wt < 8 else xo
            r0 = (wt % 8) * 128
            P.dma("sp", lambda e: e.dma_start(out=xst[s_], in_=src[r0:r0 + 128, :]), writes=[("xst", s_)])
            norm_s1(xst[s_], ("xst", s_), junkA, wt)

        a1_load(0)
        for wt in range(16):
            if wt + 1 < 16:
                a1_load(wt + 1)
            norm_s2(xst[wt % 4], ("xst", wt % 4), lnwbA, hnA[wt % 2], wt,
                    lambda kc0, wt=wt: hT[:, wt // 8, kc0:kc0 + 4, (wt % 8) * 128:(wt % 8 + 1) * 128],
                    [("h", wt)])

        if "hT" in dbg:
            for half in range(2):
                for kc in range(16):
                    for th in range(2):
                        P.dve(lambda e, half=half, kc=kc, th=th: e.tensor_copy(out=tmpf[0][:, :], in_=hT[:, half, kc, th * 512:(th + 1) * 512]),
                              reads=hkeys(0, 2048), writes=[("tmpf", 0)])
                        P.dma("sp", lambda e, half=half, kc=kc, th=th: e.dma_start(
                            out=dbg["hT"][(half * 16 + kc) * 2 + th], in_=tmpf[0][:, :]), reads=[("tmpf", 0)], writes=["dbg"])

        if limit == 1:
            P.emit(st)
            return nc
        qT = Rb(0, 1536).rearrange("p (g n) -> p g n", g=3)
        kT = Rb(1536, 3904)
        Vh = Rb(3904, 6272).rearrange("p (t d) -> p t d", d=128)
        acc_n = R[:, 6272:7296]
        acc_d = R[:, 7296:8320]
        o_attT = Rb(14336, 16384).rearrange("p (h n) -> p h n", h=4)
        Ebuf = [Rb(10368, 11904), Rb(11904, 13440)]
        A1KEYS = [("xst", 0), ("xst", 1), ("xst", 2), ("xst", 3), "hn0", "hn1", "junk", "lnwb", "mnT"]
        P.dve(lambda e: e.memset(dummy[:, 0:1], 0.0), writes=A1KEYS + ["Rfree"])

        KOFF = {0: 0, 1: 1152, 2: 2688}
        PG = {0: 128, 1: 512, 2: 1024}
        VOFF = {0: 0, 1: 9, 2: 21}
        NUMB = [0, 1]
        DENB = [2, 3]
        SCB = [4, 5, 6, 7]
        ec = [0]
        scc = [0]
        def att_load(hh, g):
            k = wslot()
            qk = wbuf[k][:, :].rearrange("p (kc t c) -> p kc t c", kc=16, t=2)
            qcol = g * 512 + hh * 128
            kcol = 1536 + g * 512 + hh * 128
            vcol = 3072 + g * 512 + hh * 128
            wdma(k, qk[:, :, 0, :], w_in_v[:, :, qcol:qcol + 128])
            wdma(k, qk[:, :, 1, :], w_in_v[:, :, kcol:kcol + 128])
            wv_, wvk = wload(w_in_v[:, :, vcol:vcol + 128], 16, 128)
            return qk, ("w", k), wv_, wvk

        HG_COL = {"f": 5632, "i": 6656, "q": 4608, "g": 7680}

        def hg_load(hp, nm):
            c0 = HG_COL[nm] + hp * 256
            return wload(w_in_v[:, :, c0:c0 + 256], 16, 256)

        def mg_load(cp):
            k = wslot()
            wab = wbuf[k]
            wa = wab[:, 0:1024].rearrange("p (a b) -> p a b", a=4)
            wb = wab[:, 1024:3072].rearrange("p (a b) -> p a b", a=8)
            wdma(k, wa, kview(w_ba)[:, :, cp * 256:(cp + 1) * 256])
            wdma(k, wb, kview(w_bb)[:, :, cp * 256:(cp + 1) * 256])
            ga, gak = wload(w_in_v[:, :, 8704 + cp * 256:8704 + (cp + 1) * 256], 16, 256)
            gb, gbk = wload(w_in_v[:, :, 10752 + cp * 256:10752 + (cp + 1) * 256], 16, 256)
            return wa, wb, ("w", k), ga, gak, gb, gbk

        def att_proj(hh, g):
            qk, kq, wv_, wvk = take(("att", hh, g), lambda: att_load(hh, g))
            if g < 2:
                prefetch(("att", hh, g + 1), lambda: att_load(hh, g + 1))
            elif hh < 3:
                prefetch(("att", hh + 1, 0), lambda: att_load(hh + 1, 0))
            else:
                for nm in ("f", "i", "q"):
                    prefetch(("hg", 0, nm), lambda nm=nm: hg_load(0, nm))
            for th in range(2):
                b = pbank()
                for kc in range(16):
                    mm(PB[b][:, :], qk[:, kc, 0, :], hT[:, 1, kc, th * 512:(th + 1) * 512], kc == 0, kc == 15,
                       [kq] + hkeys(1024 + th * 512, 1024 + (th + 1) * 512), [("ps", b)])
                evac(qT[:, g, th * 512:(th + 1) * 512], PB[b][:, :], [("ps", b), "Rfree"], [("qT", g)])
            segs = []
            w0 = 1024 - PG[g]
            while w0 < 2048:
                n = min(512, (1024 if w0 < 1024 else 2048) - w0)
                segs.append((w0, n))
                w0 += n
            for (w0, n) in segs:
                b = pbank()
                for kc in range(16):
                    mm(PB[b][:, 0:n], qk[:, kc, 1, :], hw(kc, w0, n), kc == 0, kc == 15,
                       [kq] + hkeys(w0, w0 + n), [("ps", b)])
                c0 = KOFF[g] + w0 - (1024 - PG[g])
                evac(kT[:, c0:c0 + n], PB[b][:, 0:n], [("ps", b), "Rfree"], [("kT", g)])
            tiles = []
            if g == 0:
                for n in range(9):
                    tiles.append((VOFF[0] + n, 896 + 128 * n, 1))
            elif g == 1:
                for r in range(4):
                    for n in range(3):
                        tiles.append((VOFF[1] + r * 3 + n, 512 + r + 512 * n, 4))
            else:
                for r in range(8):
                    for half in range(2):
                        tiles.append((VOFF[2] + r * 2 + half, half * 1024 + r, 8))
            for (ti, w0, step) in tiles:
                b = pbank()
                for kc in range(16):
                    mm(PB[b][:, 0:128], hw(kc, w0, 128, step), wv_[:, kc, :], kc == 0, kc == 15,
                       [wvk] + hkeys(w0, min(w0 + 128 * step, (w0 // 1024 + 1) * 1024)), [("ps", b)])
                evac(Vh[:, ti, :], PB[b][:, 0:128], [("ps", b), "Rfree"], [("V", g)])


        def att_unit(hh, g, mode):
            ei = (hh * 3 + g) % 2
            Ef = Ebuf[ei]
            if g < 2:
                E = Ef.rearrange("p (t c) -> p t c", c=256)
            else:
                E = Ef[:, 0:2048].rearrange("p (t c) -> p t c", c=128)
            ek = ("E", ei)

            def score(ti_e, kcols, qcols, ncol, ecol0, mask, bias):
                if mode != "score":
                    return
                sb_ = SCB[scc[0] % 4]
                scc[0] += 1
                mm(PB[sb_][:, 0:ncol], kcols, qcols, True, True, [("kT", g), ("qT", g)], [("ps", sb_)])
                e_ap = E[:, ti_e, ecol0:ecol0 + ncol]
                if bias is None:
                    P.act(lambda e: e.activation(out=e_ap, in_=PB[sb_][:, 0:ncol], func=AF.Exp, scale=SCALE),
                          reads=[("ps", sb_), "Rfree"], writes=[ek])
                else:
                    P.act(lambda e: e.activation(out=e_ap, in_=PB[sb_][:, 0:ncol], func=AF.Exp, scale=SCALE,
                                                 bias=bias),
                          reads=[("ps", sb_), "pm", "Rfree"], writes=[ek])
                P.pool(lambda e: e.tensor_tensor(out=e_ap, in0=e_ap, in1=mask, op=ALU.mult),
                       reads=[ek, "cb"], writes=[ek])

            def pv(col0, ncol, pairs):
                if mode != "pv":
                    return
                bn = NUMB[col0 // 512]
                bd = DENB[col0 // 512]
                c = col0 % 512
                for i, (vt, e_ap) in enumerate(pairs):
                    mm(PB[bn][:, c:c + ncol], Vh[:, vt, :], e_ap, i == 0, i == len(pairs) - 1,
                       [("V", g), ek], [("ps", bn)])
                for i, (vt, e_ap) in enumerate(pairs):
                    mm(PB[bd][:, c:c + ncol], ones, e_ap, i == 0, i == len(pairs) - 1,
                       [ek, "cb"], [("ps", bd)])

            if g == 0:
                for n in range(9):
                    kcols = kT[:, KOFF[0] + 128 * n:KOFF[0] + 128 * (n + 1)]
                    if n == 0:
                        score(n, kcols, qT[:, 0, 0:128], 128, 128, M01[:, 128:256], pm[:, 0:1])
                    elif n == 8:
                        score(n, kcols, qT[:, 0, 896:1024], 128, 0, M01[:, 0:128], None)
                    else:
                        score(n, kcols, qT[:, 0, 128 * (n - 1):128 * (n + 1)], 256, 0, M01, None)
                for qb in range(8):
                    pv(qb * 128, 128, [(VOFF[0] + qb, E[:, qb, 128:256]), (VOFF[0] + qb + 1, E[:, qb + 1, 0:128])])
            elif g == 1:
                for r in range(4):
                    for n in range(3):
                        kcols = kT[:, sl(KOFF[1] + r + 512 * n, 128, 4)]
                        ti_e = r * 3 + n
                        if n == 0:
                            score(ti_e, kcols, qT[:, 1, sl(r, 128, 4)], 128, 128, M01[:, 128:256], pm[:, 0:1])
                        elif n == 1:
                            score(ti_e, kcols, qT[:, 1, sl(r, 256, 4)], 256, 0, M01, None)
                        else:
                            score(ti_e, kcols, qT[:, 1, sl(r + 512, 128, 4)], 128, 0, M01[:, 0:128], None)
                for r in range(4):
                    for qb in range(2):
                        pv((r * 2 + qb) * 128, 128,
                           [(VOFF[1] + r * 3 + qb, E[:, r * 3 + qb, 128:256]),
                            (VOFF[1] + r * 3 + qb + 1, E[:, r * 3 + qb + 1, 0:128])])
            else:
                for r in range(8):
                    for half in range(2):
                        kcols = kT[:, sl(KOFF[2] + half * 1024 + r, 128, 8)]
                        score(r * 2 + half, kcols, qT[:, 2, sl(r, 128, 8)], 128, 0, MP2 if half == 0 else MO2,
                              pm[:, 0:1] if half == 0 else None)
                for r in range(8):
                    pv(r * 128, 128, [(VOFF[2] + r * 2, E[:, r * 2, 0:128]), (VOFF[2] + r * 2 + 1, E[:, r * 2 + 1, 0:128])])
            if mode == "pv":
                for (banks, acc, akey) in ((NUMB, acc_n, "acc_n"), (DENB, acc_d, "acc_d")):
                    for bi in range(2):
                        pb_ = PB[banks[bi]]
                        if g == 0:
                            dst = acc[:, bi * 512:(bi + 1) * 512]
                            P.dve(lambda e, dst=dst, pb_=pb_: e.tensor_copy(out=dst, in_=pb_[:, :]),
                                  reads=[("ps", banks[bi]), "Rfree"], writes=[akey])
                        else:
                            rr = 4 if g == 1 else 8
                            per = rr // 2
                            dst = acc.rearrange("p (m r) -> p r m", r=rr)[:, bi * per:(bi + 1) * per, :]
                            srcv = pb_[:, :].rearrange("p (r m) -> p r m", r=per)
                            P.dve(lambda e, dst=dst, srcv=srcv: e.tensor_tensor(out=dst, in0=dst, in1=srcv, op=ALU.add),
                                  reads=[("ps", banks[bi]), akey], writes=[akey])

        def att_fin(hh):
            P.dve(lambda e: e.reciprocal(out=acc_d, in_=acc_d), reads=["acc_d"], writes=["acc_d"])
            P.dve(lambda e, hh=hh: e.tensor_tensor(out=o_attT[:, hh, :], in0=acc_n, in1=acc_d, op=ALU.mult),
                  reads=["acc_n", "acc_d", "Rfree"], writes=["o_attT"])

        att_proj(0, 0)
        for u in range(12):
            hh, g = divmod(u, 3)
            att_unit(hh, g, "score")
            if u + 1 < 12:
                att_proj(*divmod(u + 1, 3))
            att_unit(hh, g, "pv")
            if g == 2:
                att_fin(hh)

        if "o_att" in dbg:
            for hh in range(4):
                for th in range(2):
                    P.dve(lambda e, hh=hh, th=th: e.tensor_copy(out=tmpf[0][:, :], in_=o_attT[:, hh, th * 512:(th + 1) * 512]),
                          reads=["o_attT"], writes=[("tmpf", 0)])
                    P.dma("sp", lambda e, hh=hh, th=th: e.dma_start(out=dbg["o_att"][hh * 2 + th], in_=tmpf[0][:, :]),
                          reads=[("tmpf", 0)], writes=["dbg"])

        if limit == 2:
            P.emit(st)
            return nc
        o_hgT = Rb(10240, 14336).rearrange("p (h n) -> p h n", h=8)
        HB = 0

        NPAR = 3
        PSZ = 3072

        def hscr(i, par, bf=False):
            if bf:
                base = HB + par * PSZ + 2048 + i * 128
                return R[:, base:base + 128].bitcast(BF16)
            base = HB + par * PSZ + i * 256
            return R[:, base:base + 256]
        ATT_KEYS = [("qT", g) for g in range(3)] + [("kT", g) for g in range(3)] + [("V", g) for g in range(3)] + \
                   ["acc_n", "acc_d", ("E", 0), ("E", 1)]
        P.dve(lambda e: e.memset(dummy[:, 1:2], 0.0), writes=ATT_KEYS + ["Rfree2"])

        for hp in range(4):
            c0 = hp * 256
            fw, fk = take(("hg", hp, "f"), lambda: hg_load(hp, "f"))
            iw, ik = take(("hg", hp, "i"), lambda: hg_load(hp, "i"))
            qw, qk_ = take(("hg", hp, "q"), lambda: hg_load(hp, "q"))
            gw, gk = take(("hg", hp, "g"), lambda: hg_load(hp, "g"))
            if hp < 3:
                for nm in ("f", "i", "q"):
                    prefetch(("hg", hp + 1, nm), lambda nm=nm: hg_load(hp + 1, nm))
            else:
                prefetch(("mg", 0), lambda: mg_load(0))
            P.dma("sp", lambda e, c0=c0: e.dma_start(out=lbb[:], in_=hg_lb[0:1, c0:c0 + 256].broadcast_to([128, 256])),
                  writes=["lbb"])
            P.dma("sp", lambda e, c0=c0: e.dma_start(out=lbt[:], in_=hg_lb[1:2, c0:c0 + 256].broadcast_to([128, 256])),
                  writes=["lbt"])
            P.dve(lambda e: e.tensor_tensor(out=lbb[:], in0=lbb[:], in1=lbt[:], op=ALU.subtract),
                  reads=["lbb", "lbt"], writes=["lbb"])
            P.act(lambda e: e.activation(out=lbb[:], in_=lbb[:], func=AF.Exp, scale=-1.0), reads=["lbb"], writes=["lbb"])
            P.dve(lambda e: e.tensor_scalar(out=lbb[:], in0=lbb[:], scalar1=1.0, scalar2=None, op0=ALU.add),
                  reads=["lbb"], writes=["lbb"])
            P.dve(lambda e: e.reciprocal(out=lbb[:], in_=lbb[:]), reads=["lbb"], writes=["lbb"])
            P.dve(lambda e: e.tensor_scalar(out=oml[:], in0=lbb[:], scalar1=-1.0, scalar2=1.0, op0=ALU.mult, op1=ALU.add),
                  reads=["lbb"], writes=["oml"])
            P.dve(lambda e: e.memset(state[:], 0.0), writes=["state"])

            def tvars(wt):
                par = wt % NPAR
                d = dict(own=wt >= 8, par=par, half=wt // 8, tc0=(wt % 8) * 128, hk=[("h", wt)],
                         K=(lambda name, par=par: (name, par)), ecs=ecsb[par])
                names = ("sig", "f_", "kk", "e2", "e1", "e1n", "sq", "sg")
                for i, n in enumerate(names):
                    d[n] = hscr(i, par)
                for i, n in enumerate(("v_b", "kl", "ke", "qe", "og")):
                    d[n] = hscr(i, par, True)
                b0 = HB + par * PSZ + 2688
                d["qkT"] = R[:, b0:b0 + 256].bitcast(BF16).rearrange("p (a b) -> p a b", a=4)
                d["aT"] = R[:, b0 + 256:b0 + 384].bitcast(BF16).rearrange("p (a b) -> p a b", a=2)
                return d

            def sigm_from_exp(buf, key):
                P.dve(lambda e: e.tensor_scalar(out=buf, in0=buf, scalar1=1.0, scalar2=None, op0=ALU.add),
                      reads=[key], writes=[key])
                P.dve(lambda e: e.reciprocal(out=buf, in_=buf), reads=[key], writes=[key])

            def pe_fill(n):
                for _ in range(n):
                    mm(PB[6][:, 256:512], ident, hT[:, 1, 0, 0:256], True, True, ["cb", ("h", 8), ("h", 9)], [("ps", 6)])

            def hg_f1(wt):
                v = tvars(wt)
                K, own, half, tc0, hk = v["K"], v["own"], v["half"], v["tc0"], v["hk"]
                sig, f_, kk, sq, sg, v_b = v["sig"], v["f_"], v["kk"], v["sq"], v["sg"], v["v_b"]
                bf_ = pbank()
                for kc in range(16):
                    mm(PB[bf_][:, 0:256], hT[:, half, kc, tc0:tc0 + 128], fw[:, kc, :], kc == 0, kc == 15, [fk] + hk, [("ps", bf_)])
                for kc in range(16):
                    mm(PB[bf_][:, 256:512], hT[:, half, kc, tc0:tc0 + 128], iw[:, kc, :], kc == 0, kc == 15, [ik] + hk, [("ps", bf_)])
                P.act(lambda e: e.activation(out=sig, in_=PB[bf_][:, 0:256], func=AF.Sigmoid),
                      reads=[("ps", bf_), "Rfree2"], writes=[K("sig")])
                P.dve(lambda e: e.tensor_copy(out=v_b, in_=PB[bf_][:, 256:512]),
                      reads=[("ps", bf_), "Rfree2"], writes=[K("v")])
                if own:
                    bq = pbank()
                    for kc in range(16):
                        mm(PB[bq][:, 0:256], hT[:, half, kc, tc0:tc0 + 128], qw[:, kc, :], kc == 0, kc == 15, [qk_] + hk, [("ps", bq)])
                    for kc in range(16):
                        mm(PB[bq][:, 256:512], hT[:, half, kc, tc0:tc0 + 128], gw[:, kc, :], kc == 0, kc == 15, [gk] + hk, [("ps", bq)])
                    P.act(lambda e: e.activation(out=sq, in_=PB[bq][:, 0:256], func=AF.Sigmoid),
                          reads=[("ps", bq), "Rfree2"], writes=[K("sq")])
                    P.act(lambda e: e.activation(out=sg, in_=PB[bq][:, 256:512], func=AF.Sigmoid),
                          reads=[("ps", bq), "Rfree2"], writes=[K("sg")])
                    P.dve(lambda e: e.tensor_tensor(out=sq, in0=PB[bq][:, 0:256], in1=sq, op=ALU.mult),
                          reads=[("ps", bq), K("sq")], writes=[K("sq")])
                    P.dve(lambda e: e.tensor_tensor(out=sg, in0=PB[bq][:, 256:512], in1=sg, op=ALU.mult),
                          reads=[("ps", bq), K("sg")], writes=[K("sg")])
                P.pool(lambda e: e.tensor_tensor(out=sig, in0=sig, in1=oml[:], op=ALU.mult),
                       reads=[K("sig"), "oml"], writes=[K("sig")])
                P.pool(lambda e: e.tensor_tensor(out=f_, in0=sig, in1=lbb[:], op=ALU.add),
                       reads=[K("sig"), "lbb", "Rfree2"], writes=[K("f")])
                P.pool(lambda e: e.tensor_tensor(out=kk, in0=oml[:], in1=sig, op=ALU.subtract),
                       reads=[K("sig"), "oml", "Rfree2"], writes=[K("kk")])
                P.act(lambda e: e.activation(out=f_, in_=f_, func=AF.Ln), reads=[K("f")], writes=[K("f")])

            def hg_f2(wt):
                v = tvars(wt)
                K, own, ecs = v["K"], v["own"], v["ecs"]
                f_, kk, e2, e1, e1n, sq = v["f_"], v["kk"], v["e2"], v["e1"], v["e1n"], v["sq"]
                kl, ke, qe, qkT, aT = v["kl"], v["ke"], v["qe"], v["qkT"], v["aT"]
                bd_ = pbank()
                mm(PB[bd_][:, 256:512], TD2, f_, True, True, [K("f"), "cf"], [("ps", bd_)])
                if own:
                    mm(PB[bd_][:, 0:256], TD1, f_, True, True, [K("f"), "cf"], [("ps", bd_)])
                for j in range(2):
                    mm(CSB[:, j * 4:(j + 1) * 4], f_[:, j * 128:(j + 1) * 128], SEL, True, True,
                       [K("f"), "cf"], [("ps", 7)])
                P.act(lambda e: e.activation(out=ecs[:], in_=CSB[:, 0:8], func=AF.Exp), reads=[("ps", 7)], writes=[K("ecs")])
                P.act(lambda e: e.activation(out=e2, in_=PB[bd_][:, 256:512], func=AF.Exp),
                      reads=[("ps", bd_), "Rfree2"], writes=[K("e2")])
                P.pool(lambda e: e.tensor_tensor(out=kl, in0=kk, in1=e2, op=ALU.mult),
                       reads=[K("kk"), K("e2"), "Rfree2"], writes=[K("kl")])
                if own:
                    P.act(lambda e: e.activation(out=e1, in_=PB[bd_][:, 0:256], func=AF.Exp),
                          reads=[("ps", bd_), "Rfree2"], writes=[K("e1")])
                    P.act(lambda e: e.activation(out=e1n, in_=PB[bd_][:, 0:256], func=AF.Exp, scale=-1.0),
                          reads=[("ps", bd_), "Rfree2"], writes=[K("e1n")])
                    P.pool(lambda e: e.tensor_tensor(out=ke, in0=kk, in1=e1n, op=ALU.mult),
                           reads=[K("kk"), K("e1n"), "Rfree2"], writes=[K("ke")])
                    P.dve(lambda e: e.tensor_tensor(out=qe, in0=sq, in1=e1, op=ALU.mult),
                          reads=[K("sq"), K("e1"), "Rfree2"], writes=[K("qe")])
                    pe_fill(7)
                    slot = tbc[0] % 2
                    tbc[0] += 1
                    for j in range(2):
                        P.pe(lambda e, j=j: e.transpose(out=TBk[slot][:, j * 128:(j + 1) * 128],
                                                        in_=qe[:, j * 128:(j + 1) * 128], identity=ident),
                             reads=[K("qe"), "cb"], writes=[("ps", 6 + slot)])
                    for j in range(2):
                        P.pe(lambda e, j=j: e.transpose(out=TBk[slot][:, (2 + j) * 128:(3 + j) * 128],
                                                        in_=ke[:, j * 128:(j + 1) * 128], identity=ident),
                             reads=[K("ke"), "cb"], writes=[("ps", 6 + slot)])
                    evac(qkT, TBk[slot][:, 0:512].rearrange("p (a b) -> p a b", a=4),
                         [("ps", 6 + slot), "Rfree2"], [K("qkT")])
                    ba = pbank()
                    for j in range(2):
                        mm(PB[ba][:, j * 128:(j + 1) * 128], qkT[:, 2 + j, :], qkT[:, j, :], True, True, [K("qkT")], [("ps", ba)])
                    for j in range(2):
                        P.dve(lambda e, j=j: e.tensor_tensor(out=aT[:, j, :], in0=PB[ba][:, j * 128:(j + 1) * 128],
                                                             in1=MBLK, op=ALU.mult),
                              reads=[("ps", ba), "cb", "Rfree2"], writes=[K("aT")])

            def hg_b(wt):
                v = tvars(wt)
                K, own, ecs, tc0 = v["K"], v["own"], v["ecs"], v["tc0"]
                sq, sg, v_b, kl, og, qkT, aT = v["sq"], v["sg"], v["v_b"], v["kl"], v["og"], v["qkT"], v["aT"]
                on = sq
                if own:
                    for j in range(2):
                        mm(PB[5][:, j * 128:(j + 1) * 128], aT[:, j, :], v_b[:, j * 128:(j + 1) * 128], j == 0, False,
                           [K("aT"), K("v")], [("ps", 5)], skip=True)
                for ch in range(2):
                    r0 = ch * 64
                    if own:
                        for j in range(2):
                            mm(PB[5][r0:r0 + 64, j * 128:(j + 1) * 128], qkT[:, j, r0:r0 + 64], st_sb[ch][:, j, :], False,
                               (ch == 1), [K("qkT"), ("st_s", ch)], [("ps", 5)], skip=True)
                    bc_ = pbank()
                    for j in range(2):
                        mm(PB[bc_][:, j * 128:(j + 1) * 128], kl[r0:r0 + 64, j * 128:(j + 1) * 128],
                           v_b[r0:r0 + 64, j * 128:(j + 1) * 128], True, True, [K("kl"), K("v")], [("ps", bc_)])
                    for j in range(2):
                        P.dve(lambda e, j=j, bc_=bc_: e.scalar_tensor_tensor(
                            out=state[:, j, :], in0=state[:, j, :], scalar=ecs[:, j * 4 + 2 + ch:j * 4 + 3 + ch],
                            in1=PB[bc_][:, j * 128:(j + 1) * 128], op0=ALU.mult, op1=ALU.add),
                              reads=["state", K("ecs"), ("ps", bc_)], writes=["state"])
                    if ch == 0:
                        nxt_ok, ecs_n, key_n, col = own, ecs, K("ecs"), 1
                    else:
                        nxt_ok = (wt + 1 < 16) and (wt + 1 >= 8)
                        ecs_n, key_n, col = ecsb[(wt + 1) % NPAR], ("ecs", (wt + 1) % NPAR), 0
                    if nxt_ok:
                        for j in range(2):
                            P.dve(lambda e, j=j, ecs_n=ecs_n, col=col, ch=ch: e.tensor_scalar(
                                out=st_sb[1 - ch][:, j, :], in0=state[:, j, :], scalar1=ecs_n[:, j * 4 + col:j * 4 + col + 1],
                                scalar2=None, op0=ALU.mult),
                                  reads=["state", key_n], writes=[("st_s", 1 - ch)])
                if own:
                    for j in range(2):
                        P.act(lambda e, j=j: e.activation(out=on[:, j * 128:(j + 1) * 128], in_=PB[5][:, j * 128:(j + 1) * 128],
                                                          func=AF.Square, accum_out=ssv[:, 16 + j:17 + j]),
                              reads=[("ps", 5), K("qe")], writes=[K("sq"), ("hss", j)])
                    P.dve(lambda e: e.tensor_scalar(out=rsv[:, 16:18], in0=ssv[:, 16:18], scalar1=1.0 / 128, scalar2=EPS,
                                                    op0=ALU.mult, op1=ALU.add),
                          reads=[("hss", 0), ("hss", 1)], writes=["hrs"])
                    P.act(lambda e: e.activation(out=rsv[:, 16:18], in_=rsv[:, 16:18], func=AF.Ln),
                          reads=["hrs"], writes=["hrs"])
                    P.act(lambda e: e.activation(out=rsv[:, 16:18], in_=rsv[:, 16:18], func=AF.Exp, scale=-0.5),
                          reads=["hrs"], writes=["hrs"])
                    for j in range(2):
                        P.dve(lambda e, j=j: e.scalar_tensor_tensor(
                            out=on[:, j * 128:(j + 1) * 128], in0=PB[5][:, j * 128:(j + 1) * 128], scalar=rsv[:, 16 + j:17 + j],
                            in1=hgnb[:], op0=ALU.mult, op1=ALU.mult),
                              reads=[("ps", 5), "hrs", "hgnb"], writes=[K("sq")])
                    P.pool(lambda e: e.tensor_tensor(out=og, in0=on, in1=sg, op=ALU.mult),
                           reads=[K("sq"), K("sg"), "Rfree2"], writes=[K("og")])
                    pe_fill(7)
                    slot = tbc[0] % 2
                    tbc[0] += 1
                    for j in range(2):
                        P.pe(lambda e, j=j: e.transpose(out=TBk[slot][:, j * 128:(j + 1) * 128],
                                                        in_=og[:, j * 128:(j + 1) * 128], identity=ident),
                             reads=[K("og"), "cb"], writes=[("ps", 6 + slot)])
                    evac(o_hgT[:, hp * 2:hp * 2 + 2, tc0:tc0 + 128],
                         TBk[slot][:, 0:256].rearrange("p (a b) -> p a b", a=2),
                         [("ps", 6 + slot), "Rfree2"], ["o_hgT"])

            for it in range(-2, 16):
                if 0 <= it + 2 < 16:
                    hg_f1(it + 2)
                if 0 <= it + 1 < 16:
                    hg_f2(it + 1)
                if it >= 0:
                    hg_b(it)

        if "o_hg" in dbg:
            for hh in range(8):
                for th in range(2):
                    P.dve(lambda e, hh=hh, th=th: e.tensor_copy(out=tmpf[0][:, :], in_=o_hgT[:, hh, th * 512:(th + 1) * 512]),
                          reads=["o_hgT"], writes=[("tmpf", 0)])
                    P.dma("sp", lambda e, hh=hh, th=th: e.dma_start(out=dbg["o_hg"][hh * 2 + th], in_=tmpf[0][:, :]),
                          reads=[("tmpf", 0)], writes=["dbg"])

        if limit == 3:
            P.emit(st)
            return nc
        HG_KEYS = [(n, p) for n in ("sig", "f", "kk", "e2", "e1", "e1n", "sq", "sg", "v", "kl", "ke", "qe",
                                    "og", "qkT", "aT") for p in range(3)]
        P.dve(lambda e: e.memset(dummy[:, 2:3], 0.0), writes=HG_KEYS + hkeys(0, 1024) + ["Rfree3"])
        npb[0] = 7
        mg = [R[:, i * 512:(i + 1) * 512] for i in range(8)]
        w_ba_v = kview(w_ba)
        w_bb_v = kview(w_bb)
        mi = [0]
        for cp in range(8):
            wa, wb, kab, ga, gak, gb, gbk = take(("mg", cp), lambda: mg_load(cp))
            if cp < 7:
                prefetch(("mg", cp + 1), lambda: mg_load(cp + 1))
            for cc in range(2):
                c = cp * 2 + cc
                for th in range(2):
                    tsl = slice(th * 512, (th + 1) * 512)
                    hk = hkeys(1024 + th * 512, 1024 + (th + 1) * 512)
                    b_a, b_b, b_ga, b_gb = pbank(), pbank(), pbank(), pbank()
                    for kc in range(4):
                        mm(PB[b_a][:, :], wa[:, kc, cc * 128:(cc + 1) * 128], o_attT[:, kc, tsl], kc == 0, kc == 3,
                           [kab, "o_attT"], [("ps", b_a)])
                    for kc in range(8):
                        mm(PB[b_b][:, :], wb[:, kc, cc * 128:(cc + 1) * 128], o_hgT[:, kc, tsl], kc == 0, kc == 7,
                           [kab, "o_hgT"], [("ps", b_b)])
                    for kc in range(16):
                        mm(PB[b_ga][:, :], ga[:, kc, cc * 128:(cc + 1) * 128], hT[:, 1, kc, tsl], kc == 0, kc == 15,
                           [gak] + hk, [("ps", b_ga)])
                    for kc in range(16):
                        mm(PB[b_gb][:, :], gb[:, kc, cc * 128:(cc + 1) * 128], hT[:, 1, kc, tsl], kc == 0, kc == 15,
                           [gbk] + hk, [("ps", b_gb)])
                    par = mi[0] % 2
                    mi[0] += 1
                    sa, sbb, m1, m2 = (mg[par * 4 + i] for i in range(4))
                    MK = lambda n: ("mg", n, par)
                    P.act(lambda e, sa=sa, b_ga=b_ga: e.activation(out=sa, in_=PB[b_ga][:, :], func=AF.Sigmoid),
                          reads=[("ps", b_ga), "Rfree3"], writes=[MK(0)])
                    P.act(lambda e, sbb=sbb, b_gb=b_gb: e.activation(out=sbb, in_=PB[b_gb][:, :], func=AF.Sigmoid),
                          reads=[("ps", b_gb), "Rfree3"], writes=[MK(1)])
                    P.dve(lambda e, sa=sa, m1=m1, b_a=b_a: e.tensor_tensor(out=m1, in0=PB[b_a][:, :], in1=sa, op=ALU.mult),
                          reads=[("ps", b_a), MK(0), "Rfree3"], writes=[MK(2)])
                    P.dve(lambda e, sbb=sbb, m2=m2, b_b=b_b: e.tensor_tensor(out=m2, in0=PB[b_b][:, :], in1=sbb, op=ALU.mult),
                          reads=[("ps", b_b), MK(1), "Rfree3"], writes=[MK(3)])
                    P.pool(lambda e, m1=m1, m2=m2, c=c, tsl=tsl: e.tensor_tensor(out=hT[:, 0, c, tsl], in0=m1, in1=m2, op=ALU.add),
                           reads=[MK(2), MK(3), "Rfree3"], writes=[("mT", c)])

        MG_KEYS = [("mg", n, p) for n in range(4) for p in range(2)]
        P.dve(lambda e: e.memset(dummy[:, 3:4], 0.0), writes=MG_KEYS + ["o_hgT", "o_attT", "Rfree4"])
        for t in range(8):
            P.dma("sp", lambda e, t=t: e.dma_start(out=xs[:, t, :], in_=xo[t * 128:(t + 1) * 128, :]),
                  reads=["Rfree4"], writes=[("x", t)])
        w_out_v = kview(w_out)
        MT_ALL = [("mT", c) for c in range(16)]
        for nb in range(8):
            wo_, wok = wload(w_out_v[:, :, nb * 256:(nb + 1) * 256], 16, 256)
            for t in range(8):
                b = pbank()
                for kc in range(16):
                    mm(PB[b][:, 0:256], hT[:, 0, kc, t * 128:(t + 1) * 128], wo_[:, kc, :], kc == 0, kc == 15,
                       [wok] + MT_ALL, [("ps", b)])
                P.dve(lambda e, b=b, t=t, nb=nb: e.tensor_tensor(out=xs[:, t, nb * 256:(nb + 1) * 256],
                                                              in0=xs[:, t, nb * 256:(nb + 1) * 256], in1=PB[b][:, 0:256], op=ALU.add),
                      reads=[("ps", b), ("x", t)], writes=[("x", t)])

        def dump_x(name):
            if name in dbg:
                for t in range(8):
                    P.dma("sp", lambda e, t=t: e.dma_start(out=dbg[name][t * 128:(t + 1) * 128, :], in_=xs[:, t, :]),
                          reads=[("x", t)], writes=["dbg"])
        dump_x("x1")

        if limit == 4:
            P.emit(st)
            return nc
        hnB2 = [(Hprev[:, 0:2048], "hnB"), (Hprev[:, 2048:4096], "hnB1")]
        hnB = hnB2[0]
        junkB = Hprev[:, 2048:4096]
        junkBC = Hprev[:, 14336:16384]
        lnwbB = Hprev[:, 4096:8192].bitcast(F32)
        bufX = Hprev[:, 8192:12288]
        bufY = Hprev[:, 12288:16384]
        P.dve(lambda e: e.memset(dummy[:, 4:5], 0.0), writes=MT_ALL + ["hnB", "hnB1", "junk", "lnwb", "Hfree"])

        def norm_phase(li):
            P.dma("sp", lambda e: e.dma_start(out=lnwbB, in_=lnw[li:li + 1, :].broadcast_to([128, D])),
                  reads=["Hfree"], writes=["lnwb"])
            norm_s1(xs[:, 0, :], ("x", 0), junkBC, 0)
            for t in range(8):
                if t + 1 < 8:
                    norm_s1(xs[:, t + 1, :], ("x", t + 1), junkBC, t + 1)
                norm_s2(xs[:, t, :], ("x", t), lnwbB, hnB2[t % 2], t,
                        lambda kc0, t=t: hT[:, 1, kc0:kc0 + 4, t * 128:(t + 1) * 128], [("h", 8 + t)])

        npb[0] = 5
        norm_phase(1)
        npb[0] = 7
        qcT = bufX.rearrange("p (h n) -> p h n", h=4)
        ocT = bufY.rearrange("p (h n) -> p h n", h=4)
        wq_v = kview(wq_c)
        for hh2 in range(2):
            wq_, wqk = wload(wq_v[:, :, hh2 * 256:(hh2 + 1) * 256], 16, 256)
            for j in range(2):
                hh = hh2 * 2 + j
                for th in range(2):
                    b = pbank()
                    for kc in range(16):
                        mm(PB[b][:, :], wq_[:, kc, j * 128:(j + 1) * 128], hT[:, 1, kc, th * 512:(th + 1) * 512],
                           kc == 0, kc == 15, [wqk] + hkeys(1024 + th * 512, 1536 + th * 512), [("ps", b)])
                    evac(qcT[:, hh, th * 512:(th + 1) * 512], PB[b][:, :], [("ps", b), "Hfree"], ["qcT"])
        Ecr = [tmpf[0][:, :].bitcast(BF16).rearrange("p (m n) -> p m n", m=2),
               tmpf[1][:, :].bitcast(BF16).rearrange("p (m n) -> p m n", m=2)]
        rdn = [R_ for R_ in (sb("rdn0", [128, 512], F32), sb("rdn1", [128, 512], F32))]
        ci = [0]
        for hh in range(4):
            for th in range(2):
                par = ci[0] % 2
                ci[0] += 1
                for mt in range(2):
                    b = pbank()
                    mm(PB[b][:, :], kcT[:, hh, mt * 128:(mt + 1) * 128], qcT[:, hh, th * 512:(th + 1) * 512], True, True,
                       ["kcT", "qcT"], [("ps", b)])
                    P.act(lambda e, b=b, mt=mt, par=par: e.activation(out=Ecr[par][:, mt, :], in_=PB[b][:, :], func=AF.Exp,
                                                                      scale=SCALE),
                          reads=[("ps", b)], writes=[("tmpf", par)])
                bn, bd = pbank(), pbank()
                for mt in range(2):
                    mm(PB[bn][:, :], vc[:, mt, hh * 128:(hh + 1) * 128], Ecr[par][:, mt, :], mt == 0, mt == 1,
                       ["vc", ("tmpf", par)], [("ps", bn)])
                for mt in range(2):
                    mm(PB[bd][:, :], ones, Ecr[par][:, mt, :], mt == 0, mt == 1, ["cb", ("tmpf", par)], [("ps", bd)])
                P.dve(lambda e, bd=bd, par=par: e.reciprocal(out=rdn[par][:, :], in_=PB[bd][:, :]),
                      reads=[("ps", bd)], writes=[("rdn", par)])
                P.dve(lambda e, bn=bn, par=par, hh=hh, th=th: e.tensor_tensor(out=ocT[:, hh, th * 512:(th + 1) * 512],
                                                                           in0=PB[bn][:, :], in1=rdn[par][:, :], op=ALU.mult),
                      reads=[("ps", bn), ("rdn", par), "Hfree"], writes=["ocT"])
        wo_v = kview(wo_c)
        for nb in range(2):
            wo2, wo2k = wload(wo_v[:, :, nb * 1024:(nb + 1) * 1024], 4, 1024)
            for t in range(8):
                for n2 in range(2):
                    b = pbank()
                    for kc in range(4):
                        mm(PB[b][:, :], ocT[:, kc, t * 128:(t + 1) * 128], wo2[:, kc, n2 * 512:(n2 + 1) * 512], kc == 0, kc == 3,
                           [wo2k, "ocT"], [("ps", b)])
                    c0 = nb * 1024 + n2 * 512
                    P.dve(lambda e, b=b, t=t, c0=c0: e.tensor_tensor(out=xs[:, t, c0:c0 + 512], in0=xs[:, t, c0:c0 + 512],
                                                                  in1=PB[b][:, :], op=ALU.add),
                          reads=[("ps", b), ("x", t)], writes=[("x", t)])
        dump_x("x2")

        if limit == 5:
            P.emit(st)
            return nc
        P.dve(lambda e: e.memset(dummy[:, 5:6], 0.0), writes=["qcT", "ocT", "hnB", "hnB1", "junk", "lnwb", "Hfree"])
        npb[0] = 5
        norm_phase(3)
        npb[0] = 7
        uT = [bufX.rearrange("p (j n) -> p j n", j=2), bufY.rearrange("p (j n) -> p j n", j=2)]
        uT = [bufX[:, 0:2048].rearrange("p (j n) -> p j n", j=2), bufX[:, 2048:4096].rearrange("p (j n) -> p j n", j=2),
              bufY[:, 0:2048].rearrange("p (j n) -> p j n", j=2)]
        w1_v, w3_v, w2_v = kview(w1), kview(w3), kview(w2)
        NFB = 22
        HOWN = hkeys(1024, 2048)
        sil = [tmpf[0], tmpf[1]]
        sic = [0]

        def ffn_up(fb):
            w1b, w1k = wload(w1_v[:, :, fb * 256:(fb + 1) * 256], 16, 256)
            w3b, w3k = wload(w3_v[:, :, fb * 256:(fb + 1) * 256], 16, 256)
            u = uT[fb % 3]
            for j in range(2):
                for th in range(2):
                    tsl = slice(th * 512, (th + 1) * 512)
                    b1, b3 = pbank(), pbank()
                    for kc in range(16):
                        mm(PB[b1][:, :], w1b[:, kc, j * 128:(j + 1) * 128], hT[:, 1, kc, tsl], kc == 0, kc == 15,
                           [w1k] + HOWN, [("ps", b1)])
                    for kc in range(16):
                        mm(PB[b3][:, :], w3b[:, kc, j * 128:(j + 1) * 128], hT[:, 1, kc, tsl], kc == 0, kc == 15,
                           [w3k] + HOWN, [("ps", b3)])
                    par = sic[0] % 2
                    sic[0] += 1
                    P.act(lambda e, b1=b1, par=par: e.activation(out=sil[par][:, :], in_=PB[b1][:, :], func=AF.Silu),
                          reads=[("ps", b1)], writes=[("tmpf", par)])
                    P.dve(lambda e, b3=b3, par=par, u=u, j=j, tsl=tsl: e.tensor_tensor(out=u[:, j, tsl], in0=PB[b3][:, :],
                                                                                    in1=sil[par][:, :], op=ALU.mult),
                          reads=[("ps", b3), ("tmpf", par), "Hfree"], writes=[("uT", fb % 3)])

        def final_tile(t):
            par = t % 2
            norm_tile(xs[:, t, :], ("x", t), lnwbF, None, junkF, t, None, None, out_f32=(ystF[par], ("yst", par)))
            P.dma("sp", lambda e, t=t, par=par: e.dma_start(out=y[t * 128:(t + 1) * 128, :], in_=ystF[par]),
                  reads=[("yst", par)], writes=["y"])

        def ffn_down(fb):
            w2b, w2k = wload(w2_v[:, fb * 2:(fb + 1) * 2, :], 2, 2048)
            u = uT[fb % 3]
            last = (fb == NFB - 1)
            if last:
                P.dma("sp", lambda e: e.dma_start(out=lnwbF, in_=lnw[4:5, :].broadcast_to([128, D])),
                      reads=["Hfree"], writes=["lnwb"])
            for t in range(8):
                for nb in range(4):
                    b = pbank()
                    for j in range(2):
                        mm(PB[b][:, :], u[:, j, t * 128:(t + 1) * 128], w2b[:, j, nb * 512:(nb + 1) * 512], j == 0, j == 1,
                           [w2k, ("uT", fb % 3)], [("ps", b)])
                    P.dve(lambda e, b=b, t=t, nb=nb: e.tensor_tensor(out=xs[:, t, nb * 512:(nb + 1) * 512],
                                                                  in0=xs[:, t, nb * 512:(nb + 1) * 512], in1=PB[b][:, :], op=ALU.add),
                          reads=[("ps", b), ("x", t)], writes=[("x", t)])
                if last:
                    final_tile(t)

        lnwbF = lnwbB
        ystF = [bufY.bitcast(F32), Hprev[:, 0:4096].bitcast(F32)]
        junkF = bufX[:, 2048:4096]
        ffn_up(0)
        for fb in range(NFB):
            if fb + 1 < NFB:
                ffn_up(fb + 1)
            ffn_down(fb)
        dump_x("x3")

        if limit == 6:
            P.emit(st)
            return nc

        P.emit(st)
    return nc


def _const_tables():
    c = np.zeros((128, 264), np.float32)
    s = np.arange(128)[:, None]
    cc = np.arange(128)[None, :]
    same = (s // 64) == (cc // 64)
    pos_s = s % 64
    tri = same & (s <= cc)
    triref = same & (pos_s <= 32)
    c[:, 0:128] = tri.astype(np.float32) - triref.astype(np.float32)
    c[:, 128:256] = (same & (s > cc)).astype(np.float32)
    sv = np.arange(128)
    c[:, 256] = ((sv < 64) & (sv % 64 <= 32))
    c[:, 257] = ((sv >= 64) & (sv % 64 <= 32))
    c[:, 258] = (sv < 64)
    c[:, 259] = (sv >= 64)
    b = np.zeros((128, 1024), np.float32)
    b[:, 0:128] = np.eye(128)
    k = np.arange(128)[:, None]
    q = np.arange(128)[None, :]
    b[:, 128:256] = (k <= q)
    b[:, 256:384] = (k >= q)
    q64 = np.arange(64)[None, :]
    b[:, 384:448] = (k <= 64 + q64)
    b[:, 448:576] = (same & (cc >= s))
    b[:, 576:704] = 1.0
    b[:, 704:832] = ((k % 2) == (q % 2))
    b[:, 832:960] = (((k % 2) == (q % 2)) & (k <= q))
    return c, b


_NC_CACHE = {}


def _prep_inputs(inputs):
    x = np.asarray(inputs["x"], np.float32)
    mem = np.asarray(inputs["mem"], np.float32)
    lnw = np.stack([np.asarray(inputs["ln_mix_w"], np.float32)[0], np.asarray(inputs["ln_cross_w"], np.float32)[0],
                    np.asarray(inputs["ln_mem_w"], np.float32)[0], np.asarray(inputs["ln_ffn_w"], np.float32)[0],
                    np.asarray(inputs["ln_final_w"], np.float32)], axis=0)
    cst, cstb = _const_tables()
    shared = {
        "w_in": np.ascontiguousarray(inputs["w_in"][0], np.float32),
        "hg_lb": np.ascontiguousarray(inputs["hg_lower_bounds"], np.float32),
        "hg_nw": np.ascontiguousarray(inputs["hg_norm_w"], np.float32).reshape(1, 128),
        "w_ba": np.ascontiguousarray(inputs["w_branch_a"][0], np.float32),
        "w_bb": np.ascontiguousarray(inputs["w_branch_b"][0], np.float32),
        "w_out": np.ascontiguousarray(inputs["w_out"][0], np.float32),
        "wq_c": np.ascontiguousarray(inputs["wq_cross"][0], np.float32),
        "wkv_c": np.ascontiguousarray(inputs["wkv_cross"][0], np.float32),
        "wo_c": np.ascontiguousarray(inputs["wo_cross"][0], np.float32),
        "w1": np.ascontiguousarray(inputs["w1"][0], np.float32),
        "w3": np.ascontiguousarray(inputs["w3"][0], np.float32),
        "w2": np.ascontiguousarray(inputs["w2"][0], np.float32),
        "lnw": np.ascontiguousarray(lnw),
        "cst": cst,
        "cstb": cstb,
    }
    in_maps = []
    for c in range(8):
        b, h = c // 2, c % 2
        m = dict(shared)
        m["xo"] = np.ascontiguousarray(x[b, h * T:(h + 1) * T])
        pmk = np.zeros((128, 2), np.float32)
        if h == 1:
            m["xp"] = np.ascontiguousarray(x[b, 0:T])
        else:
            m["xp"] = np.zeros((T, D), np.float32)
            pmk[:, 0] = -30000.0
            pmk[0:64, 1] = -30000.0
        m["pmk"] = pmk
        m["memb"] = np.ascontiguousarray(mem[b])
        in_maps.append(m)
    return in_maps


def kernel(**inputs):
    if "nc" not in _NC_CACHE:
        _NC_CACHE["nc"] = build()
    nc = _NC_CACHE["nc"]
    in_maps = _prep_inputs(inputs)
    res = run_bass_kernel_spmd(nc, in_maps, core_ids=list(range(8)))
    out = np.zeros((4, 2048, D), np.float32)
    for c in range(8):
        b, h = c // 2, c % 2
        out[b, h * T:(h + 1) * T] = res.results[c]["y"]
    return out
```

```python
import numpy as np
from contextlib import ExitStack
import concourse.bass as bass
import concourse.mybir as mybir
from concourse.bass_utils import run_bass_kernel_spmd

F32 = mybir.dt.float32
BF16 = mybir.dt.bfloat16
AF = mybir.ActivationFunctionType
ALU = mybir.AluOpType

ENGS = ("pe", "act", "dve", "pool", "sp")
D = 2048
T = 1024
EPS = 1e-6
NW = 7
SCALE = 128.0 ** -0.5


class Op:
    __slots__ = ("eng", "fn", "dma", "idx", "eidx", "deps", "seq", "dsem", "dval", "has_dep")

    def __init__(self, eng, fn, dma):
        self.eng = eng
        self.fn = fn
        self.dma = dma
        self.deps = []
        self.seq = None
        self.dsem = None
        self.dval = None
        self.has_dep = False


class Prog:
    N_DMA_SEMS = 40

    def __init__(self, nc):
        self.nc = nc
        self.ops = []
        self.last_w = {}
        self.readers = {}
        self.ecount = {e: 0 for e in ENGS}

    def op(self, eng, fn, reads=(), writes=(), dma=False):
        xk = [k for k in reads if isinstance(k, tuple) and k[0] in ("ps", "tb")]
        if xk:
            reads = [k for k in reads if k not in xk]
            writes = list(writes) + [k for k in xk if k not in writes]
        o = Op(eng, fn, dma)
        o.idx = len(self.ops)
        o.eidx = self.ecount[eng]
        self.ecount[eng] += 1
        deps = {}
        for k in reads:
            w = self.last_w.get(k)
            if w is not None:
                deps[w.idx] = (w, "raw")
        for k in writes:
            w = self.last_w.get(k)
            if w is not None and w.idx not in deps:
                deps[w.idx] = (w, "waw")
            rd = self.readers.get(k)
            if rd:
                for r in rd.values():
                    if r.idx not in deps and r is not o:
                        deps[r.idx] = (r, "war")
        for k in reads:
            rd = self.readers.setdefault(k, {})
            rd[("dma", o.idx) if dma else eng] = o
        for k in writes:
            self.last_w[k] = o
            self.readers[k] = {}
        for p, kind in deps.values():
            if (not p.dma) and p.eng == eng and not dma:
                if kind == "raw" and eng != "pe" and p.eidx >= o.eidx - 2:
                    o.deps.append(p)
                    p.has_dep = True
            else:
                o.deps.append(p)
                p.has_dep = True
        self.ops.append(o)
        return o

    def pe(self, fn, reads=(), writes=()):
        return self.op("pe", fn, reads, writes)

    def act(self, fn, reads=(), writes=()):
        return self.op("act", fn, reads, writes)

    def dve(self, fn, reads=(), writes=()):
        return self.op("dve", fn, reads, writes)

    def pool(self, fn, reads=(), writes=()):
        return self.op("pool", fn, reads, writes)

    def dma(self, q, fn, reads=(), writes=()):
        return self.op(q, fn, reads, writes, dma=True)

    def emit(self, stack):
        nc = self.nc
        esem = {e: stack.enter_context(nc.semaphore("s_" + e)) for e in ENGS}
        dsems = [stack.enter_context(nc.semaphore("d%d" % i)) for i in range(self.N_DMA_SEMS)]
        cnt = {e: 0 for e in ENGS}
        dtot = [0] * self.N_DMA_SEMS
        NSW = 24
        nxt = {"sw": 0, "hw": 0}
        for o in self.ops:
            if o.dma:
                if o.eng == "pool":
                    d = nxt["sw"]
                    nxt["sw"] = (d + 1) % NSW
                else:
                    d = NSW + nxt["hw"]
                    nxt["hw"] = (nxt["hw"] + 1) % (self.N_DMA_SEMS - NSW)
                o.dsem = d
                dtot[d] += 16
                o.dval = dtot[d]
            elif o.has_dep:
                cnt[o.eng] += 1
                o.seq = cnt[o.eng]
        for e in ENGS:
            assert cnt[e] < 60000, (e, cnt[e])
        for d in range(self.N_DMA_SEMS):
            assert dtot[d] < 60000
        block = stack.enter_context(nc.Block())
        by_eng = {e: [o for o in self.ops if o.eng == e] for e in ENGS}

        def run(ename, eng):
            waited_e = {e: 0 for e in ENGS}
            waited_d = [0] * self.N_DMA_SEMS
            for o in by_eng[ename]:
                need_e = {}
                need_d = {}
                for p in o.deps:
                    if p.dma:
                        need_d[p.dsem] = max(need_d.get(p.dsem, 0), p.dval)
                    else:
                        need_e[p.eng] = max(need_e.get(p.eng, 0), p.seq)
                if o.dma and o.dval > 16:
                    need_d[o.dsem] = max(need_d.get(o.dsem, 0), o.dval - 16)
                for e, v in need_e.items():
                    if v > waited_e[e]:
                        eng.wait_ge(esem[e], v)
                        waited_e[e] = v
                for d, v in need_d.items():
                    if v > waited_d[d]:
                        eng.wait_ge(dsems[d], v)
                        waited_d[d] = v
                ins = o.fn(eng)
                if o.dma:
                    ins.then_inc(dsems[o.dsem], 16)
                elif o.seq is not None:
                    ins.then_inc(esem[ename], 1)
            if ename == "sp":
                for d in range(self.N_DMA_SEMS):
                    if dtot[d] > waited_d[d]:
                        eng.wait_ge(dsems[d], dtot[d])

        block.tensor(lambda eng: run("pe", eng))
        block.scalar(lambda eng: run("act", eng))
        block.vector(lambda eng: run("dve", eng))
        block.gpsimd(lambda eng: run("pool", eng))
        block.sync(lambda eng: run("sp", eng))


def sl(start, n, step=1):
    return slice(start, start + (n - 1) * step + 1, step)


def build(debug=(), limit=99):
    nc = bass.Bass("TRN2", target_bir_lowering=False)

    def din(name, shape):
        return nc.dram_tensor(name, shape, F32, kind="ExternalInput").ap()

    xo = din("xo", [T, D])
    xp = din("xp", [T, D])
    memb = din("memb", [256, D])
    w_in = din("w_in", [D, 12800])
    hg_lb = din("hg_lb", [2, 1024])
    hg_nw = din("hg_nw", [1, 128])
    w_ba = din("w_ba", [512, D])
    w_bb = din("w_bb", [1024, D])
    w_out = din("w_out", [D, D])
    wq_c = din("wq_c", [D, 512])
    wkv_c = din("wkv_c", [D, 1024])
    wo_c = din("wo_c", [512, D])
    w1 = din("w1", [D, 5632])
    w3 = din("w3", [D, 5632])
    w2 = din("w2", [5632, D])
    lnw = din("lnw", [5, D])
    pmk = din("pmk", [128, 2])
    cst = din("cst", [128, 264])
    cstb = din("cstb", [128, 1024])
    y = nc.dram_tensor("y", [T, D], F32, kind="ExternalOutput").ap()
    dbg = {}
    for name, shape in debug:
        dbg[name] = nc.dram_tensor("dbg_" + name, shape, F32, kind="ExternalOutput").ap()

    def kview(w):
        return w.rearrange("(kc p) n -> p kc n", p=128)

    w_in_v = kview(w_in)

    with ExitStack() as st:
        P = Prog(nc)

        def sb(name, shape, dt):
            return st.enter_context(nc.sbuf_tensor(name, shape, dt))

        def ps(name, shape, dt):
            return st.enter_context(nc.psum_tensor(name, shape, dt))

        hT = sb("hT", [128, 2, 16, 1024], BF16)
        R = sb("R", [128, 16384], F32)
        wbuf = [sb("wb%d" % i, [128, 4096], BF16) for i in range(NW)]
        cf = sb("cf", [128, 264], F32)
        cb = sb("cb", [128, 1024], BF16)
        pm = sb("pm", [128, 2], F32)
        lbb = sb("lbb", [128, 256], F32)
        oml = sb("oml", [128, 256], F32)
        lbt = sb("lbt", [128, 256], F32)
        kcT = sb("kcT", [128, 4, 256], BF16)
        vc = sb("vc", [128, 2, 512], BF16)
        state = sb("state", [128, 2, 128], F32)
        st_sb = [sb("st_s%d" % i, [128, 2, 128], BF16) for i in range(2)]
        ssv = sb("ssv", [128, 32], F32)
        dummy = sb("dummyk", [128, 8], F32)
        rsv = sb("rsv", [128, 32], F32)
        ecsb = [sb("ecs%d" % i, [128, 8], F32) for i in range(3)]
        tmpf = [sb("tmpf%d" % i, [128, 512], F32) for i in range(2)]
        hgnb = sb("hgnb", [128, 128], F32)

        PB = [ps("P%d" % i, [128, 512], F32) for i in range(8)]
        TBk = [PB[6][:, :].bitcast(BF16), PB[7][:, 0:256].bitcast(BF16)]
        CSB = PB[7][:, 256:512]

        Hprev = hT[:, 0].rearrange("p a b -> p (a b)")
        xs = R[:].rearrange("p (t d) -> p t d", t=8)

        def Rb(a, b):
            return R[:, a:b].bitcast(BF16)

        P.dma("sp", lambda e: e.dma_start(out=cf[:], in_=cst), writes=["cf"])
        P.dma("sp", lambda e: e.dma_start(out=pm[:], in_=pmk), writes=["pm"])
        P.dma("pool", lambda e: e.dma_start(out=cb[:], in_=cstb), writes=["cb"])
        TD1 = cf[:, 0:128]
        TD2 = cf[:, 128:256]
        SEL = cf[:, 256:260]
        ident = cb[:, 0:128]
        M01 = cb[:, 128:384]
        M2 = cb[:, 384:448]
        MBLK = cb[:, 448:576]
        ones = cb[:, 576:704]
        MP2 = cb[:, 704:832]
        MO2 = cb[:, 832:960]
        P.dma("sp", lambda e: e.dma_start(out=hgnb[:], in_=hg_nw.broadcast_to([128, 128])), writes=["hgnb"])

        if limit == 0:
            P.dma("sp", lambda e: e.dma_start(out=y[0:128, 0:264], in_=cf[:]), reads=["cf", "cb", "pm", "hgnb"], writes=["y"])
            P.emit(st)
            return nc
        ring = [0]

        def wslot():
            k = ring[0] % NW
            ring[0] += 1
            return k

        def wdma(k, dst, src):
            P.dma("pool", lambda e: e.dma_start(out=dst, in_=src), writes=[("w", k)])

        def wload(src, a, b):
            k = wslot()
            dst = wbuf[k][:, 0:a * b].rearrange("p (a b) -> p a b", a=a)
            wdma(k, dst, src)
            return dst, ("w", k)

        pre = {}

        def prefetch(name, fn):
            if name not in pre:
                pre[name] = fn()

        def take(name, fn):
            if name not in pre:
                pre[name] = fn()
            return pre.pop(name)

        def mm(out, lhsT, rhs, start, stop, reads, writes, skip=False):
            P.pe(lambda e: e.matmul(out, lhsT, rhs, start=start, stop=stop, skip_group_check=skip), reads=reads, writes=writes)

        evc = [0]

        def evac(out, in_, reads, writes, eng=None):
            if eng is None:
                eng = "act" if evc[0] % 2 == 0 else "dve"
                evc[0] += 1
            if eng == "act":
                P.act(lambda e: e.copy(out=out, in_=in_), reads=reads, writes=writes)
            elif eng == "dve":
                P.dve(lambda e: e.tensor_copy(out=out, in_=in_), reads=reads, writes=writes)
            else:
                P.pool(lambda e: e.tensor_copy(out=out, in_=in_), reads=reads, writes=writes)

        pbc = [0]

        npb = [5]

        def pbank():
            i = pbc[0] % npb[0]
            pbc[0] += 1
            if npb[0] == 7 and i >= 5:
                i += 1
            return i

        tbc = [0]

        def norm_tile(src, src_key, lnwb, hn, junk, idx, dst_fn, dst_keys, out_f32=None):
            P.act(lambda e: e.activation(out=junk, in_=src, func=AF.Square, accum_out=ssv[:, idx:idx + 1]),
                  reads=[src_key], writes=["junk", ("ss", idx)])
            P.dve(lambda e: e.tensor_scalar(out=rsv[:, idx:idx + 1], in0=ssv[:, idx:idx + 1], scalar1=1.0 / D, scalar2=EPS,
                                            op0=ALU.mult, op1=ALU.add),
                  reads=[("ss", idx)], writes=[("rs", idx)])
            P.act(lambda e: e.activation(out=rsv[:, idx:idx + 1], in_=rsv[:, idx:idx + 1], func=AF.Sqrt),
                  reads=[("rs", idx)], writes=[("rs", idx)])
            P.dve(lambda e: e.reciprocal(out=rsv[:, idx:idx + 1], in_=rsv[:, idx:idx + 1]),
                  reads=[("rs", idx)], writes=[("rs", idx)])
            if out_f32 is not None:
                out_ap, out_key = out_f32
                P.dve(lambda e: e.scalar_tensor_tensor(out=out_ap, in0=src, scalar=rsv[:, idx:idx + 1], in1=lnwb,
                                                       op0=ALU.mult, op1=ALU.mult),
                      reads=[src_key, ("rs", idx), "lnwb"], writes=[out_key])
                return
            hn_ap, hn_key = hn
            P.dve(lambda e: e.scalar_tensor_tensor(out=hn_ap, in0=src, scalar=rsv[:, idx:idx + 1], in1=lnwb,
                                                   op0=ALU.mult, op1=ALU.mult),
                  reads=[src_key, ("rs", idx), "lnwb"], writes=[hn_key])
            for r4 in range(4):
                slot = tbc[0] % 2
                tbc[0] += 1
                for j in range(4):
                    kc = r4 * 4 + j
                    o_ap = TBk[slot][:, j * 128:(j + 1) * 128]
                    i_ap = hn_ap[:, kc * 128:(kc + 1) * 128]
                    P.pe(lambda e, o_ap=o_ap, i_ap=i_ap: e.transpose(out=o_ap, in_=i_ap, identity=ident),
                         reads=[hn_key, "cb"], writes=[("ps", 6 + slot)])
                evac(dst_fn(r4 * 4), TBk[slot][:, 0:512].rearrange("p (a b) -> p a b", a=4),
                     reads=[("ps", 6 + slot)], writes=dst_keys)

        def load_lnw(i, lnwb):
            P.dma("sp", lambda e: e.dma_start(out=lnwb, in_=lnw[i:i + 1, :].broadcast_to([128, D])),
                  writes=["lnwb"])

        def hkeys(w0, w1_):
            return [("h", t) for t in range(w0 // 128, (w1_ - 1) // 128 + 1)]

        def hw(kc, w0, n, step=1):
            half = w0 // 1024
            c0 = w0 % 1024
            if step == 1:
                return hT[:, half, kc, c0:c0 + n]
            return hT[:, half, kc, sl(c0, n, step)]

        xst = [R[:, 0:2048], R[:, 2048:4096]]
        hnA = [(Rb(4096, 5120), "hn0"), (Rb(5120, 6144), "hn1")]
        junkA = Rb(6144, 7168)
        lnwbA = R[:, 8192:10240]
        mnT = Rb(10240, 12288).rearrange("p (a b) -> p a b", a=16)

        load_lnw(2, lnwbA)
        for mt in range(2):
            s = mt % 2
            P.dma("sp", lambda e, s=s, mt=mt: e.dma_start(out=xst[s], in_=memb[mt * 128:(mt + 1) * 128, :]),
                  writes=[("xst", s)])
            norm_tile(xst[s], ("xst", s), lnwbA, hnA[s], junkA, mt,
                      lambda kc0, mt=mt: mnT[:, kc0:kc0 + 4, mt * 128:(mt + 1) * 128], ["mnT"])
        wkv_v = kview(wkv_c)
        npb[0] = 7
        for hh2 in range(2):
            wk, wkk = wload(wkv_v[:, :, hh2 * 256:(hh2 + 1) * 256], 16, 256)
            for j in range(2):
                hh = hh2 * 2 + j
                b = pbank()
                for kc in range(16):
                    mm(PB[b][:, 0:256], wk[:, kc, j * 128:(j + 1) * 128], mnT[:, kc, :], kc == 0, kc == 15,
                       [wkk, "mnT"], [("ps", b)])
                evac(kcT[:, hh, :], PB[b][:, 0:256], [("ps", b)], ["kcT"])
        for vb in range(2):
            wv_, wvk = wload(wkv_v[:, :, 512 + vb * 256:512 + (vb + 1) * 256], 16, 256)
            for mt in range(2):
                b = pbank()
                for kc in range(16):
                    mm(PB[b][:, 0:256], mnT[:, kc, mt * 128:(mt + 1) * 128], wv_[:, kc, :], kc == 0, kc == 15,
                       [wvk, "mnT"], [("ps", b)])
                evac(vc[:, mt, vb * 256:(vb + 1) * 256], PB[b][:, 0:256], [("ps", b)], ["vc"])

        npb[0] = 5
        load_lnw(0, lnwbA)
        for wt in range(16):
            s = wt % 2
            src = xp if wt < 8 else xo
            r0 = (wt % 8) * 128
            P.dma("sp", lambda e, s=s, src=src, r0=r0: e.dma_start(out=xst[s], in_=src[r0:r0 + 128, :]),
                  writes=[("xst", s)])
            norm_tile(xst[s], ("xst", s), lnwbA, hnA[s], junkA, wt,
                      lambda kc0, wt=wt: hT[:, wt // 8, kc0:kc0 + 4, (wt % 8) * 128:(wt % 8 + 1) * 128],
                      [("h", wt)])

        if "hT" in dbg:
            for half in range(2):
                for kc in range(16):
                    for th in range(2):
                        P.dve(lambda e, half=half, kc=kc, th=th: e.tensor_copy(out=tmpf[0][:, :], in_=hT[:, half, kc, th * 512:(th + 1) * 512]),
                              reads=hkeys(0, 2048), writes=[("tmpf", 0)])
                        P.dma("sp", lambda e, half=half, kc=kc, th=th: e.dma_start(
                            out=dbg["hT"][(half * 16 + kc) * 2 + th], in_=tmpf[0][:, :]), reads=[("tmpf", 0)], writes=["dbg"])

        if limit == 1:
            P.emit(st)
            return nc
        qT = Rb(0, 1536).rearrange("p (g n) -> p g n", g=3)
        kT = Rb(1536, 3904)
        Vh = Rb(3904, 6272).rearrange("p (t d) -> p t d", d=128)
        acc_n = R[:, 6272:7296]
        acc_d = R[:, 7296:8320]
        o_attT = Rb(14336, 16384).rearrange("p (h n) -> p h n", h=4)
        Ebuf = [Rb(10368, 11904), Rb(11904, 13440)]
        A1KEYS = [("xst", 0), ("xst", 1), "hn0", "hn1", "junk", "lnwb", "mnT"]
        P.dve(lambda e: e.memset(dummy[:, 0:1], 0.0), writes=A1KEYS + ["Rfree"])

        KOFF = {0: 0, 1: 1152, 2: 2688}
        PG = {0: 128, 1: 512, 2: 1024}
        VOFF = {0: 0, 1: 9, 2: 21}
        NUMB = [0, 1]
        DENB = [2, 3]
        SCB = [4, 5, 6, 7]
        ec = [0]
        scc = [0]
        def att_load(hh, g):
            k = wslot()
            qk = wbuf[k][:, :].rearrange("p (kc t c) -> p kc t c", kc=16, t=2)
            qcol = g * 512 + hh * 128
            kcol = 1536 + g * 512 + hh * 128
            vcol = 3072 + g * 512 + hh * 128
            wdma(k, qk[:, :, 0, :], w_in_v[:, :, qcol:qcol + 128])
            wdma(k, qk[:, :, 1, :], w_in_v[:, :, kcol:kcol + 128])
            wv_, wvk = wload(w_in_v[:, :, vcol:vcol + 128], 16, 128)
            return qk, ("w", k), wv_, wvk

        HG_COL = {"f": 5632, "i": 6656, "q": 4608, "g": 7680}

        def hg_load(hp, nm):
            c0 = HG_COL[nm] + hp * 256
            return wload(w_in_v[:, :, c0:c0 + 256], 16, 256)

        def mg_load(cp):
            k = wslot()
            wab = wbuf[k]
            wa = wab[:, 0:1024].rearrange("p (a b) -> p a b", a=4)
            wb = wab[:, 1024:3072].rearrange("p (a b) -> p a b", a=8)
            wdma(k, wa, kview(w_ba)[:, :, cp * 256:(cp + 1) * 256])
            wdma(k, wb, kview(w_bb)[:, :, cp * 256:(cp + 1) * 256])
            ga, gak = wload(w_in_v[:, :, 8704 + cp * 256:8704 + (cp + 1) * 256], 16, 256)
            gb, gbk = wload(w_in_v[:, :, 10752 + cp * 256:10752 + (cp + 1) * 256], 16, 256)
            return wa, wb, ("w", k), ga, gak, gb, gbk

        def att_proj(hh, g):
            qk, kq, wv_, wvk = take(("att", hh, g), lambda: att_load(hh, g))
            if g < 2:
                prefetch(("att", hh, g + 1), lambda: att_load(hh, g + 1))
            elif hh < 3:
                prefetch(("att", hh + 1, 0), lambda: att_load(hh + 1, 0))
            else:
                for nm in ("f", "i", "q"):
                    prefetch(("hg", 0, nm), lambda nm=nm: hg_load(0, nm))
            for th in range(2):
                b = pbank()
                for kc in range(16):
                    mm(PB[b][:, :], qk[:, kc, 0, :], hT[:, 1, kc, th * 512:(th + 1) * 512], kc == 0, kc == 15,
                       [kq] + hkeys(1024 + th * 512, 1024 + (th + 1) * 512), [("ps", b)])
                evac(qT[:, g, th * 512:(th + 1) * 512], PB[b][:, :], [("ps", b), "Rfree"], [("qT", g)])
            segs = []
            w0 = 1024 - PG[g]
            while w0 < 2048:
                n = min(512, (1024 if w0 < 1024 else 2048) - w0)
                segs.append((w0, n))
                w0 += n
            for (w0, n) in segs:
                b = pbank()
                for kc in range(16):
                    mm(PB[b][:, 0:n], qk[:, kc, 1, :], hw(kc, w0, n), kc == 0, kc == 15,
                       [kq] + hkeys(w0, w0 + n), [("ps", b)])
                c0 = KOFF[g] + w0 - (1024 - PG[g])
                evac(kT[:, c0:c0 + n], PB[b][:, 0:n], [("ps", b), "Rfree"], [("kT", g)])
            tiles = []
            if g == 0:
                for n in range(9):
                    tiles.append((VOFF[0] + n, 896 + 128 * n, 1))
            elif g == 1:
                for r in range(4):
                    for n in range(3):
                        tiles.append((VOFF[1] + r * 3 + n, 512 + r + 512 * n, 4))
            else:
                for r in range(8):
                    for half in range(2):
                        tiles.append((VOFF[2] + r * 2 + half, half * 1024 + r, 8))
            for (ti, w0, step) in tiles:
                b = pbank()
                for kc in range(16):
                    mm(PB[b][:, 0:128], hw(kc, w0, 128, step), wv_[:, kc, :], kc == 0, kc == 15,
                       [wvk] + hkeys(w0, min(w0 + 128 * step, (w0 // 1024 + 1) * 1024)), [("ps", b)])
                evac(Vh[:, ti, :], PB[b][:, 0:128], [("ps", b), "Rfree"], [("V", g)])


        def att_unit(hh, g, mode):
            ei = (hh * 3 + g) % 2
            Ef = Ebuf[ei]
            if g < 2:
                E = Ef.rearrange("p (t c) -> p t c", c=256)
            else:
                E = Ef[:, 0:2048].rearrange("p (t c) -> p t c", c=128)
            ek = ("E", ei)

            def score(ti_e, kcols, qcols, ncol, ecol0, mask, bias):
                if mode != "score":
                    return
                sb_ = SCB[scc[0] % 4]
                scc[0] += 1
                mm(PB[sb_][:, 0:ncol], kcols, qcols, True, True, [("kT", g), ("qT", g)], [("ps", sb_)])
                e_ap = E[:, ti_e, ecol0:ecol0 + ncol]
                if bias is None:
                    P.act(lambda e: e.activation(out=e_ap, in_=PB[sb_][:, 0:ncol], func=AF.Exp, scale=SCALE),
                          reads=[("ps", sb_), "Rfree"], writes=[ek])
                else:
                    P.act(lambda e: e.activation(out=e_ap, in_=PB[sb_][:, 0:ncol], func=AF.Exp, scale=SCALE,
                                                 bias=bias),
                          reads=[("ps", sb_), "pm", "Rfree"], writes=[ek])
                P.pool(lambda e: e.tensor_tensor(out=e_ap, in0=e_ap, in1=mask, op=ALU.mult),
                       reads=[ek, "cb"], writes=[ek])

            def pv(col0, ncol, pairs):
                if mode != "pv":
                    return
                bn = NUMB[col0 // 512]
                bd = DENB[col0 // 512]
                c = col0 % 512
                for i, (vt, e_ap) in enumerate(pairs):
                    mm(PB[bn][:, c:c + ncol], Vh[:, vt, :], e_ap, i == 0, i == len(pairs) - 1,
                       [("V", g), ek], [("ps", bn)])
                for i, (vt, e_ap) in enumerate(pairs):
                    mm(PB[bd][:, c:c + ncol], ones, e_ap, i == 0, i == len(pairs) - 1,
                       [ek, "cb"], [("ps", bd)])

            if g == 0:
                for n in range(9):
                    kcols = kT[:, KOFF[0] + 128 * n:KOFF[0] + 128 * (n + 1)]
                    if n == 0:
                        score(n, kcols, qT[:, 0, 0:128], 128, 128, M01[:, 128:256], pm[:, 0:1])
                    elif n == 8:
                        score(n, kcols, qT[:, 0, 896:1024], 128, 0, M01[:, 0:128], None)
                    else:
                        score(n, kcols, qT[:, 0, 128 * (n - 1):128 * (n + 1)], 256, 0, M01, None)
                for qb in range(8):
                    pv(qb * 128, 128, [(VOFF[0] + qb, E[:, qb, 128:256]), (VOFF[0] + qb + 1, E[:, qb + 1, 0:128])])
            elif g == 1:
                for r in range(4):
                    for n in range(3):
                        kcols = kT[:, sl(KOFF[1] + r + 512 * n, 128, 4)]
                        ti_e = r * 3 + n
                        if n == 0:
                            score(ti_e, kcols, qT[:, 1, sl(r, 128, 4)], 128, 128, M01[:, 128:256], pm[:, 0:1])
                        elif n == 1:
                            score(ti_e, kcols, qT[:, 1, sl(r, 256, 4)], 256, 0, M01, None)
                        else:
                            score(ti_e, kcols, qT[:, 1, sl(r + 512, 128, 4)], 128, 0, M01[:, 0:128], None)
                for r in range(4):
                    for qb in range(2):
                        pv((r * 2 + qb) * 128, 128,
                           [(VOFF[1] + r * 3 + qb, E[:, r * 3 + qb, 128:256]),
                            (VOFF[1] + r * 3 + qb + 1, E[:, r * 3 + qb + 1, 0:128])])
            else:
                for r in range(8):
                    for half in range(2):
                        kcols = kT[:, sl(KOFF[2] + half * 1024 + r, 128, 8)]
                        score(r * 2 + half, kcols, qT[:, 2, sl(r, 128, 8)], 128, 0, MP2 if half == 0 else MO2,
                              pm[:, 0:1] if half == 0 else None)
                for r in range(8):
                    pv(r * 128, 128, [(VOFF[2] + r * 2, E[:, r * 2, 0:128]), (VOFF[2] + r * 2 + 1, E[:, r * 2 + 1, 0:128])])
            if mode == "pv":
                for (banks, acc, akey) in ((NUMB, acc_n, "acc_n"), (DENB, acc_d, "acc_d")):
                    for bi in range(2):
                        pb_ = PB[banks[bi]]
                        if g == 0:
                            dst = acc[:, bi * 512:(bi + 1) * 512]
                            P.dve(lambda e, dst=dst, pb_=pb_: e.tensor_copy(out=dst, in_=pb_[:, :]),
                                  reads=[("ps", banks[bi]), "Rfree"], writes=[akey])
                        else:
                            rr = 4 if g == 1 else 8
                            per = rr // 2
                            dst = acc.rearrange("p (m r) -> p r m", r=rr)[:, bi * per:(bi + 1) * per, :]
                            srcv = pb_[:, :].rearrange("p (r m) -> p r m", r=per)
                            P.dve(lambda e, dst=dst, srcv=srcv: e.tensor_tensor(out=dst, in0=dst, in1=srcv, op=ALU.add),
                                  reads=[("ps", banks[bi]), akey], writes=[akey])

        def att_fin(hh):
            P.dve(lambda e: e.reciprocal(out=acc_d, in_=acc_d), reads=["acc_d"], writes=["acc_d"])
            P.dve(lambda e, hh=hh: e.tensor_tensor(out=o_attT[:, hh, :], in0=acc_n, in1=acc_d, op=ALU.mult),
                  reads=["acc_n", "acc_d", "Rfree"], writes=["o_attT"])

        att_proj(0, 0)
        for u in range(12):
            hh, g = divmod(u, 3)
            att_unit(hh, g, "score")
            if u + 1 < 12:
                att_proj(*divmod(u + 1, 3))
            att_unit(hh, g, "pv")
            if g == 2:
                att_fin(hh)

        if "o_att" in dbg:
            for hh in range(4):
                for th in range(2):
                    P.dve(lambda e, hh=hh, th=th: e.tensor_copy(out=tmpf[0][:, :], in_=o_attT[:, hh, th * 512:(th + 1) * 512]),
                          reads=["o_attT"], writes=[("tmpf", 0)])
                    P.dma("sp", lambda e, hh=hh, th=th: e.dma_start(out=dbg["o_att"][hh * 2 + th], in_=tmpf[0][:, :]),
                          reads=[("tmpf", 0)], writes=["dbg"])

        if limit == 2:
            P.emit(st)
            return nc
        o_hgT = Rb(10240, 14336).rearrange("p (h n) -> p h n", h=8)
        HB = 0

        NPAR = 3
        PSZ = 3072

        def hscr(i, par, bf=False):
            if bf:
                base = HB + par * PSZ + 2048 + i * 128
                return R[:, base:base + 128].bitcast(BF16)
            base = HB + par * PSZ + i * 256
            return R[:, base:base + 256]
        ATT_KEYS = [("qT", g) for g in range(3)] + [("kT", g) for g in range(3)] + [("V", g) for g in range(3)] + \
                   ["acc_n", "acc_d", ("E", 0), ("E", 1)]
        P.dve(lambda e: e.memset(dummy[:, 1:2], 0.0), writes=ATT_KEYS + ["Rfree2"])

        for hp in range(4):
            c0 = hp * 256
            fw, fk = take(("hg", hp, "f"), lambda: hg_load(hp, "f"))
            iw, ik = take(("hg", hp, "i"), lambda: hg_load(hp, "i"))
            qw, qk_ = take(("hg", hp, "q"), lambda: hg_load(hp, "q"))
            gw, gk = take(("hg", hp, "g"), lambda: hg_load(hp, "g"))
            if hp < 3:
                for nm in ("f", "i", "q"):
                    prefetch(("hg", hp + 1, nm), lambda nm=nm: hg_load(hp + 1, nm))
            else:
                prefetch(("mg", 0), lambda: mg_load(0))
            P.dma("sp", lambda e, c0=c0: e.dma_start(out=lbb[:], in_=hg_lb[0:1, c0:c0 + 256].broadcast_to([128, 256])),
                  writes=["lbb"])
            P.dma("sp", lambda e, c0=c0: e.dma_start(out=lbt[:], in_=hg_lb[1:2, c0:c0 + 256].broadcast_to([128, 256])),
                  writes=["lbt"])
            P.dve(lambda e: e.tensor_tensor(out=lbb[:], in0=lbb[:], in1=lbt[:], op=ALU.subtract),
                  reads=["lbb", "lbt"], writes=["lbb"])
            P.act(lambda e: e.activation(out=lbb[:], in_=lbb[:], func=AF.Exp, scale=-1.0), reads=["lbb"], writes=["lbb"])
            P.dve(lambda e: e.tensor_scalar(out=lbb[:], in0=lbb[:], scalar1=1.0, scalar2=None, op0=ALU.add),
                  reads=["lbb"], writes=["lbb"])
            P.dve(lambda e: e.reciprocal(out=lbb[:], in_=lbb[:]), reads=["lbb"], writes=["lbb"])
            P.dve(lambda e: e.tensor_scalar(out=oml[:], in0=lbb[:], scalar1=-1.0, scalar2=1.0, op0=ALU.mult, op1=ALU.add),
                  reads=["lbb"], writes=["oml"])
            P.dve(lambda e: e.memset(state[:], 0.0), writes=["state"])

            def tvars(wt):
                par = wt % NPAR
                d = dict(own=wt >= 8, par=par, half=wt // 8, tc0=(wt % 8) * 128, hk=[("h", wt)],
                         K=(lambda name, par=par: (name, par)), ecs=ecsb[par])
                names = ("sig", "f_", "kk", "e2", "e1", "e1n", "sq", "sg")
                for i, n in enumerate(names):
                    d[n] = hscr(i, par)
                for i, n in enumerate(("v_b", "kl", "ke", "qe", "og")):
                    d[n] = hscr(i, par, True)
                b0 = HB + par * PSZ + 2688
                d["qkT"] = R[:, b0:b0 + 256].bitcast(BF16).rearrange("p (a b) -> p a b", a=4)
                d["aT"] = R[:, b0 + 256:b0 + 384].bitcast(BF16).rearrange("p (a b) -> p a b", a=2)
                return d

            def sigm_from_exp(buf, key):
                P.dve(lambda e: e.tensor_scalar(out=buf, in0=buf, scalar1=1.0, scalar2=None, op0=ALU.add),
                      reads=[key], writes=[key])
                P.dve(lambda e: e.reciprocal(out=buf, in_=buf), reads=[key], writes=[key])

            def pe_fill(n):
                for _ in range(n):
                    mm(PB[6][:, 256:512], ident, hT[:, 1, 0, 0:256], True, True, ["cb", ("h", 8), ("h", 9)], [("ps", 6)])

            def hg_f1(wt):
                v = tvars(wt)
                K, own, half, tc0, hk = v["K"], v["own"], v["half"], v["tc0"], v["hk"]
                sig, f_, kk, sq, sg, v_b = v["sig"], v["f_"], v["kk"], v["sq"], v["sg"], v["v_b"]
                bf_ = pbank()
                for kc in range(16):
                    mm(PB[bf_][:, 0:256], hT[:, half, kc, tc0:tc0 + 128], fw[:, kc, :], kc == 0, kc == 15, [fk] + hk, [("ps", bf_)])
                for kc in range(16):
                    mm(PB[bf_][:, 256:512], hT[:, half, kc, tc0:tc0 + 128], iw[:, kc, :], kc == 0, kc == 15, [ik] + hk, [("ps", bf_)])
                P.act(lambda e: e.activation(out=sig, in_=PB[bf_][:, 0:256], func=AF.Sigmoid),
                      reads=[("ps", bf_), "Rfree2"], writes=[K("sig")])
                P.dve(lambda e: e.tensor_copy(out=v_b, in_=PB[bf_][:, 256:512]),
                      reads=[("ps", bf_), "Rfree2"], writes=[K("v")])
                if own:
                    bq = pbank()
                    for kc in range(16):
                        mm(PB[bq][:, 0:256], hT[:, half, kc, tc0:tc0 + 128], qw[:, kc, :], kc == 0, kc == 15, [qk_] + hk, [("ps", bq)])
                    for kc in range(16):
                        mm(PB[bq][:, 256:512], hT[:, half, kc, tc0:tc0 + 128], gw[:, kc, :], kc == 0, kc == 15, [gk] + hk, [("ps", bq)])
                    P.act(lambda e: e.activation(out=sq, in_=PB[bq][:, 0:256], func=AF.Sigmoid),
                          reads=[("ps", bq), "Rfree2"], writes=[K("sq")])
                    P.act(lambda e: e.activation(out=sg, in_=PB[bq][:, 256:512], func=AF.Sigmoid),
                          reads=[("ps", bq), "Rfree2"], writes=[K("sg")])
                    P.dve(lambda e: e.tensor_tensor(out=sq, in0=PB[bq][:, 0:256], in1=sq, op=ALU.mult),
                          reads=[("ps", bq), K("sq")], writes=[K("sq")])
                    P.dve(lambda e: e.tensor_tensor(out=sg, in0=PB[bq][:, 256:512], in1=sg, op=ALU.mult),
                          reads=[("ps", bq), K("sg")], writes=[K("sg")])
                P.pool(lambda e: e.tensor_tensor(out=sig, in0=sig, in1=oml[:], op=ALU.mult),
                       reads=[K("sig"), "oml"], writes=[K("sig")])
                P.pool(lambda e: e.tensor_tensor(out=f_, in0=sig, in1=lbb[:], op=ALU.add),
                       reads=[K("sig"), "lbb", "Rfree2"], writes=[K("f")])
                P.pool(lambda e: e.tensor_tensor(out=kk, in0=oml[:], in1=sig, op=ALU.subtract),
                       reads=[K("sig"), "oml", "Rfree2"], writes=[K("kk")])
                P.act(lambda e: e.activation(out=f_, in_=f_, func=AF.Ln), reads=[K("f")], writes=[K("f")])

            def hg_f2(wt):
                v = tvars(wt)
                K, own, ecs = v["K"], v["own"], v["ecs"]
                f_, kk, e2, e1, e1n, sq = v["f_"], v["kk"], v["e2"], v["e1"], v["e1n"], v["sq"]
                kl, ke, qe, qkT, aT = v["kl"], v["ke"], v["qe"], v["qkT"], v["aT"]
                if not own:
                    pe_fill(8)
                bd_ = pbank()
                mm(PB[bd_][:, 256:512], TD2, f_, True, True, [K("f"), "cf"], [("ps", bd_)])
                if own:
                    mm(PB[bd_][:, 0:256], TD1, f_, True, True, [K("f"), "cf"], [("ps", bd_)])
                for j in range(2):
                    mm(CSB[:, j * 4:(j + 1) * 4], f_[:, j * 128:(j + 1) * 128], SEL, True, True,
                       [K("f"), "cf"], [("ps", 7)])
                P.act(lambda e: e.activation(out=ecs[:], in_=CSB[:, 0:8], func=AF.Exp), reads=[("ps", 7)], writes=[K("ecs")])
                P.act(lambda e: e.activation(out=e2, in_=PB[bd_][:, 256:512], func=AF.Exp),
                      reads=[("ps", bd_), "Rfree2"], writes=[K("e2")])
                P.pool(lambda e: e.tensor_tensor(out=kl, in0=kk, in1=e2, op=ALU.mult),
                       reads=[K("kk"), K("e2"), "Rfree2"], writes=[K("kl")])
                if own:
                    P.act(lambda e: e.activation(out=e1, in_=PB[bd_][:, 0:256], func=AF.Exp),
                          reads=[("ps", bd_), "Rfree2"], writes=[K("e1")])
                    P.act(lambda e: e.activation(out=e1n, in_=PB[bd_][:, 0:256], func=AF.Exp, scale=-1.0),
                          reads=[("ps", bd_), "Rfree2"], writes=[K("e1n")])
                    P.pool(lambda e: e.tensor_tensor(out=ke, in0=kk, in1=e1n, op=ALU.mult),
                           reads=[K("kk"), K("e1n"), "Rfree2"], writes=[K("ke")])
                    P.dve(lambda e: e.tensor_tensor(out=qe, in0=sq, in1=e1, op=ALU.mult),
                          reads=[K("sq"), K("e1"), "Rfree2"], writes=[K("qe")])
                    pe_fill(14)
                    slot = tbc[0] % 2
                    tbc[0] += 1
                    for j in range(2):
                        P.pe(lambda e, j=j: e.transpose(out=TBk[slot][:, j * 128:(j + 1) * 128],
                                                        in_=qe[:, j * 128:(j + 1) * 128], identity=ident),
                             reads=[K("qe"), "cb"], writes=[("ps", 6 + slot)])
                    for j in range(2):
                        P.pe(lambda e, j=j: e.transpose(out=TBk[slot][:, (2 + j) * 128:(3 + j) * 128],
                                                        in_=ke[:, j * 128:(j + 1) * 128], identity=ident),
                             reads=[K("ke"), "cb"], writes=[("ps", 6 + slot)])
                    evac(qkT, TBk[slot][:, 0:512].rearrange("p (a b) -> p a b", a=4),
                         [("ps", 6 + slot), "Rfree2"], [K("qkT")])
                    ba = pbank()
                    for j in range(2):
                        mm(PB[ba][:, j * 128:(j + 1) * 128], qkT[:, 2 + j, :], qkT[:, j, :], True, True, [K("qkT")], [("ps", ba)])
                    for j in range(2):
                        P.dve(lambda e, j=j: e.tensor_tensor(out=aT[:, j, :], in0=PB[ba][:, j * 128:(j + 1) * 128],
                                                             in1=MBLK, op=ALU.mult),
                              reads=[("ps", ba), "cb", "Rfree2"], writes=[K("aT")])

            def hg_b(wt):
                v = tvars(wt)
                K, own, ecs, tc0 = v["K"], v["own"], v["ecs"], v["tc0"]
                sq, sg, v_b, kl, og, qkT, aT = v["sq"], v["sg"], v["v_b"], v["kl"], v["og"], v["qkT"], v["aT"]
                on = sq
                if own:
                    for j in range(2):
                        mm(PB[5][:, j * 128:(j + 1) * 128], aT[:, j, :], v_b[:, j * 128:(j + 1) * 128], j == 0, False,
                           [K("aT"), K("v")], [("ps", 5)], skip=True)
                for ch in range(2):
                    r0 = ch * 64
                    if own:
                        for j in range(2):
                            mm(PB[5][r0:r0 + 64, j * 128:(j + 1) * 128], qkT[:, j, r0:r0 + 64], st_sb[ch][:, j, :], False,
                               (ch == 1), [K("qkT"), ("st_s", ch)], [("ps", 5)], skip=True)
                    bc_ = pbank()
                    for j in range(2):
                        mm(PB[bc_][:, j * 128:(j + 1) * 128], kl[r0:r0 + 64, j * 128:(j + 1) * 128],
                           v_b[r0:r0 + 64, j * 128:(j + 1) * 128], True, True, [K("kl"), K("v")], [("ps", bc_)])
                    for j in range(2):
                        P.dve(lambda e, j=j, bc_=bc_: e.scalar_tensor_tensor(
                            out=state[:, j, :], in0=state[:, j, :], scalar=ecs[:, j * 4 + 2 + ch:j * 4 + 3 + ch],
                            in1=PB[bc_][:, j * 128:(j + 1) * 128], op0=ALU.mult, op1=ALU.add),
                              reads=["state", K("ecs"), ("ps", bc_)], writes=["state"])
                    if ch == 0:
                        nxt_ok, ecs_n, key_n, col = own, ecs, K("ecs"), 1
                    else:
                        nxt_ok = (wt + 1 < 16) and (wt + 1 >= 8)
                        ecs_n, key_n, col = ecsb[(wt + 1) % NPAR], ("ecs", (wt + 1) % NPAR), 0
                    if nxt_ok:
                        for j in range(2):
                            P.dve(lambda e, j=j, ecs_n=ecs_n, col=col, ch=ch: e.tensor_scalar(
                                out=st_sb[1 - ch][:, j, :], in0=state[:, j, :], scalar1=ecs_n[:, j * 4 + col:j * 4 + col + 1],
                                scalar2=None, op0=ALU.mult),
                                  reads=["state", key_n], writes=[("st_s", 1 - ch)])
                if own:
                    for j in range(2):
                        P.act(lambda e, j=j: e.activation(out=on[:, j * 128:(j + 1) * 128], in_=PB[5][:, j * 128:(j + 1) * 128],
                                                          func=AF.Square, accum_out=ssv[:, 16 + j:17 + j]),
                              reads=[("ps", 5), K("qe")], writes=[K("sq"), ("hss", j)])
                    P.dve(lambda e: e.tensor_scalar(out=rsv[:, 16:18], in0=ssv[:, 16:18], scalar1=1.0 / 128, scalar2=EPS,
                                                    op0=ALU.mult, op1=ALU.add),
                          reads=[("hss", 0), ("hss", 1)], writes=["hrs"])
                    P.act(lambda e: e.activation(out=rsv[:, 16:18], in_=rsv[:, 16:18], func=AF.Ln),
                          reads=["hrs"], writes=["hrs"])
                    P.act(lambda e: e.activation(out=rsv[:, 16:18], in_=rsv[:, 16:18], func=AF.Exp, scale=-0.5),
                          reads=["hrs"], writes=["hrs"])
                    for j in range(2):
                        P.dve(lambda e, j=j: e.scalar_tensor_tensor(
                            out=on[:, j * 128:(j + 1) * 128], in0=PB[5][:, j * 128:(j + 1) * 128], scalar=rsv[:, 16 + j:17 + j],
                            in1=hgnb[:], op0=ALU.mult, op1=ALU.mult),
                              reads=[("ps", 5), "hrs", "hgnb"], writes=[K("sq")])
                    P.pool(lambda e: e.tensor_tensor(out=og, in0=on, in1=sg, op=ALU.mult),
                           reads=[K("sq"), K("sg"), "Rfree2"], writes=[K("og")])
                    pe_fill(14)
                    slot = tbc[0] % 2
                    tbc[0] += 1
                    for j in range(2):
                        P.pe(lambda e, j=j: e.transpose(out=TBk[slot][:, j * 128:(j + 1) * 128],
                                                        in_=og[:, j * 128:(j + 1) * 128], identity=ident),
                             reads=[K("og"), "cb"], writes=[("ps", 6 + slot)])
                    evac(o_hgT[:, hp * 2:hp * 2 + 2, tc0:tc0 + 128],
                         TBk[slot][:, 0:256].rearrange("p (a b) -> p a b", a=2),
                         [("ps", 6 + slot), "Rfree2"], ["o_hgT"])

            for it in range(-2, 16):
                if 0 <= it + 2 < 16:
                    hg_f1(it + 2)
                if 0 <= it + 1 < 16:
                    hg_f2(it + 1)
                if it >= 0:
                    hg_b(it)

        if "o_hg" in dbg:
            for hh in range(8):
                for th in range(2):
                    P.dve(lambda e, hh=hh, th=th: e.tensor_copy(out=tmpf[0][:, :], in_=o_hgT[:, hh, th * 512:(th + 1) * 512]),
                          reads=["o_hgT"], writes=[("tmpf", 0)])
                    P.dma("sp", lambda e, hh=hh, th=th: e.dma_start(out=dbg["o_hg"][hh * 2 + th], in_=tmpf[0][:, :]),
                          reads=[("tmpf", 0)], writes=["dbg"])

        if limit == 3:
            P.emit(st)
            return nc
        HG_KEYS = [(n, p) for n in ("sig", "f", "kk", "e2", "e1", "e1n", "sq", "sg", "v", "kl", "ke", "qe",
                                    "og", "qkT", "aT") for p in range(3)]
        P.dve(lambda e: e.memset(dummy[:, 2:3], 0.0), writes=HG_KEYS + hkeys(0, 1024) + ["Rfree3"])
        npb[0] = 7
        mg = [R[:, i * 512:(i + 1) * 512] for i in range(8)]
        w_ba_v = kview(w_ba)
        w_bb_v = kview(w_bb)
        mi = [0]
        for cp in range(8):
            wa, wb, kab, ga, gak, gb, gbk = take(("mg", cp), lambda: mg_load(cp))
            if cp < 7:
                prefetch(("mg", cp + 1), lambda: mg_load(cp + 1))
            for cc in range(2):
                c = cp * 2 + cc
                for th in range(2):
                    tsl = slice(th * 512, (th + 1) * 512)
                    hk = hkeys(1024 + th * 512, 1024 + (th + 1) * 512)
                    b_a, b_b, b_ga, b_gb = pbank(), pbank(), pbank(), pbank()
                    for kc in range(4):
                        mm(PB[b_a][:, :], wa[:, kc, cc * 128:(cc + 1) * 128], o_attT[:, kc, tsl], kc == 0, kc == 3,
                           [kab, "o_attT"], [("ps", b_a)])
                    for kc in range(8):
                        mm(PB[b_b][:, :], wb[:, kc, cc * 128:(cc + 1) * 128], o_hgT[:, kc, tsl], kc == 0, kc == 7,
                           [kab, "o_hgT"], [("ps", b_b)])
                    for kc in range(16):
                        mm(PB[b_ga][:, :], ga[:, kc, cc * 128:(cc + 1) * 128], hT[:, 1, kc, tsl], kc == 0, kc == 15,
                           [gak] + hk, [("ps", b_ga)])
                    for kc in range(16):
                        mm(PB[b_gb][:, :], gb[:, kc, cc * 128:(cc + 1) * 128], hT[:, 1, kc, tsl], kc == 0, kc == 15,
                           [gbk] + hk, [("ps", b_gb)])
                    par = mi[0] % 2
                    mi[0] += 1
                    sa, sbb, m1, m2 = (mg[par * 4 + i] for i in range(4))
                    MK = lambda n: ("mg", n, par)
                    P.act(lambda e, sa=sa, b_ga=b_ga: e.activation(out=sa, in_=PB[b_ga][:, :], func=AF.Sigmoid),
                          reads=[("ps", b_ga), "Rfree3"], writes=[MK(0)])
                    P.act(lambda e, sbb=sbb, b_gb=b_gb: e.activation(out=sbb, in_=PB[b_gb][:, :], func=AF.Sigmoid),
                          reads=[("ps", b_gb), "Rfree3"], writes=[MK(1)])
                    P.dve(lambda e, sa=sa, m1=m1, b_a=b_a: e.tensor_tensor(out=m1, in0=PB[b_a][:, :], in1=sa, op=ALU.mult),
                          reads=[("ps", b_a), MK(0), "Rfree3"], writes=[MK(2)])
                    P.dve(lambda e, sbb=sbb, m2=m2, b_b=b_b: e.tensor_tensor(out=m2, in0=PB[b_b][:, :], in1=sbb, op=ALU.mult),
                          reads=[("ps", b_b), MK(1), "Rfree3"], writes=[MK(3)])
                    P.pool(lambda e, m1=m1, m2=m2, c=c, tsl=tsl: e.tensor_tensor(out=hT[:, 0, c, tsl], in0=m1, in1=m2, op=ALU.add),
                           reads=[MK(2), MK(3), "Rfree3"], writes=[("mT", c)])

        MG_KEYS = [("mg", n, p) for n in range(4) for p in range(2)]
        P.dve(lambda e: e.memset(dummy[:, 3:4], 0.0), writes=MG_KEYS + ["o_hgT", "o_attT", "Rfree4"])
        for t in range(8):
            P.dma("sp", lambda e, t=t: e.dma_start(out=xs[:, t, :], in_=xo[t * 128:(t + 1) * 128, :]),
                  reads=["Rfree4"], writes=[("x", t)])
        w_out_v = kview(w_out)
        MT_ALL = [("mT", c) for c in range(16)]
        for nb in range(8):
            wo_, wok = wload(w_out_v[:, :, nb * 256:(nb + 1) * 256], 16, 256)
            for t in range(8):
                b = pbank()
                for kc in range(16):
                    mm(PB[b][:, 0:256], hT[:, 0, kc, t * 128:(t + 1) * 128], wo_[:, kc, :], kc == 0, kc == 15,
                       [wok] + MT_ALL, [("ps", b)])
                P.dve(lambda e, b=b, t=t, nb=nb: e.tensor_tensor(out=xs[:, t, nb * 256:(nb + 1) * 256],
                                                              in0=xs[:, t, nb * 256:(nb + 1) * 256], in1=PB[b][:, 0:256], op=ALU.add),
                      reads=[("ps", b), ("x", t)], writes=[("x", t)])

        def dump_x(name):
            if name in dbg:
                for t in range(8):
                    P.dma("sp", lambda e, t=t: e.dma_start(out=dbg[name][t * 128:(t + 1) * 128, :], in_=xs[:, t, :]),
                          reads=[("x", t)], writes=["dbg"])
        dump_x("x1")

        if limit == 4:
            P.emit(st)
            return nc
        hnB2 = [(Hprev[:, 0:2048], "hnB"), (Hprev[:, 2048:4096], "hnB1")]
        hnB = hnB2[0]
        junkB = Hprev[:, 2048:4096]
        junkBC = Hprev[:, 14336:16384]
        lnwbB = Hprev[:, 4096:8192].bitcast(F32)
        bufX = Hprev[:, 8192:12288]
        bufY = Hprev[:, 12288:16384]
        P.dve(lambda e: e.memset(dummy[:, 4:5], 0.0), writes=MT_ALL + ["hnB", "hnB1", "junk", "lnwb", "Hfree"])

        def norm_phase(li):
            P.dma("sp", lambda e: e.dma_start(out=lnwbB, in_=lnw[li:li + 1, :].broadcast_to([128, D])),
                  reads=["Hfree"], writes=["lnwb"])
            for t in range(8):
                norm_tile(xs[:, t, :], ("x", t), lnwbB, hnB2[t % 2], junkBC, t,
                          lambda kc0, t=t: hT[:, 1, kc0:kc0 + 4, t * 128:(t + 1) * 128], [("h", 8 + t)])

        npb[0] = 5
        norm_phase(1)
        npb[0] = 7
        qcT = bufX.rearrange("p (h n) -> p h n", h=4)
        ocT = bufY.rearrange("p (h n) -> p h n", h=4)
        wq_v = kview(wq_c)
        for hh2 in range(2):
            wq_, wqk = wload(wq_v[:, :, hh2 * 256:(hh2 + 1) * 256], 16, 256)
            for j in range(2):
                hh = hh2 * 2 + j
                for th in range(2):
                    b = pbank()
                    for kc in range(16):
                        mm(PB[b][:, :], wq_[:, kc, j * 128:(j + 1) * 128], hT[:, 1, kc, th * 512:(th + 1) * 512],
                           kc == 0, kc == 15, [wqk] + hkeys(1024 + th * 512, 1536 + th * 512), [("ps", b)])
                    evac(qcT[:, hh, th * 512:(th + 1) * 512], PB[b][:, :], [("ps", b), "Hfree"], ["qcT"])
        Ecr = [tmpf[0][:, :].bitcast(BF16).rearrange("p (m n) -> p m n", m=2),
               tmpf[1][:, :].bitcast(BF16).rearrange("p (m n) -> p m n", m=2)]
        rdn = [R_ for R_ in (sb("rdn0", [128, 512], F32), sb("rdn1", [128, 512], F32))]
        ci = [0]
        for hh in range(4):
            for th in range(2):
                par = ci[0] % 2
                ci[0] += 1
                for mt in range(2):
                    b = pbank()
                    mm(PB[b][:, :], kcT[:, hh, mt * 128:(mt + 1) * 128], qcT[:, hh, th * 512:(th + 1) * 512], True, True,
                       ["kcT", "qcT"], [("ps", b)])
                    P.act(lambda e, b=b, mt=mt, par=par: e.activation(out=Ecr[par][:, mt, :], in_=PB[b][:, :], func=AF.Exp,
                                                                      scale=SCALE),
                          reads=[("ps", b)], writes=[("tmpf", par)])
                bn, bd = pbank(), pbank()
                for mt in range(2):
                    mm(PB[bn][:, :], vc[:, mt, hh * 128:(hh + 1) * 128], Ecr[par][:, mt, :], mt == 0, mt == 1,
                       ["vc", ("tmpf", par)], [("ps", bn)])
                for mt in range(2):
                    mm(PB[bd][:, :], ones, Ecr[par][:, mt, :], mt == 0, mt == 1, ["cb", ("tmpf", par)], [("ps", bd)])
                P.dve(lambda e, bd=bd, par=par: e.reciprocal(out=rdn[par][:, :], in_=PB[bd][:, :]),
                      reads=[("ps", bd)], writes=[("rdn", par)])
                P.dve(lambda e, bn=bn, par=par, hh=hh, th=th: e.tensor_tensor(out=ocT[:, hh, th * 512:(th + 1) * 512],
                                                                           in0=PB[bn][:, :], in1=rdn[par][:, :], op=ALU.mult),
                      reads=[("ps", bn), ("rdn", par), "Hfree"], writes=["ocT"])
        wo_v = kview(wo_c)
        for nb in range(2):
            wo2, wo2k = wload(wo_v[:, :, nb * 1024:(nb + 1) * 1024], 4, 1024)
            for t in range(8):
                for n2 in range(2):
                    b = pbank()
                    for kc in range(4):
                        mm(PB[b][:, :], ocT[:, kc, t * 128:(t + 1) * 128], wo2[:, kc, n2 * 512:(n2 + 1) * 512], kc == 0, kc == 3,
                           [wo2k, "ocT"], [("ps", b)])
                    c0 = nb * 1024 + n2 * 512
                    P.dve(lambda e, b=b, t=t, c0=c0: e.tensor_tensor(out=xs[:, t, c0:c0 + 512], in0=xs[:, t, c0:c0 + 512],
                                                                  in1=PB[b][:, :], op=ALU.add),
                          reads=[("ps", b), ("x", t)], writes=[("x", t)])
        dump_x("x2")

        if limit == 5:
            P.emit(st)
            return nc
        P.dve(lambda e: e.memset(dummy[:, 5:6], 0.0), writes=["qcT", "ocT", "hnB", "hnB1", "junk", "lnwb", "Hfree"])
        npb[0] = 5
        norm_phase(3)
        npb[0] = 7
        uT = [bufX.rearrange("p (j n) -> p j n", j=2), bufY.rearrange("p (j n) -> p j n", j=2)]
        uT = [bufX[:, 0:2048].rearrange("p (j n) -> p j n", j=2), bufX[:, 2048:4096].rearrange("p (j n) -> p j n", j=2),
              bufY[:, 0:2048].rearrange("p (j n) -> p j n", j=2)]
        w1_v, w3_v, w2_v = kview(w1), kview(w3), kview(w2)
        NFB = 22
        HOWN = hkeys(1024, 2048)
        sil = [tmpf[0], tmpf[1]]
        sic = [0]

        def ffn_up(fb):
            w1b, w1k = wload(w1_v[:, :, fb * 256:(fb + 1) * 256], 16, 256)
            w3b, w3k = wload(w3_v[:, :, fb * 256:(fb + 1) * 256], 16, 256)
            u = uT[fb % 3]
            for j in range(2):
                for th in range(2):
                    tsl = slice(th * 512, (th + 1) * 512)
                    b1, b3 = pbank(), pbank()
                    for kc in range(16):
                        mm(PB[b1][:, :], w1b[:, kc, j * 128:(j + 1) * 128], hT[:, 1, kc, tsl], kc == 0, kc == 15,
                           [w1k] + HOWN, [("ps", b1)])
                    for kc in range(16):
                        mm(PB[b3][:, :], w3b[:, kc, j * 128:(j + 1) * 128], hT[:, 1, kc, tsl], kc == 0, kc == 15,
                           [w3k] + HOWN, [("ps", b3)])
                    par = sic[0] % 2
                    sic[0] += 1
                    P.act(lambda e, b1=b1, par=par: e.activation(out=sil[par][:, :], in_=PB[b1][:, :], func=AF.Silu),
                          reads=[("ps", b1)], writes=[("tmpf", par)])
                    P.dve(lambda e, b3=b3, par=par, u=u, j=j, tsl=tsl: e.tensor_tensor(out=u[:, j, tsl], in0=PB[b3][:, :],
                                                                                    in1=sil[par][:, :], op=ALU.mult),
                          reads=[("ps", b3), ("tmpf", par), "Hfree"], writes=[("uT", fb % 3)])

        def ffn_down(fb):
            w2b, w2k = wload(w2_v[:, fb * 2:(fb + 1) * 2, :], 2, 2048)
            u = uT[fb % 3]
            for t in range(8):
                for nb in range(4):
                    b = pbank()
                    for j in range(2):
                        mm(PB[b][:, :], u[:, j, t * 128:(t + 1) * 128], w2b[:, j, nb * 512:(nb + 1) * 512], j == 0, j == 1,
                           [w2k, ("uT", fb % 3)], [("ps", b)])
                    P.dve(lambda e, b=b, t=t, nb=nb: e.tensor_tensor(out=xs[:, t, nb * 512:(nb + 1) * 512],
                                                                  in0=xs[:, t, nb * 512:(nb + 1) * 512], in1=PB[b][:, :], op=ALU.add),
                          reads=[("ps", b), ("x", t)], writes=[("x", t)])

        ffn_up(0)
        for fb in range(NFB):
            if fb + 1 < NFB:
                ffn_up(fb + 1)
            ffn_down(fb)
        dump_x("x3")

        if limit == 6:
            P.emit(st)
            return nc
        P.dve(lambda e: e.memset(dummy[:, 6:7], 0.0), writes=[("uT", i) for i in range(3)] + ["hnB", "hnB1", "junk", "lnwb", "Hfree"])
        P.dma("sp", lambda e: e.dma_start(out=lnwbB, in_=lnw[4:5, :].broadcast_to([128, D])), reads=["Hfree"], writes=["lnwb"])
        yst = [bufX.bitcast(F32), bufY.bitcast(F32)]
        for t in range(8):
            par = t % 2
            norm_tile(xs[:, t, :], ("x", t), lnwbB, None, junkB, t, None, None, out_f32=(yst[par], ("yst", par)))
            P.dma("sp", lambda e, t=t, par=par: e.dma_start(out=y[t * 128:(t + 1) * 128, :], in_=yst[par]),
                  reads=[("yst", par)], writes=["y"])

        P.emit(st)
    return nc


def _const_tables():
    c = np.zeros((128, 264), np.float32)
    s = np.arange(128)[:, None]
    cc = np.arange(128)[None, :]
    same = (s // 64) == (cc // 64)
    pos_s = s % 64
    tri = same & (s <= cc)
    triref = same & (pos_s <= 32)
    c[:, 0:128] = tri.astype(np.float32) - triref.astype(np.float32)
    c[:, 128:256] = (same & (s > cc)).astype(np.float32)
    sv = np.arange(128)
    c[:, 256] = ((sv < 64) & (sv % 64 <= 32))
    c[:, 257] = ((sv >= 64) & (sv % 64 <= 32))
    c[:, 258] = (sv < 64)
    c[:, 259] = (sv >= 64)
    b = np.zeros((128, 1024), np.float32)
    b[:, 0:128] = np.eye(128)
    k = np.arange(128)[:, None]
    q = np.arange(128)[None, :]
    b[:, 128:256] = (k <= q)
    b[:, 256:384] = (k >= q)
    q64 = np.arange(64)[None, :]
    b[:, 384:448] = (k <= 64 + q64)
    b[:, 448:576] = (same & (cc >= s))
    b[:, 576:704] = 1.0
    b[:, 704:832] = ((k % 2) == (q % 2))
    b[:, 832:960] = (((k % 2) == (q % 2)) & (k <= q))
    return c, b


_NC_CACHE = {}


def _prep_inputs(inputs):
    x = np.asarray(inputs["x"], np.float32)
    mem = np.asarray(inputs["mem"], np.float32)
    lnw = np.stack([np.asarray(inputs["ln_mix_w"], np.float32)[0], np.asarray(inputs["ln_cross_w"], np.float32)[0],
                    np.asarray(inputs["ln_mem_w"], np.float32)[0], np.asarray(inputs["ln_ffn_w"], np.float32)[0],
                    np.asarray(inputs["ln_final_w"], np.float32)], axis=0)
    cst, cstb = _const_tables()
    shared = {
        "w_in": np.ascontiguousarray(inputs["w_in"][0], np.float32),
        "hg_lb": np.ascontiguousarray(inputs["hg_lower_bounds"], np.float32),
        "hg_nw": np.ascontiguousarray(inputs["hg_norm_w"], np.float32).reshape(1, 128),
        "w_ba": np.ascontiguousarray(inputs["w_branch_a"][0], np.float32),
        "w_bb": np.ascontiguousarray(inputs["w_branch_b"][0], np.float32),
        "w_out": np.ascontiguousarray(inputs["w_out"][0], np.float32),
        "wq_c": np.ascontiguousarray(inputs["wq_cross"][0], np.float32),
        "wkv_c": np.ascontiguousarray(inputs["wkv_cross"][0], np.float32),
        "wo_c": np.ascontiguousarray(inputs["wo_cross"][0], np.float32),
        "w1": np.ascontiguousarray(inputs["w1"][0], np.float32),
        "w3": np.ascontiguousarray(inputs["w3"][0], np.float32),
        "w2": np.ascontiguousarray(inputs["w2"][0], np.float32),
        "lnw": np.ascontiguousarray(lnw),
        "cst": cst,
        "cstb": cstb,
    }
    in_maps = []
    for c in range(8):
        b, h = c // 2, c % 2
        m = dict(shared)
        m["xo"] = np.ascontiguousarray(x[b, h * T:(h + 1) * T])
        pmk = np.zeros((128, 2), np.float32)
        if h == 1:
            m["xp"] = np.ascontiguousarray(x[b, 0:T])
        else:
            m["xp"] = np.zeros((T, D), np.float32)
            pmk[:, 0] = -30000.0
            pmk[0:64, 1] = -30000.0
        m["pmk"] = pmk
        m["memb"] = np.ascontiguousarray(mem[b])
        in_maps.append(m)
    return in_maps


def kernel(**inputs):
    if "nc" not in _NC_CACHE:
        _NC_CACHE["nc"] = build()
    nc = _NC_CACHE["nc"]
    in_maps = _prep_inputs(inputs)
    res = run_bass_kernel_spmd(nc, in_maps, core_ids=list(range(8)))
    out = np.zeros((4, 2048, D), np.float32)
    for c in range(8):
        b, h = c // 2, c % 2
        out[b, h * T:(h + 1) * T] = res.results[c]["y"]
    return out
```

```python
import numpy as np
from contextlib import ExitStack
import concourse.bass as bass
import concourse.mybir as mybir
from concourse.bass_utils import run_bass_kernel_spmd

F32 = mybir.dt.float32
BF16 = mybir.dt.bfloat16
AF = mybir.ActivationFunctionType
ALU = mybir.AluOpType

ENGS = ("pe", "act", "dve", "pool", "sp")
D = 2048
T = 1024
EPS = 1e-6
NW = 7
SCALE = 128.0 ** -0.5


class Op:
    __slots__ = ("eng", "fn", "dma", "idx", "eidx", "deps", "seq", "dsem", "dval", "has_dep")

    def __init__(self, eng, fn, dma):
        self.eng = eng
        self.fn = fn
        self.dma = dma
        self.deps = []
        self.seq = None
        self.dsem = None
        self.dval = None
        self.has_dep = False


class Prog:
    N_DMA_SEMS = 40

    def __init__(self, nc):
        self.nc = nc
        self.ops = []
        self.last_w = {}
        self.readers = {}
        self.ecount = {e: 0 for e in ENGS}

    def op(self, eng, fn, reads=(), writes=(), dma=False):
        xk = [k for k in reads if isinstance(k, tuple) and k[0] in ("ps", "tb")]
        if xk:
            reads = [k for k in reads if k not in xk]
            writes = list(writes) + [k for k in xk if k not in writes]
        o = Op(eng, fn, dma)
        o.idx = len(self.ops)
        o.eidx = self.ecount[eng]
        self.ecount[eng] += 1
        deps = {}
        for k in reads:
            w = self.last_w.get(k)
            if w is not None:
                deps[w.idx] = (w, "raw")
        for k in writes:
            w = self.last_w.get(k)
            if w is not None and w.idx not in deps:
                deps[w.idx] = (w, "waw")
            rd = self.readers.get(k)
            if rd:
                for r in rd.values():
                    if r.idx not in deps and r is not o:
                        deps[r.idx] = (r, "war")
        for k in reads:
            rd = self.readers.setdefault(k, {})
            rd[("dma", o.idx) if dma else eng] = o
        for k in writes:
            self.last_w[k] = o
            self.readers[k] = {}
        for p, kind in deps.values():
            if (not p.dma) and p.eng == eng and not dma:
                if kind == "raw" and eng != "pe" and p.eidx >= o.eidx - 2:
                    o.deps.append(p)
                    p.has_dep = True
            else:
                o.deps.append(p)
                p.has_dep = True
        self.ops.append(o)
        return o

    def pe(self, fn, reads=(), writes=()):
        return self.op("pe", fn, reads, writes)

    def act(self, fn, reads=(), writes=()):
        return self.op("act", fn, reads, writes)

    def dve(self, fn, reads=(), writes=()):
        return self.op("dve", fn, reads, writes)

    def pool(self, fn, reads=(), writes=()):
        return self.op("pool", fn, reads, writes)

    def dma(self, q, fn, reads=(), writes=()):
        return self.op(q, fn, reads, writes, dma=True)

    def emit(self, stack):
        nc = self.nc
        esem = {e: stack.enter_context(nc.semaphore("s_" + e)) for e in ENGS}
        dsems = [stack.enter_context(nc.semaphore("d%d" % i)) for i in range(self.N_DMA_SEMS)]
        cnt = {e: 0 for e in ENGS}
        dtot = [0] * self.N_DMA_SEMS
        NSW = 24
        nxt = {"sw": 0, "hw": 0}
        for o in self.ops:
            if o.dma:
                if o.eng == "pool":
                    d = nxt["sw"]
                    nxt["sw"] = (d + 1) % NSW
                else:
                    d = NSW + nxt["hw"]
                    nxt["hw"] = (nxt["hw"] + 1) % (self.N_DMA_SEMS - NSW)
                o.dsem = d
                dtot[d] += 16
                o.dval = dtot[d]
            elif o.has_dep:
                cnt[o.eng] += 1
                o.seq = cnt[o.eng]
        for e in ENGS:
            assert cnt[e] < 60000, (e, cnt[e])
        for d in range(self.N_DMA_SEMS):
            assert dtot[d] < 60000
        block = stack.enter_context(nc.Block())
        by_eng = {e: [o for o in self.ops if o.eng == e] for e in ENGS}

        def run(ename, eng):
            waited_e = {e: 0 for e in ENGS}
            waited_d = [0] * self.N_DMA_SEMS
            for o in by_eng[ename]:
                need_e = {}
                need_d = {}
                for p in o.deps:
                    if p.dma:
                        need_d[p.dsem] = max(need_d.get(p.dsem, 0), p.dval)
                    else:
                        need_e[p.eng] = max(need_e.get(p.eng, 0), p.seq)
                if o.dma and o.dval > 16:
                    need_d[o.dsem] = max(need_d.get(o.dsem, 0), o.dval - 16)
                for e, v in need_e.items():
                    if v > waited_e[e]:
                        eng.wait_ge(esem[e], v)
                        waited_e[e] = v
                for d, v in need_d.items():
                    if v > waited_d[d]:
                        eng.wait_ge(dsems[d], v)
                        waited_d[d] = v
                ins = o.fn(eng)
                if o.dma:
                    ins.then_inc(dsems[o.dsem], 16)
                elif o.seq is not None:
                    ins.then_inc(esem[ename], 1)
            if ename == "sp":
                for d in range(self.N_DMA_SEMS):
                    if dtot[d] > waited_d[d]:
                        eng.wait_ge(dsems[d], dtot[d])

        block.tensor(lambda eng: run("pe", eng))
        block.scalar(lambda eng: run("act", eng))
        block.vector(lambda eng: run("dve", eng))
        block.gpsimd(lambda eng: run("pool", eng))
        block.sync(lambda eng: run("sp", eng))


def sl(start, n, step=1):
    return slice(start, start + (n - 1) * step + 1, step)


def build(debug=(), limit=99):
    nc = bass.Bass("TRN2", target_bir_lowering=False)

    def din(name, shape):
        return nc.dram_tensor(name, shape, F32, kind="ExternalInput").ap()

    xo = din("xo", [T, D])
    xp = din("xp", [T, D])
    memb = din("memb", [256, D])
    w_in = din("w_in", [D, 12800])
    hg_lb = din("hg_lb", [2, 1024])
    hg_nw = din("hg_nw", [1, 128])
    w_ba = din("w_ba", [512, D])
    w_bb = din("w_bb", [1024, D])
    w_out = din("w_out", [D, D])
    wq_c = din("wq_c", [D, 512])
    wkv_c = din("wkv_c", [D, 1024])
    wo_c = din("wo_c", [512, D])
    w1 = din("w1", [D, 5632])
    w3 = din("w3", [D, 5632])
    w2 = din("w2", [5632, D])
    lnw = din("lnw", [5, D])
    pmk = din("pmk", [128, 2])
    cst = din("cst", [128, 264])
    cstb = din("cstb", [128, 1024])
    y = nc.dram_tensor("y", [T, D], F32, kind="ExternalOutput").ap()
    dbg = {}
    for name, shape in debug:
        dbg[name] = nc.dram_tensor("dbg_" + name, shape, F32, kind="ExternalOutput").ap()

    def kview(w):
        return w.rearrange("(kc p) n -> p kc n", p=128)

    w_in_v = kview(w_in)

    with ExitStack() as st:
        P = Prog(nc)

        def sb(name, shape, dt):
            return st.enter_context(nc.sbuf_tensor(name, shape, dt))

        def ps(name, shape, dt):
            return st.enter_context(nc.psum_tensor(name, shape, dt))

        hT = sb("hT", [128, 2, 16, 1024], BF16)
        R = sb("R", [128, 16384], F32)
        wbuf = [sb("wb%d" % i, [128, 4096], BF16) for i in range(NW)]
        cf = sb("cf", [128, 264], F32)
        cb = sb("cb", [128, 1024], BF16)
        pm = sb("pm", [128, 2], F32)
        lbb = sb("lbb", [128, 256], F32)
        oml = sb("oml", [128, 256], F32)
        lbt = sb("lbt", [128, 256], F32)
        kcT = sb("kcT", [128, 4, 256], BF16)
        vc = sb("vc", [128, 2, 512], BF16)
        state = sb("state", [128, 2, 128], F32)
        st_sb = [sb("st_s%d" % i, [128, 2, 128], BF16) for i in range(2)]
        ssv = sb("ssv", [128, 32], F32)
        dummy = sb("dummyk", [128, 8], F32)
        rsv = sb("rsv", [128, 32], F32)
        ecsb = [sb("ecs%d" % i, [128, 8], F32) for i in range(3)]
        tmpf = [sb("tmpf%d" % i, [128, 512], F32) for i in range(2)]
        hgnb = sb("hgnb", [128, 128], F32)

        PB = [ps("P%d" % i, [128, 512], F32) for i in range(8)]
        TBk = [PB[6][:, :].bitcast(BF16), PB[7][:, 0:256].bitcast(BF16)]
        CSB = PB[7][:, 256:512]

        Hprev = hT[:, 0].rearrange("p a b -> p (a b)")
        xs = R[:].rearrange("p (t d) -> p t d", t=8)

        def Rb(a, b):
            return R[:, a:b].bitcast(BF16)

        P.dma("sp", lambda e: e.dma_start(out=cf[:], in_=cst), writes=["cf"])
        P.dma("sp", lambda e: e.dma_start(out=pm[:], in_=pmk), writes=["pm"])
        P.dma("pool", lambda e: e.dma_start(out=cb[:], in_=cstb), writes=["cb"])
        TD1 = cf[:, 0:128]
        TD2 = cf[:, 128:256]
        SEL = cf[:, 256:260]
        ident = cb[:, 0:128]
        M01 = cb[:, 128:384]
        M2 = cb[:, 384:448]
        MBLK = cb[:, 448:576]
        ones = cb[:, 576:704]
        MP2 = cb[:, 704:832]
        MO2 = cb[:, 832:960]
        P.dma("sp", lambda e: e.dma_start(out=hgnb[:], in_=hg_nw.broadcast_to([128, 128])), writes=["hgnb"])

        if limit == 0:
            P.dma("sp", lambda e: e.dma_start(out=y[0:128, 0:264], in_=cf[:]), reads=["cf", "cb", "pm", "hgnb"], writes=["y"])
            P.emit(st)
            return nc
        ring = [0]

        def wslot():
            k = ring[0] % NW
            ring[0] += 1
            return k

        def wdma(k, dst, src):
            P.dma("pool", lambda e: e.dma_start(out=dst, in_=src), writes=[("w", k)])

        def wload(src, a, b):
            k = wslot()
            dst = wbuf[k][:, 0:a * b].rearrange("p (a b) -> p a b", a=a)
            wdma(k, dst, src)
            return dst, ("w", k)

        pre = {}

        def prefetch(name, fn):
            if name not in pre:
                pre[name] = fn()

        def take(name, fn):
            if name not in pre:
                pre[name] = fn()
            return pre.pop(name)

        def mm(out, lhsT, rhs, start, stop, reads, writes, skip=False):
            P.pe(lambda e: e.matmul(out, lhsT, rhs, start=start, stop=stop, skip_group_check=skip), reads=reads, writes=writes)

        evc = [0]

        def evac(out, in_, reads, writes, eng=None):
            if eng is None:
                eng = "act" if evc[0] % 2 == 0 else "dve"
                evc[0] += 1
            if eng == "act":
                P.act(lambda e: e.copy(out=out, in_=in_), reads=reads, writes=writes)
            elif eng == "dve":
                P.dve(lambda e: e.tensor_copy(out=out, in_=in_), reads=reads, writes=writes)
            else:
                P.pool(lambda e: e.tensor_copy(out=out, in_=in_), reads=reads, writes=writes)

        pbc = [0]

        npb = [5]

        def pbank():
            i = pbc[0] % npb[0]
            pbc[0] += 1
            if npb[0] == 7 and i >= 5:
                i += 1
            return i

        tbc = [0]

        def norm_tile(src, src_key, lnwb, hn, junk, idx, dst_fn, dst_keys, out_f32=None):
            P.act(lambda e: e.activation(out=junk, in_=src, func=AF.Square, accum_out=ssv[:, idx:idx + 1]),
                  reads=[src_key], writes=["junk", ("ss", idx)])
            P.dve(lambda e: e.tensor_scalar(out=rsv[:, idx:idx + 1], in0=ssv[:, idx:idx + 1], scalar1=1.0 / D, scalar2=EPS,
                                            op0=ALU.mult, op1=ALU.add),
                  reads=[("ss", idx)], writes=[("rs", idx)])
            P.act(lambda e: e.activation(out=rsv[:, idx:idx + 1], in_=rsv[:, idx:idx + 1], func=AF.Sqrt),
                  reads=[("rs", idx)], writes=[("rs", idx)])
            P.dve(lambda e: e.reciprocal(out=rsv[:, idx:idx + 1], in_=rsv[:, idx:idx + 1]),
                  reads=[("rs", idx)], writes=[("rs", idx)])
            if out_f32 is not None:
                out_ap, out_key = out_f32
                P.dve(lambda e: e.scalar_tensor_tensor(out=out_ap, in0=src, scalar=rsv[:, idx:idx + 1], in1=lnwb,
                                                       op0=ALU.mult, op1=ALU.mult),
                      reads=[src_key, ("rs", idx), "lnwb"], writes=[out_key])
                return
            hn_ap, hn_key = hn
            P.dve(lambda e: e.scalar_tensor_tensor(out=hn_ap, in0=src, scalar=rsv[:, idx:idx + 1], in1=lnwb,
                                                   op0=ALU.mult, op1=ALU.mult),
                  reads=[src_key, ("rs", idx), "lnwb"], writes=[hn_key])
            for r4 in range(4):
                slot = tbc[0] % 2
                tbc[0] += 1
                for j in range(4):
                    kc = r4 * 4 + j
                    o_ap = TBk[slot][:, j * 128:(j + 1) * 128]
                    i_ap = hn_ap[:, kc * 128:(kc + 1) * 128]
                    P.pe(lambda e, o_ap=o_ap, i_ap=i_ap: e.transpose(out=o_ap, in_=i_ap, identity=ident),
                         reads=[hn_key, "cb"], writes=[("ps", 6 + slot)])
                evac(dst_fn(r4 * 4), TBk[slot][:, 0:512].rearrange("p (a b) -> p a b", a=4),
                     reads=[("ps", 6 + slot)], writes=dst_keys)

        def load_lnw(i, lnwb):
            P.dma("sp", lambda e: e.dma_start(out=lnwb, in_=lnw[i:i + 1, :].broadcast_to([128, D])),
                  writes=["lnwb"])

        def hkeys(w0, w1_):
            return [("h", t) for t in range(w0 // 128, (w1_ - 1) // 128 + 1)]

        def hw(kc, w0, n, step=1):
            half = w0 // 1024
            c0 = w0 % 1024
            if step == 1:
                return hT[:, half, kc, c0:c0 + n]
            return hT[:, half, kc, sl(c0, n, step)]

        xst = [R[:, 0:2048], R[:, 2048:4096]]
        hnA = [(Rb(4096, 5120), "hn0"), (Rb(5120, 6144), "hn1")]
        junkA = Rb(6144, 7168)
        lnwbA = R[:, 8192:10240]
        mnT = Rb(10240, 12288).rearrange("p (a b) -> p a b", a=16)

        load_lnw(2, lnwbA)
        for mt in range(2):
            s = mt % 2
            P.dma("sp", lambda e, s=s, mt=mt: e.dma_start(out=xst[s], in_=memb[mt * 128:(mt + 1) * 128, :]),
                  writes=[("xst", s)])
            norm_tile(xst[s], ("xst", s), lnwbA, hnA[s], junkA, mt,
                      lambda kc0, mt=mt: mnT[:, kc0:kc0 + 4, mt * 128:(mt + 1) * 128], ["mnT"])
        wkv_v = kview(wkv_c)
        npb[0] = 7
        for hh2 in range(2):
            wk, wkk = wload(wkv_v[:, :, hh2 * 256:(hh2 + 1) * 256], 16, 256)
            for j in range(2):
                hh = hh2 * 2 + j
                b = pbank()
                for kc in range(16):
                    mm(PB[b][:, 0:256], wk[:, kc, j * 128:(j + 1) * 128], mnT[:, kc, :], kc == 0, kc == 15,
                       [wkk, "mnT"], [("ps", b)])
                evac(kcT[:, hh, :], PB[b][:, 0:256], [("ps", b)], ["kcT"])
        for vb in range(2):
            wv_, wvk = wload(wkv_v[:, :, 512 + vb * 256:512 + (vb + 1) * 256], 16, 256)
            for mt in range(2):
                b = pbank()
                for kc in range(16):
                    mm(PB[b][:, 0:256], mnT[:, kc, mt * 128:(mt + 1) * 128], wv_[:, kc, :], kc == 0, kc == 15,
                       [wvk, "mnT"], [("ps", b)])
                evac(vc[:, mt, vb * 256:(vb + 1) * 256], PB[b][:, 0:256], [("ps", b)], ["vc"])

        npb[0] = 5
        load_lnw(0, lnwbA)
        for wt in range(16):
            s = wt % 2
            src = xp if wt < 8 else xo
            r0 = (wt % 8) * 128
            P.dma("sp", lambda e, s=s, src=src, r0=r0: e.dma_start(out=xst[s], in_=src[r0:r0 + 128, :]),
                  writes=[("xst", s)])
            norm_tile(xst[s], ("xst", s), lnwbA, hnA[s], junkA, wt,
                      lambda kc0, wt=wt: hT[:, wt // 8, kc0:kc0 + 4, (wt % 8) * 128:(wt % 8 + 1) * 128],
                      [("h", wt)])

        if "hT" in dbg:
            for half in range(2):
                for kc in range(16):
                    for th in range(2):
                        P.dve(lambda e, half=half, kc=kc, th=th: e.tensor_copy(out=tmpf[0][:, :], in_=hT[:, half, kc, th * 512:(th + 1) * 512]),
                              reads=hkeys(0, 2048), writes=[("tmpf", 0)])
                        P.dma("sp", lambda e, half=half, kc=kc, th=th: e.dma_start(
                            out=dbg["hT"][(half * 16 + kc) * 2 + th], in_=tmpf[0][:, :]), reads=[("tmpf", 0)], writes=["dbg"])

        if limit == 1:
            P.emit(st)
            return nc
        qT = Rb(0, 1536).rearrange("p (g n) -> p g n", g=3)
        kT = Rb(1536, 3904)
        Vh = Rb(3904, 6272).rearrange("p (t d) -> p t d", d=128)
        acc_n = R[:, 6272:7296]
        acc_d = R[:, 7296:8320]
        o_attT = Rb(14336, 16384).rearrange("p (h n) -> p h n", h=4)
        Ebuf = [Rb(10368, 11904), Rb(11904, 13440)]
        A1KEYS = [("xst", 0), ("xst", 1), "hn0", "hn1", "junk", "lnwb", "mnT"]
        P.dve(lambda e: e.memset(dummy[:, 0:1], 0.0), writes=A1KEYS + ["Rfree"])

        KOFF = {0: 0, 1: 1152, 2: 2688}
        PG = {0: 128, 1: 512, 2: 1024}
        VOFF = {0: 0, 1: 9, 2: 21}
        NUMB = [0, 1]
        DENB = [2, 3]
        SCB = [4, 5, 6, 7]
        ec = [0]
        scc = [0]
        def att_load(hh, g):
            k = wslot()
            qk = wbuf[k][:, :].rearrange("p (kc t c) -> p kc t c", kc=16, t=2)
            qcol = g * 512 + hh * 128
            kcol = 1536 + g * 512 + hh * 128
            vcol = 3072 + g * 512 + hh * 128
            wdma(k, qk[:, :, 0, :], w_in_v[:, :, qcol:qcol + 128])
            wdma(k, qk[:, :, 1, :], w_in_v[:, :, kcol:kcol + 128])
            wv_, wvk = wload(w_in_v[:, :, vcol:vcol + 128], 16, 128)
            return qk, ("w", k), wv_, wvk

        HG_COL = {"f": 5632, "i": 6656, "q": 4608, "g": 7680}

        def hg_load(hp, nm):
            c0 = HG_COL[nm] + hp * 256
            return wload(w_in_v[:, :, c0:c0 + 256], 16, 256)

        def mg_load(cp):
            k = wslot()
            wab = wbuf[k]
            wa = wab[:, 0:1024].rearrange("p (a b) -> p a b", a=4)
            wb = wab[:, 1024:3072].rearrange("p (a b) -> p a b", a=8)
            wdma(k, wa, kview(w_ba)[:, :, cp * 256:(cp + 1) * 256])
            wdma(k, wb, kview(w_bb)[:, :, cp * 256:(cp + 1) * 256])
            ga, gak = wload(w_in_v[:, :, 8704 + cp * 256:8704 + (cp + 1) * 256], 16, 256)
            gb, gbk = wload(w_in_v[:, :, 10752 + cp * 256:10752 + (cp + 1) * 256], 16, 256)
            return wa, wb, ("w", k), ga, gak, gb, gbk

        def att_proj(hh, g):
            qk, kq, wv_, wvk = take(("att", hh, g), lambda: att_load(hh, g))
            if g < 2:
                prefetch(("att", hh, g + 1), lambda: att_load(hh, g + 1))
            elif hh < 3:
                prefetch(("att", hh + 1, 0), lambda: att_load(hh + 1, 0))
            else:
                for nm in ("f", "i", "q"):
                    prefetch(("hg", 0, nm), lambda nm=nm: hg_load(0, nm))
            for th in range(2):
                b = pbank()
                for kc in range(16):
                    mm(PB[b][:, :], qk[:, kc, 0, :], hT[:, 1, kc, th * 512:(th + 1) * 512], kc == 0, kc == 15,
                       [kq] + hkeys(1024 + th * 512, 1024 + (th + 1) * 512), [("ps", b)])
                evac(qT[:, g, th * 512:(th + 1) * 512], PB[b][:, :], [("ps", b), "Rfree"], [("qT", g)])
            segs = []
            w0 = 1024 - PG[g]
            while w0 < 2048:
                n = min(512, (1024 if w0 < 1024 else 2048) - w0)
                segs.append((w0, n))
                w0 += n
            for (w0, n) in segs:
                b = pbank()
                for kc in range(16):
                    mm(PB[b][:, 0:n], qk[:, kc, 1, :], hw(kc, w0, n), kc == 0, kc == 15,
                       [kq] + hkeys(w0, w0 + n), [("ps", b)])
                c0 = KOFF[g] + w0 - (1024 - PG[g])
                evac(kT[:, c0:c0 + n], PB[b][:, 0:n], [("ps", b), "Rfree"], [("kT", g)])
            tiles = []
            if g == 0:
                for n in range(9):
                    tiles.append((VOFF[0] + n, 896 + 128 * n, 1))
            elif g == 1:
                for r in range(4):
                    for n in range(3):
                        tiles.append((VOFF[1] + r * 3 + n, 512 + r + 512 * n, 4))
            else:
                for r in range(8):
                    for half in range(2):
                        tiles.append((VOFF[2] + r * 2 + half, half * 1024 + r, 8))
            for (ti, w0, step) in tiles:
                b = pbank()
                for kc in range(16):
                    mm(PB[b][:, 0:128], hw(kc, w0, 128, step), wv_[:, kc, :], kc == 0, kc == 15,
                       [wvk] + hkeys(w0, min(w0 + 128 * step, (w0 // 1024 + 1) * 1024)), [("ps", b)])
                evac(Vh[:, ti, :], PB[b][:, 0:128], [("ps", b), "Rfree"], [("V", g)])


        def att_unit(hh, g, mode):
            ei = (hh * 3 + g) % 2
            Ef = Ebuf[ei]
            if g < 2:
                E = Ef.rearrange("p (t c) -> p t c", c=256)
            else:
                E = Ef[:, 0:2048].rearrange("p (t c) -> p t c", c=128)
            ek = ("E", ei)

            def score(ti_e, kcols, qcols, ncol, ecol0, mask, bias):
                if mode != "score":
                    return
                sb_ = SCB[scc[0] % 4]
                scc[0] += 1
                mm(PB[sb_][:, 0:ncol], kcols, qcols, True, True, [("kT", g), ("qT", g)], [("ps", sb_)])
                e_ap = E[:, ti_e, ecol0:ecol0 + ncol]
                if bias is None:
                    P.act(lambda e: e.activation(out=e_ap, in_=PB[sb_][:, 0:ncol], func=AF.Exp, scale=SCALE),
                          reads=[("ps", sb_), "Rfree"], writes=[ek])
                else:
                    P.act(lambda e: e.activation(out=e_ap, in_=PB[sb_][:, 0:ncol], func=AF.Exp, scale=SCALE,
                                                 bias=bias),
                          reads=[("ps", sb_), "pm", "Rfree"], writes=[ek])
                P.pool(lambda e: e.tensor_tensor(out=e_ap, in0=e_ap, in1=mask, op=ALU.mult),
                       reads=[ek, "cb"], writes=[ek])

            def pv(col0, ncol, pairs):
                if mode != "pv":
                    return
                bn = NUMB[col0 // 512]
                bd = DENB[col0 // 512]
                c = col0 % 512
                for i, (vt, e_ap) in enumerate(pairs):
                    mm(PB[bn][:, c:c + ncol], Vh[:, vt, :], e_ap, i == 0, i == len(pairs) - 1,
                       [("V", g), ek], [("ps", bn)])
                for i, (vt, e_ap) in enumerate(pairs):
                    mm(PB[bd][:, c:c + ncol], ones, e_ap, i == 0, i == len(pairs) - 1,
                       [ek, "cb"], [("ps", bd)])

            if g == 0:
                for n in range(9):
                    kcols = kT[:, KOFF[0] + 128 * n:KOFF[0] + 128 * (n + 1)]
                    if n == 0:
                        score(n, kcols, qT[:, 0, 0:128], 128, 128, M01[:, 128:256], pm[:, 0:1])
                    elif n == 8:
                        score(n, kcols, qT[:, 0, 896:1024], 128, 0, M01[:, 0:128], None)
                    else:
                        score(n, kcols, qT[:, 0, 128 * (n - 1):128 * (n + 1)], 256, 0, M01, None)
                for qb in range(8):
                    pv(qb * 128, 128, [(VOFF[0] + qb, E[:, qb, 128:256]), (VOFF[0] + qb + 1, E[:, qb + 1, 0:128])])
            elif g == 1:
                for r in range(4):
                    for n in range(3):
                        kcols = kT[:, sl(KOFF[1] + r + 512 * n, 128, 4)]
                        ti_e = r * 3 + n
                        if n == 0:
                            score(ti_e, kcols, qT[:, 1, sl(r, 128, 4)], 128, 128, M01[:, 128:256], pm[:, 0:1])
                        elif n == 1:
                            score(ti_e, kcols, qT[:, 1, sl(r, 256, 4)], 256, 0, M01, None)
                        else:
                            score(ti_e, kcols, qT[:, 1, sl(r + 512, 128, 4)], 128, 0, M01[:, 0:128], None)
                for r in range(4):
                    for qb in range(2):
                        pv((r * 2 + qb) * 128, 128,
                           [(VOFF[1] + r * 3 + qb, E[:, r * 3 + qb, 128:256]),
                            (VOFF[1] + r * 3 + qb + 1, E[:, r * 3 + qb + 1, 0:128])])
            else:
                for r in range(8):
                    for half in range(2):
                        kcols = kT[:, sl(KOFF[2] + half * 1024 + r, 128, 8)]
                        score(r * 2 + half, kcols, qT[:, 2, sl(r, 128, 8)], 128, 0, MP2 if half == 0 else MO2,
                              pm[:, 0:1] if half == 0 else None)
                for r in range(8):
                    pv(r * 128, 128, [(VOFF[2] + r * 2, E[:, r * 2, 0:128]), (VOFF[2] + r * 2 + 1, E[:, r * 2 + 1, 0:128])])
            if mode == "pv":
                for (banks, acc, akey) in ((NUMB, acc_n, "acc_n"), (DENB, acc_d, "acc_d")):
                    for bi in range(2):
                        pb_ = PB[banks[bi]]
                        if g == 0:
                            dst = acc[:, bi * 512:(bi + 1) * 512]
                            P.dve(lambda e, dst=dst, pb_=pb_: e.tensor_copy(out=dst, in_=pb_[:, :]),
                                  reads=[("ps", banks[bi]), "Rfree"], writes=[akey])
                        else:
                            rr = 4 if g == 1 else 8
                            per = rr // 2
                            dst = acc.rearrange("p (m r) -> p r m", r=rr)[:, bi * per:(bi + 1) * per, :]
                            srcv = pb_[:, :].rearrange("p (r m) -> p r m", r=per)
                            P.dve(lambda e, dst=dst, srcv=srcv: e.tensor_tensor(out=dst, in0=dst, in1=srcv, op=ALU.add),
                                  reads=[("ps", banks[bi]), akey], writes=[akey])

        def att_fin(hh):
            P.dve(lambda e: e.reciprocal(out=acc_d, in_=acc_d), reads=["acc_d"], writes=["acc_d"])
            P.dve(lambda e, hh=hh: e.tensor_tensor(out=o_attT[:, hh, :], in0=acc_n, in1=acc_d, op=ALU.mult),
                  reads=["acc_n", "acc_d", "Rfree"], writes=["o_attT"])

        att_proj(0, 0)
        for u in range(12):
            hh, g = divmod(u, 3)
            att_unit(hh, g, "score")
            if u + 1 < 12:
                att_proj(*divmod(u + 1, 3))
            att_unit(hh, g, "pv")
            if g == 2:
                att_fin(hh)

        if "o_att" in dbg:
            for hh in range(4):
                for th in range(2):
                    P.dve(lambda e, hh=hh, th=th: e.tensor_copy(out=tmpf[0][:, :], in_=o_attT[:, hh, th * 512:(th + 1) * 512]),
                          reads=["o_attT"], writes=[("tmpf", 0)])
                    P.dma("sp", lambda e, hh=hh, th=th: e.dma_start(out=dbg["o_att"][hh * 2 + th], in_=tmpf[0][:, :]),
                          reads=[("tmpf", 0)], writes=["dbg"])

        if limit == 2:
            P.emit(st)
            return nc
        o_hgT = Rb(10240, 14336).rearrange("p (h n) -> p h n", h=8)
        HB = 0

        NPAR = 3
        PSZ = 3072

        def hscr(i, par, bf=False):
            if bf:
                base = HB + par * PSZ + 2048 + i * 128
                return R[:, base:base + 128].bitcast(BF16)
            base = HB + par * PSZ + i * 256
            return R[:, base:base + 256]
        ATT_KEYS = [("qT", g) for g in range(3)] + [("kT", g) for g in range(3)] + [("V", g) for g in range(3)] + \
                   ["acc_n", "acc_d", ("E", 0), ("E", 1)]
        P.dve(lambda e: e.memset(dummy[:, 1:2], 0.0), writes=ATT_KEYS + ["Rfree2"])

        for hp in range(4):
            c0 = hp * 256
            fw, fk = take(("hg", hp, "f"), lambda: hg_load(hp, "f"))
            iw, ik = take(("hg", hp, "i"), lambda: hg_load(hp, "i"))
            qw, qk_ = take(("hg", hp, "q"), lambda: hg_load(hp, "q"))
            gw, gk = take(("hg", hp, "g"), lambda: hg_load(hp, "g"))
            if hp < 3:
                for nm in ("f", "i", "q"):
                    prefetch(("hg", hp + 1, nm), lambda nm=nm: hg_load(hp + 1, nm))
            else:
                prefetch(("mg", 0), lambda: mg_load(0))
            P.dma("sp", lambda e, c0=c0: e.dma_start(out=lbb[:], in_=hg_lb[0:1, c0:c0 + 256].broadcast_to([128, 256])),
                  writes=["lbb"])
            P.dma("sp", lambda e, c0=c0: e.dma_start(out=lbt[:], in_=hg_lb[1:2, c0:c0 + 256].broadcast_to([128, 256])),
                  writes=["lbt"])
            P.dve(lambda e: e.tensor_tensor(out=lbb[:], in0=lbb[:], in1=lbt[:], op=ALU.subtract),
                  reads=["lbb", "lbt"], writes=["lbb"])
            P.act(lambda e: e.activation(out=lbb[:], in_=lbb[:], func=AF.Exp, scale=-1.0), reads=["lbb"], writes=["lbb"])
            P.dve(lambda e: e.tensor_scalar(out=lbb[:], in0=lbb[:], scalar1=1.0, scalar2=None, op0=ALU.add),
                  reads=["lbb"], writes=["lbb"])
            P.dve(lambda e: e.reciprocal(out=lbb[:], in_=lbb[:]), reads=["lbb"], writes=["lbb"])
            P.dve(lambda e: e.tensor_scalar(out=oml[:], in0=lbb[:], scalar1=-1.0, scalar2=1.0, op0=ALU.mult, op1=ALU.add),
                  reads=["lbb"], writes=["oml"])
            P.dve(lambda e: e.memset(state[:], 0.0), writes=["state"])

            def tvars(wt):
                par = wt % NPAR
                d = dict(own=wt >= 8, par=par, half=wt // 8, tc0=(wt % 8) * 128, hk=[("h", wt)],
                         K=(lambda name, par=par: (name, par)), ecs=ecsb[par])
                names = ("sig", "f_", "kk", "e2", "e1", "e1n", "sq", "sg")
                for i, n in enumerate(names):
                    d[n] = hscr(i, par)
                for i, n in enumerate(("v_b", "kl", "ke", "qe", "og")):
                    d[n] = hscr(i, par, True)
                b0 = HB + par * PSZ + 2688
                d["qkT"] = R[:, b0:b0 + 256].bitcast(BF16).rearrange("p (a b) -> p a b", a=4)
                d["aT"] = R[:, b0 + 256:b0 + 384].bitcast(BF16).rearrange("p (a b) -> p a b", a=2)
                return d

            def sigm_from_exp(buf, key):
                P.dve(lambda e: e.tensor_scalar(out=buf, in0=buf, scalar1=1.0, scalar2=None, op0=ALU.add),
                      reads=[key], writes=[key])
                P.dve(lambda e: e.reciprocal(out=buf, in_=buf), reads=[key], writes=[key])

            def pe_fill(n):
                for _ in range(n):
                    mm(PB[6][:, 256:512], ident, hT[:, 1, 0, 0:256], True, True, ["cb", ("h", 8), ("h", 9)], [("ps", 6)])

            def hg_f1(wt):
                v = tvars(wt)
                K, own, half, tc0, hk = v["K"], v["own"], v["half"], v["tc0"], v["hk"]
                sig, f_, kk, sq, sg, v_b = v["sig"], v["f_"], v["kk"], v["sq"], v["sg"], v["v_b"]
                bf_ = pbank()
                for kc in range(16):
                    mm(PB[bf_][:, 0:256], hT[:, half, kc, tc0:tc0 + 128], fw[:, kc, :], kc == 0, kc == 15, [fk] + hk, [("ps", bf_)])
                for kc in range(16):
                    mm(PB[bf_][:, 256:512], hT[:, half, kc, tc0:tc0 + 128], iw[:, kc, :], kc == 0, kc == 15, [ik] + hk, [("ps", bf_)])
                P.act(lambda e: e.activation(out=sig, in_=PB[bf_][:, 0:256], func=AF.Sigmoid),
                      reads=[("ps", bf_), "Rfree2"], writes=[K("sig")])
                P.dve(lambda e: e.tensor_copy(out=v_b, in_=PB[bf_][:, 256:512]),
                      reads=[("ps", bf_), "Rfree2"], writes=[K("v")])
                if own:
                    bq = pbank()
                    for kc in range(16):
                        mm(PB[bq][:, 0:256], hT[:, half, kc, tc0:tc0 + 128], qw[:, kc, :], kc == 0, kc == 15, [qk_] + hk, [("ps", bq)])
                    for kc in range(16):
                        mm(PB[bq][:, 256:512], hT[:, half, kc, tc0:tc0 + 128], gw[:, kc, :], kc == 0, kc == 15, [gk] + hk, [("ps", bq)])
                    P.act(lambda e: e.activation(out=sq, in_=PB[bq][:, 0:256], func=AF.Sigmoid),
                          reads=[("ps", bq), "Rfree2"], writes=[K("sq")])
                    P.act(lambda e: e.activation(out=sg, in_=PB[bq][:, 256:512], func=AF.Sigmoid),
                          reads=[("ps", bq), "Rfree2"], writes=[K("sg")])
                    P.dve(lambda e: e.tensor_tensor(out=sq, in0=PB[bq][:, 0:256], in1=sq, op=ALU.mult),
                          reads=[("ps", bq), K("sq")], writes=[K("sq")])
                    P.dve(lambda e: e.tensor_tensor(out=sg, in0=PB[bq][:, 256:512], in1=sg, op=ALU.mult),
                          reads=[("ps", bq), K("sg")], writes=[K("sg")])
                P.pool(lambda e: e.tensor_tensor(out=sig, in0=sig, in1=oml[:], op=ALU.mult),
                       reads=[K("sig"), "oml"], writes=[K("sig")])
                P.pool(lambda e: e.tensor_tensor(out=f_, in0=sig, in1=lbb[:], op=ALU.add),
                       reads=[K("sig"), "lbb", "Rfree2"], writes=[K("f")])
                P.pool(lambda e: e.tensor_tensor(out=kk, in0=oml[:], in1=sig, op=ALU.subtract),
                       reads=[K("sig"), "oml", "Rfree2"], writes=[K("kk")])
                P.act(lambda e: e.activation(out=f_, in_=f_, func=AF.Ln), reads=[K("f")], writes=[K("f")])

            def hg_f2(wt):
                v = tvars(wt)
                K, own, ecs = v["K"], v["own"], v["ecs"]
                f_, kk, e2, e1, e1n, sq = v["f_"], v["kk"], v["e2"], v["e1"], v["e1n"], v["sq"]
                kl, ke, qe, qkT, aT = v["kl"], v["ke"], v["qe"], v["qkT"], v["aT"]
                if not own:
                    pe_fill(8)
                bd_ = pbank()
                mm(PB[bd_][:, 256:512], TD2, f_, True, True, [K("f"), "cf"], [("ps", bd_)])
                if own:
                    mm(PB[bd_][:, 0:256], TD1, f_, True, True, [K("f"), "cf"], [("ps", bd_)])
                for j in range(2):
                    mm(CSB[:, j * 4:(j + 1) * 4], f_[:, j * 128:(j + 1) * 128], SEL, True, True,
                       [K("f"), "cf"], [("ps", 7)])
                P.act(lambda e: e.activation(out=ecs[:], in_=CSB[:, 0:8], func=AF.Exp), reads=[("ps", 7)], writes=[K("ecs")])
                P.act(lambda e: e.activation(out=e2, in_=PB[bd_][:, 256:512], func=AF.Exp),
                      reads=[("ps", bd_), "Rfree2"], writes=[K("e2")])
                P.pool(lambda e: e.tensor_tensor(out=kl, in0=kk, in1=e2, op=ALU.mult),
                       reads=[K("kk"), K("e2"), "Rfree2"], writes=[K("kl")])
                if own:
                    P.act(lambda e: e.activation(out=e1, in_=PB[bd_][:, 0:256], func=AF.Exp),
                          reads=[("ps", bd_), "Rfree2"], writes=[K("e1")])
                    P.act(lambda e: e.activation(out=e1n, in_=PB[bd_][:, 0:256], func=AF.Exp, scale=-1.0),
                          reads=[("ps", bd_), "Rfree2"], writes=[K("e1n")])
                    P.pool(lambda e: e.tensor_tensor(out=ke, in0=kk, in1=e1n, op=ALU.mult),
                           reads=[K("kk"), K("e1n"), "Rfree2"], writes=[K("ke")])
                    P.dve(lambda e: e.tensor_tensor(out=qe, in0=sq, in1=e1, op=ALU.mult),
                          reads=[K("sq"), K("e1"), "Rfree2"], writes=[K("qe")])
                    pe_fill(14)
                    slot = tbc[0] % 2
                    tbc[0] += 1
                    for j in range(2):
                        P.pe(lambda e, j=j: e.transpose(out=TBk[slot][:, j * 128:(j + 1) * 128],
                                                        in_=qe[:, j * 128:(j + 1) * 128], identity=ident),
                             reads=[K("qe"), "cb"], writes=[("ps", 6 + slot)])
                    for j in range(2):
                        P.pe(lambda e, j=j: e.transpose(out=TBk[slot][:, (2 + j) * 128:(3 + j) * 128],
                                                        in_=ke[:, j * 128:(j + 1) * 128], identity=ident),
                             reads=[K("ke"), "cb"], writes=[("ps", 6 + slot)])
                    evac(qkT, TBk[slot][:, 0:512].rearrange("p (a b) -> p a b", a=4),
                         [("ps", 6 + slot), "Rfree2"], [K("qkT")])
                    ba = pbank()
                    for j in range(2):
                        mm(PB[ba][:, j * 128:(j + 1) * 128], qkT[:, 2 + j, :], qkT[:, j, :], True, True, [K("qkT")], [("ps", ba)])
                    for j in range(2):
                        P.dve(lambda e, j=j: e.tensor_tensor(out=aT[:, j, :], in0=PB[ba][:, j * 128:(j + 1) * 128],
                                                             in1=MBLK, op=ALU.mult),
                              reads=[("ps", ba), "cb", "Rfree2"], writes=[K("aT")])

            def hg_b(wt):
                v = tvars(wt)
                K, own, ecs, tc0 = v["K"], v["own"], v["ecs"], v["tc0"]
                sq, sg, v_b, kl, og, qkT, aT = v["sq"], v["sg"], v["v_b"], v["kl"], v["og"], v["qkT"], v["aT"]
                on = sq
                if own:
                    for j in range(2):
                        mm(PB[5][:, j * 128:(j + 1) * 128], aT[:, j, :], v_b[:, j * 128:(j + 1) * 128], j == 0, False,
                           [K("aT"), K("v")], [("ps", 5)], skip=True)
                for ch in range(2):
                    r0 = ch * 64
                    if own:
                        for j in range(2):
                            mm(PB[5][r0:r0 + 64, j * 128:(j + 1) * 128], qkT[:, j, r0:r0 + 64], st_sb[ch][:, j, :], False,
                               (ch == 1), [K("qkT"), ("st_s", ch)], [("ps", 5)], skip=True)
                    bc_ = pbank()
                    for j in range(2):
                        mm(PB[bc_][:, j * 128:(j + 1) * 128], kl[r0:r0 + 64, j * 128:(j + 1) * 128],
                           v_b[r0:r0 + 64, j * 128:(j + 1) * 128], True, True, [K("kl"), K("v")], [("ps", bc_)])
                    for j in range(2):
                        P.dve(lambda e, j=j, bc_=bc_: e.scalar_tensor_tensor(
                            out=state[:, j, :], in0=state[:, j, :], scalar=ecs[:, j * 4 + 2 + ch:j * 4 + 3 + ch],
                            in1=PB[bc_][:, j * 128:(j + 1) * 128], op0=ALU.mult, op1=ALU.add),
                              reads=["state", K("ecs"), ("ps", bc_)], writes=["state"])
                    if ch == 0:
                        nxt_ok, ecs_n, key_n, col = own, ecs, K("ecs"), 1
                    else:
                        nxt_ok = (wt + 1 < 16) and (wt + 1 >= 8)
                        ecs_n, key_n, col = ecsb[(wt + 1) % NPAR], ("ecs", (wt + 1) % NPAR), 0
                    if nxt_ok:
                        for j in range(2):
                            P.dve(lambda e, j=j, ecs_n=ecs_n, col=col, ch=ch: e.tensor_scalar(
                                out=st_sb[1 - ch][:, j, :], in0=state[:, j, :], scalar1=ecs_n[:, j * 4 + col:j * 4 + col + 1],
                                scalar2=None, op0=ALU.mult),
                                  reads=["state", key_n], writes=[("st_s", 1 - ch)])
                if own:
                    for j in range(2):
                        P.act(lambda e, j=j: e.activation(out=on[:, j * 128:(j + 1) * 128], in_=PB[5][:, j * 128:(j + 1) * 128],
                                                          func=AF.Square, accum_out=ssv[:, 16 + j:17 + j]),
                              reads=[("ps", 5), K("qe")], writes=[K("sq"), ("hss", j)])
                    P.dve(lambda e: e.tensor_scalar(out=rsv[:, 16:18], in0=ssv[:, 16:18], scalar1=1.0 / 128, scalar2=EPS,
                                                    op0=ALU.mult, op1=ALU.add),
                          reads=[("hss", 0), ("hss", 1)], writes=["hrs"])
                    P.act(lambda e: e.activation(out=rsv[:, 16:18], in_=rsv[:, 16:18], func=AF.Ln),
                          reads=["hrs"], writes=["hrs"])
                    P.act(lambda e: e.activation(out=rsv[:, 16:18], in_=rsv[:, 16:18], func=AF.Exp, scale=-0.5),
                          reads=["hrs"], writes=["hrs"])
                    for j in range(2):
                        P.dve(lambda e, j=j: e.scalar_tensor_tensor(
                            out=on[:, j * 128:(j + 1) * 128], in0=PB[5][:, j * 128:(j + 1) * 128], scalar=rsv[:, 16 + j:17 + j],
                            in1=hgnb[:], op0=ALU.mult, op1=ALU.mult),
                              reads=[("ps", 5), "hrs", "hgnb"], writes=[K("sq")])
                    P.pool(lambda e: e.tensor_tensor(out=og, in0=on, in1=sg, op=ALU.mult),
                           reads=[K("sq"), K("sg"), "Rfree2"], writes=[K("og")])
                    pe_fill(14)
                    slot = tbc[0] % 2
                    tbc[0] += 1
                    for j in range(2):
                        P.pe(lambda e, j=j: e.transpose(out=TBk[slot][:, j * 128:(j + 1) * 128],
                                                        in_=og[:, j * 128:(j + 1) * 128], identity=ident),
                             reads=[K("og"), "cb"], writes=[("ps", 6 + slot)])
                    evac(o_hgT[:, hp * 2:hp * 2 + 2, tc0:tc0 + 128],
                         TBk[slot][:, 0:256].rearrange("p (a b) -> p a b", a=2),
                         [("ps", 6 + slot), "Rfree2"], ["o_hgT"])

            for it in range(-2, 16):
                if 0 <= it + 2 < 16:
                    hg_f1(it + 2)
                if 0 <= it + 1 < 16:
                    hg_f2(it + 1)
                if it >= 0:
                    hg_b(it)

        if "o_hg" in dbg:
            for hh in range(8):
                for th in range(2):
                    P.dve(lambda e, hh=hh, th=th: e.tensor_copy(out=tmpf[0][:, :], in_=o_hgT[:, hh, th * 512:(th + 1) * 512]),
                          reads=["o_hgT"], writes=[("tmpf", 0)])
                    P.dma("sp", lambda e, hh=hh, th=th: e.dma_start(out=dbg["o_hg"][hh * 2 + th], in_=tmpf[0][:, :]),
                          reads=[("tmpf", 0)], writes=["dbg"])

        if limit == 3:
            P.emit(st)
            return nc
        HG_KEYS = [(n, p) for n in ("sig", "f", "kk", "e2", "e1", "e1n", "sq", "sg", "v", "kl", "ke", "qe",
                                    "og", "qkT", "aT") for p in range(3)]
        P.dve(lambda e: e.memset(dummy[:, 2:3], 0.0), writes=HG_KEYS + hkeys(0, 1024) + ["Rfree3"])
        npb[0] = 7
        mg = [R[:, i * 512:(i + 1) * 512] for i in range(8)]
        w_ba_v = kview(w_ba)
        w_bb_v = kview(w_bb)
        mi = [0]
        for cp in range(8):
            wa, wb, kab, ga, gak, gb, gbk = take(("mg", cp), lambda: mg_load(cp))
            if cp < 7:
                prefetch(("mg", cp + 1), lambda: mg_load(cp + 1))
            for cc in range(2):
                c = cp * 2 + cc
                for th in range(2):
                    tsl = slice(th * 512, (th + 1) * 512)
                    hk = hkeys(1024 + th * 512, 1024 + (th + 1) * 512)
                    b_a, b_b, b_ga, b_gb = pbank(), pbank(), pbank(), pbank()
                    for kc in range(4):
                        mm(PB[b_a][:, :], wa[:, kc, cc * 128:(cc + 1) * 128], o_attT[:, kc, tsl], kc == 0, kc == 3,
                           [kab, "o_attT"], [("ps", b_a)])
                    for kc in range(8):
                        mm(PB[b_b][:, :], wb[:, kc, cc * 128:(cc + 1) * 128], o_hgT[:, kc, tsl], kc == 0, kc == 7,
                           [kab, "o_hgT"], [("ps", b_b)])
                    for kc in range(16):
                        mm(PB[b_ga][:, :], ga[:, kc, cc * 128:(cc + 1) * 128], hT[:, 1, kc, tsl], kc == 0, kc == 15,
                           [gak] + hk, [("ps", b_ga)])
                    for kc in range(16):
                        mm(PB[b_gb][:, :], gb[:, kc, cc * 128:(cc + 1) * 128], hT[:, 1, kc, tsl], kc == 0, kc == 15,
                           [gbk] + hk, [("ps", b_gb)])
                    par = mi[0] % 2
                    mi[0] += 1
                    sa, sbb, m1, m2 = (mg[par * 4 + i] for i in range(4))
                    MK = lambda n: ("mg", n, par)
                    P.act(lambda e, sa=sa, b_ga=b_ga: e.activation(out=sa, in_=PB[b_ga][:, :], func=AF.Sigmoid),
                          reads=[("ps", b_ga), "Rfree3"], writes=[MK(0)])
                    P.act(lambda e, sbb=sbb, b_gb=b_gb: e.activation(out=sbb, in_=PB[b_gb][:, :], func=AF.Sigmoid),
                          reads=[("ps", b_gb), "Rfree3"], writes=[MK(1)])
                    P.dve(lambda e, sa=sa, m1=m1, b_a=b_a: e.tensor_tensor(out=m1, in0=PB[b_a][:, :], in1=sa, op=ALU.mult),
                          reads=[("ps", b_a), MK(0), "Rfree3"], writes=[MK(2)])
                    P.dve(lambda e, sbb=sbb, m2=m2, b_b=b_b: e.tensor_tensor(out=m2, in0=PB[b_b][:, :], in1=sbb, op=ALU.mult),
                          reads=[("ps", b_b), MK(1), "Rfree3"], writes=[MK(3)])
                    P.pool(lambda e, m1=m1, m2=m2, c=c, tsl=tsl: e.tensor_tensor(out=hT[:, 0, c, tsl], in0=m1, in1=m2, op=ALU.add),
                           reads=[MK(2), MK(3), "Rfree3"], writes=[("mT", c)])

        MG_KEYS = [("mg", n, p) for n in range(4) for p in range(2)]
        P.dve(lambda e: e.memset(dummy[:, 3:4], 0.0), writes=MG_KEYS + ["o_hgT", "o_attT", "Rfree4"])
        for t in range(8):
            P.dma("sp", lambda e, t=t: e.dma_start(out=xs[:, t, :], in_=xo[t * 128:(t + 1) * 128, :]),
                  reads=["Rfree4"], writes=[("x", t)])
        w_out_v = kview(w_out)
        MT_ALL = [("mT", c) for c in range(16)]
        for nb in range(8):
            wo_, wok = wload(w_out_v[:, :, nb * 256:(nb + 1) * 256], 16, 256)
            for t in range(8):
                b = pbank()
                for kc in range(16):
                    mm(PB[b][:, 0:256], hT[:, 0, kc, t * 128:(t + 1) * 128], wo_[:, kc, :], kc == 0, kc == 15,
                       [wok] + MT_ALL, [("ps", b)])
                P.dve(lambda e, b=b, t=t, nb=nb: e.tensor_tensor(out=xs[:, t, nb * 256:(nb + 1) * 256],
                                                              in0=xs[:, t, nb * 256:(nb + 1) * 256], in1=PB[b][:, 0:256], op=ALU.add),
                      reads=[("ps", b), ("x", t)], writes=[("x", t)])

        def dump_x(name):
            if name in dbg:
                for t in range(8):
                    P.dma("sp", lambda e, t=t: e.dma_start(out=dbg[name][t * 128:(t + 1) * 128, :], in_=xs[:, t, :]),
                          reads=[("x", t)], writes=["dbg"])
        dump_x("x1")

        if limit == 4:
            P.emit(st)
            return nc
        hnB2 = [(Hprev[:, 0:2048], "hnB"), (Hprev[:, 2048:4096], "hnB1")]
        hnB = hnB2[0]
        junkB = Hprev[:, 2048:4096]
        junkBC = Hprev[:, 14336:16384]
        lnwbB = Hprev[:, 4096:8192].bitcast(F32)
        bufX = Hprev[:, 8192:12288]
        bufY = Hprev[:, 12288:16384]
        P.dve(lambda e: e.memset(dummy[:, 4:5], 0.0), writes=MT_ALL + ["hnB", "hnB1", "junk", "lnwb", "Hfree"])

        def norm_phase(li):
            P.dma("sp", lambda e: e.dma_start(out=lnwbB, in_=lnw[li:li + 1, :].broadcast_to([128, D])),
                  reads=["Hfree"], writes=["lnwb"])
            for t in range(8):
                norm_tile(xs[:, t, :], ("x", t), lnwbB, hnB2[t % 2], junkBC, t,
                          lambda kc0, t=t: hT[:, 1, kc0:kc0 + 4, t * 128:(t + 1) * 128], [("h", 8 + t)])

        npb[0] = 5
        norm_phase(1)
        npb[0] = 7
        qcT = bufX.rearrange("p (h n) -> p h n", h=4)
        ocT = bufY.rearrange("p (h n) -> p h n", h=4)
        wq_v = kview(wq_c)
        for hh2 in range(2):
            wq_, wqk = wload(wq_v[:, :, hh2 * 256:(hh2 + 1) * 256], 16, 256)
            for j in range(2):
                hh = hh2 * 2 + j
                for th in range(2):
                    b = pbank()
                    for kc in range(16):
                        mm(PB[b][:, :], wq_[:, kc, j * 128:(j + 1) * 128], hT[:, 1, kc, th * 512:(th + 1) * 512],
                           kc == 0, kc == 15, [wqk] + hkeys(1024 + th * 512, 1536 + th * 512), [("ps", b)])
                    evac(qcT[:, hh, th * 512:(th + 1) * 512], PB[b][:, :], [("ps", b), "Hfree"], ["qcT"])
        Ecr = [tmpf[0][:, :].bitcast(BF16).rearrange("p (m n) -> p m n", m=2),
               tmpf[1][:, :].bitcast(BF16).rearrange("p (m n) -> p m n", m=2)]
        rdn = [R_ for R_ in (sb("rdn0", [128, 512], F32), sb("rdn1", [128, 512], F32))]
        ci = [0]
        for hh in range(4):
            for th in range(2):
                par = ci[0] % 2
                ci[0] += 1
                for mt in range(2):
                    b = pbank()
                    mm(PB[b][:, :], kcT[:, hh, mt * 128:(mt + 1) * 128], qcT[:, hh, th * 512:(th + 1) * 512], True, True,
                       ["kcT", "qcT"], [("ps", b)])
                    P.act(lambda e, b=b, mt=mt, par=par: e.activation(out=Ecr[par][:, mt, :], in_=PB[b][:, :], func=AF.Exp,
                                                                      scale=SCALE),
                          reads=[("ps", b)], writes=[("tmpf", par)])
                bn, bd = pbank(), pbank()
                for mt in range(2):
                    mm(PB[bn][:, :], vc[:, mt, hh * 128:(hh + 1) * 128], Ecr[par][:, mt, :], mt == 0, mt == 1,
                       ["vc", ("tmpf", par)], [("ps", bn)])
                for mt in range(2):
                    mm(PB[bd][:, :], ones, Ecr[par][:, mt, :], mt == 0, mt == 1, ["cb", ("tmpf", par)], [("ps", bd)])
                P.dve(lambda e, bd=bd, par=par: e.reciprocal(out=rdn[par][:, :], in_=PB[bd][:, :]),
                      reads=[("ps", bd)], writes=[("rdn", par)])
                P.dve(lambda e, bn=bn, par=par, hh=hh, th=th: e.tensor_tensor(out=ocT[:, hh, th * 512:(th + 1) * 512],
                                                                           in0=PB[bn][:, :], in1=rdn[par][:, :], op=ALU.mult),
                      reads=[("ps", bn), ("rdn", par), "Hfree"], writes=["ocT"])
        wo_v = kview(wo_c)
        for nb in range(2):
            wo2, wo2k = wload(wo_v[:, :, nb * 1024:(nb + 1) * 1024], 4, 1024)
            for t in range(8):
                for n2 in range(2):
                    b = pbank()
                    for kc in range(4):
                        mm(PB[b][:, :], ocT[:, kc, t * 128:(t + 1) * 128], wo2[:, kc, n2 * 512:(n2 + 1) * 512], kc == 0, kc == 3,
                           [wo2k, "ocT"], [("ps", b)])
                    c0 = nb * 1024 + n2 * 512
                    P.dve(lambda e, b=b, t=t, c0=c0: e.tensor_tensor(out=xs[:, t, c0:c0 + 512], in0=xs[:, t, c0:c0 + 512],
                                                                  in1=PB[b][:, :], op=ALU.add),
                          reads=[("ps", b), ("x", t)], writes=[("x", t)])
        dump_x("x2")

        if limit == 5:
            P.emit(st)
            return nc
        P.dve(lambda e: e.memset(dummy[:, 5:6], 0.0), writes=["qcT", "ocT", "hnB", "hnB1", "junk", "lnwb", "Hfree"])
        npb[0] = 5
        norm_phase(3)
        npb[0] = 7
        uT = [bufX.rearrange("p (j n) -> p j n", j=2), bufY.rearrange("p (j n) -> p j n", j=2)]
        uT = [bufX[:, 0:2048].rearrange("p (j n) -> p j n", j=2), bufX[:, 2048:4096].rearrange("p (j n) -> p j n", j=2),
              bufY[:, 0:2048].rearrange("p (j n) -> p j n", j=2), bufY[:, 2048:4096].rearrange("p (j n) -> p j n", j=2)]
        w1_v, w3_v, w2_v = kview(w1), kview(w3), kview(w2)
        NFB = 22
        HOWN = hkeys(1024, 2048)
        sil = [tmpf[0], tmpf[1]]
        sic = [0]

        def ffn_up(fb):
            w1b, w1k = wload(w1_v[:, :, fb * 256:(fb + 1) * 256], 16, 256)
            w3b, w3k = wload(w3_v[:, :, fb * 256:(fb + 1) * 256], 16, 256)
            u = uT[fb % 4]
            for j in range(2):
                for th in range(2):
                    tsl = slice(th * 512, (th + 1) * 512)
                    b1, b3 = pbank(), pbank()
                    for kc in range(16):
                        mm(PB[b1][:, :], w1b[:, kc, j * 128:(j + 1) * 128], hT[:, 1, kc, tsl], kc == 0, kc == 15,
                           [w1k] + HOWN, [("ps", b1)])
                    for kc in range(16):
                        mm(PB[b3][:, :], w3b[:, kc, j * 128:(j + 1) * 128], hT[:, 1, kc, tsl], kc == 0, kc == 15,
                           [w3k] + HOWN, [("ps", b3)])
                    par = sic[0] % 2
                    sic[0] += 1
                    P.act(lambda e, b1=b1, par=par: e.activation(out=sil[par][:, :], in_=PB[b1][:, :], func=AF.Silu),
                          reads=[("ps", b1)], writes=[("tmpf", par)])
                    P.dve(lambda e, b3=b3, par=par, u=u, j=j, tsl=tsl: e.tensor_tensor(out=u[:, j, tsl], in0=PB[b3][:, :],
                                                                                    in1=sil[par][:, :], op=ALU.mult),
                          reads=[("ps", b3), ("tmpf", par), "Hfree"], writes=[("uT", fb % 4)])

        def ffn_down2(fp):
            fbs = (2 * fp, 2 * fp + 1)
            ws = [wload(w2_v[:, fb * 2:(fb + 1) * 2, :], 2, 2048) for fb in fbs]
            for t in range(8):
                for nb in range(4):
                    b = pbank()
                    n = 0
                    for (w2b, w2k), fb in zip(ws, fbs):
                        u = uT[fb % 4]
                        for j in range(2):
                            mm(PB[b][:, :], u[:, j, t * 128:(t + 1) * 128], w2b[:, j, nb * 512:(nb + 1) * 512], n == 0, n == 3,
                               [w2k, ("uT", fb % 4)], [("ps", b)])
                            n += 1
                    P.dve(lambda e, b=b, t=t, nb=nb: e.tensor_tensor(out=xs[:, t, nb * 512:(nb + 1) * 512],
                                                                  in0=xs[:, t, nb * 512:(nb + 1) * 512], in1=PB[b][:, :], op=ALU.add),
                          reads=[("ps", b), ("x", t)], writes=[("x", t)])

        ffn_up(0)
        ffn_up(1)
        for fp in range(NFB // 2):
            if 2 * fp + 2 < NFB:
                ffn_up(2 * fp + 2)
                ffn_up(2 * fp + 3)
            ffn_down2(fp)
        dump_x("x3")

        if limit == 6:
            P.emit(st)
            return nc
        P.dve(lambda e: e.memset(dummy[:, 6:7], 0.0), writes=[("uT", i) for i in range(4)] + ["hnB", "hnB1", "junk", "lnwb", "Hfree"])
        P.dma("sp", lambda e: e.dma_start(out=lnwbB, in_=lnw[4:5, :].broadcast_to([128, D])), reads=["Hfree"], writes=["lnwb"])
        yst = [bufX.bitcast(F32), bufY.bitcast(F32)]
        for t in range(8):
            par = t % 2
            norm_tile(xs[:, t, :], ("x", t), lnwbB, None, junkB, t, None, None, out_f32=(yst[par], ("yst", par)))
            P.dma("sp", lambda e, t=t, par=par: e.dma_start(out=y[t * 128:(t + 1) * 128, :], in_=yst[par]),
                  reads=[("yst", par)], writes=["y"])

        P.emit(st)
    return nc


def _const_tables():
    c = np.zeros((128, 264), np.float32)
    s = np.arange(128)[:, None]
    cc = np.arange(128)[None, :]
    same = (s // 64) == (cc // 64)
    pos_s = s % 64
    tri = same & (s <= cc)
    triref = same & (pos_s <= 32)
    c[:, 0:128] = tri.astype(np.float32) - triref.astype(np.float32)
    c[:, 128:256] = (same & (s > cc)).astype(np.float32)
    sv = np.arange(128)
    c[:, 256] = ((sv < 64) & (sv % 64 <= 32))
    c[:, 257] = ((sv >= 64) & (sv % 64 <= 32))
    c[:, 258] = (sv < 64)
    c[:, 259] = (sv >= 64)
    b = np.zeros((128, 1024), np.float32)
    b[:, 0:128] = np.eye(128)
    k = np.arange(128)[:, None]
    q = np.arange(128)[None, :]
    b[:, 128:256] = (k <= q)
    b[:, 256:384] = (k >= q)
    q64 = np.arange(64)[None, :]
    b[:, 384:448] = (k <= 64 + q64)
    b[:, 448:576] = (same & (cc >= s))
    b[:, 576:704] = 1.0
    b[:, 704:832] = ((k % 2) == (q % 2))
    b[:, 832:960] = (((k % 2) == (q % 2)) & (k <= q))
    return c, b


_NC_CACHE = {}


def _prep_inputs(inputs):
    x = np.asarray(inputs["x"], np.float32)
    mem = np.asarray(inputs["mem"], np.float32)
    lnw = np.stack([np.asarray(inputs["ln_mix_w"], np.float32)[0], np.asarray(inputs["ln_cross_w"], np.float32)[0],
                    np.asarray(inputs["ln_mem_w"], np.float32)[0], np.asarray(inputs["ln_ffn_w"], np.float32)[0],
                    np.asarray(inputs["ln_final_w"], np.float32)], axis=0)
    cst, cstb = _const_tables()
    shared = {
        "w_in": np.ascontiguousarray(inputs["w_in"][0], np.float32),
        "hg_lb": np.ascontiguousarray(inputs["hg_lower_bounds"], np.float32),
        "hg_nw": np.ascontiguousarray(inputs["hg_norm_w"], np.float32).reshape(1, 128),
        "w_ba": np.ascontiguousarray(inputs["w_branch_a"][0], np.float32),
        "w_bb": np.ascontiguousarray(inputs["w_branch_b"][0], np.float32),
        "w_out": np.ascontiguousarray(inputs["w_out"][0], np.float32),
        "wq_c": np.ascontiguousarray(inputs["wq_cross"][0], np.float32),
        "wkv_c": np.ascontiguousarray(inputs["wkv_cross"][0], np.float32),
        "wo_c": np.ascontiguousarray(inputs["wo_cross"][0], np.float32),
        "w1": np.ascontiguousarray(inputs["w1"][0], np.float32),
        "w3": np.ascontiguousarray(inputs["w3"][0], np.float32),
        "w2": np.ascontiguousarray(inputs["w2"][0], np.float32),
        "lnw": np.ascontiguousarray(lnw),
        "cst": cst,
        "cstb": cstb,
    }
    in_maps = []
    for c in range(8):
        b, h = c // 2, c % 2
        m = dict(shared)
        m["xo"] = np.ascontiguousarray(x[b, h * T:(h + 1) * T])
        pmk = np.zeros((128, 2), np.float32)
        if h == 1:
            m["xp"] = np.ascontiguousarray(x[b, 0:T])
        else:
            m["xp"] = np.zeros((T, D), np.float32)
            pmk[:, 0] = -30000.0
            pmk[0:64, 1] = -30000.0
        m["pmk"] = pmk
        m["memb"] = np.ascontiguousarray(mem[b])
        in_maps.append(m)
    return in_maps


def kernel(**inputs):
    if "nc" not in _NC_CACHE:
        _NC_CACHE["nc"] = build()
    nc = _NC_CACHE["nc"]
    in_maps = _prep_inputs(inputs)
    res = run_bass_kernel_spmd(nc, in_maps, core_ids=list(range(8)))
    out = np.zeros((4, 2048, D), np.float32)
    for c in range(8):
        b, h = c // 2, c % 2
        out[b, h * T:(h + 1) * T] = res.results[c]["y"]
    return out
```
